# Optimizing a Trainium2 kernel written in Bass

```python
import math
import jax, jax.numpy as jnp
from jax import lax
import numpy as np

D_MODEL = 1024
BATCH = 8
SEQ = 2048
DEPTH = 4
DEC_BATCH = 128
DEC_SEQ = 4
PAST_LEN = 16384
PAGE_SIZE = 128

N_MIXERS = 2
N_SSD_LAYERS = (DEPTH + 1) // 2
N_HGRN_LAYERS = DEPTH // 2
SSD_EXPAND = 2
SSD_D_INNER = SSD_EXPAND * D_MODEL
SSD_HEAD_DIM = 64
SSD_N_HEADS = SSD_D_INNER // SSD_HEAD_DIM
SSD_N_GROUPS = 4
SSD_HEADS_PER_GROUP = SSD_N_HEADS // SSD_N_GROUPS
SSD_D_STATE = 128
SSD_CONV_W = 4
SSD_GN = SSD_N_GROUPS * SSD_D_STATE
SSD_CONV_DIM = SSD_D_INNER + 2 * SSD_GN
SSD_IN_DIM = SSD_D_INNER + SSD_CONV_DIM + SSD_N_HEADS
SSD_CHUNK = 64
HGRN_EXPAND = 128
HGRN_N_HEADS = D_MODEL // HGRN_EXPAND
HGRN_DK = HGRN_EXPAND
HGRN_DV = D_MODEL // HGRN_N_HEADS
HGRN_F = HGRN_N_HEADS * HGRN_DK
HGRN_IN_DIM = 2 * HGRN_F + 2 * D_MODEL
HGRN_CHUNK = 32
LB_FLOOR = 1e-20
D_FF = 4 * D_MODEL
EPS = 1e-5

kernel_name = 'hybrid_ssd_hgrn2_decoder_step'


def rmsnorm(x, w):
    xf = x.astype(jnp.float32)
    y = xf * lax.rsqrt(jnp.mean(xf * xf, axis=-1, keepdims=True) + EPS)
    return (y * w.astype(jnp.float32)).astype(x.dtype)


def group_rmsnorm(x, w, n_groups):
    shp = x.shape
    xf = x.astype(jnp.float32).reshape(shp[:-1] + (n_groups, shp[-1] // n_groups))
    xf = xf * lax.rsqrt(jnp.mean(xf * xf, axis=-1, keepdims=True) + EPS)
    return xf.reshape(shp) * w.astype(jnp.float32)


def _pad_time(t, lp):
    pad = lp - t.shape[1]
    if pad == 0:
        return t
    return jnp.pad(t, [(0, 0), (0, pad)] + [(0, 0)] * (t.ndim - 2))


def _chunk(t, q):
    bt, lp = t.shape[:2]
    return jnp.moveaxis(t.reshape((bt, lp // q, q) + t.shape[2:]), 1, 0)


def _unchunk(t):
    nc, bt, q = t.shape[:3]
    return jnp.moveaxis(t, 0, 1).reshape((bt, nc * q) + t.shape[3:])


def _masked_decay(seg, mask):
    return jnp.where(mask, jnp.exp(jnp.where(mask, seg, 0.0)), 0.0)


def ssd_chunk_scan(x, dt, a, b, c, h0):
    bt, L = x.shape[:2]
    q = min(SSD_CHUNK, L)
    lp = -(-L // q) * q
    x, dt, b, c = (_pad_time(t, lp) for t in (x, dt, b, c))
    G, R, P, N = SSD_N_GROUPS, SSD_HEADS_PER_GROUP, SSD_HEAD_DIM, SSD_D_STATE
    x = x.reshape(bt, lp, G, R, P)
    dt = dt.reshape(bt, lp, G, R)
    a = a.reshape(G, R)
    causal = jnp.tril(jnp.ones((q, q), dtype=bool))[None, :, :, None, None]

    def step(h, inp):
        xq, dtq, bq, cq = inp
        cum = jnp.cumsum(dtq * a, axis=1)
        seg = cum[:, :, None] - cum[:, None, :]
        decay = _masked_decay(seg, causal)
        cb = jnp.einsum('bign,bjgn->bijg', cq, bq)
        w = cb[..., None] * decay * dtq[:, None]
        y = jnp.einsum('bijgr,bjgrp->bigrp', w, xq)
        y = y + jnp.einsum('bign,bgrpn->bigrp', cq, h) * jnp.exp(cum)[..., None]
        tail = jnp.exp(cum[:, -1:] - cum) * dtq
        h = jnp.exp(cum[:, -1])[..., None, None] * h + jnp.einsum('bjgr,bjgrp,bjgn->bgrpn', tail, xq, bq)
        return h, y

    h0 = h0.astype(jnp.float32).reshape(bt, G, R, P, N)
    hT, ys = lax.scan(step, h0, tuple(_chunk(t, q) for t in (x, dt, b, c)))
    y = _unchunk(ys)[:, :L].reshape(bt, L, SSD_N_HEADS, P)
    return y, hT.reshape(bt, SSD_N_HEADS, P, N)


def hgrn_chunk_scan(q, k, v, logf, s0):
    bt, L = q.shape[:2]
    cs = min(HGRN_CHUNK, L)
    lp = -(-L // cs) * cs
    q, k, v, logf = (_pad_time(t, lp) for t in (q, k, v, logf))
    causal = jnp.tril(jnp.ones((cs, cs), dtype=bool))[None, :, :, None, None]

    def step(s, inp):
        qc, kc, vc, lc = inp
        cum = jnp.cumsum(lc, axis=1)
        seg = cum[:, :, None] - cum[:, None]
        decay = _masked_decay(seg, causal)
        att = jnp.einsum('bihk,bjhk,bijhk->bijh', qc, kc, decay)
        o = jnp.einsum('bijh,bjhv->bihv', att, vc)
        o = o + jnp.einsum('bihk,bhkv->bihv', qc * jnp.exp(cum), s)
        kt = kc * jnp.exp(cum[:, -1:] - cum)
        s = jnp.exp(cum[:, -1])[..., None] * s + jnp.einsum('bjhk,bjhv->bhkv', kt, vc)
        return s, o

    sT, os_ = lax.scan(step, s0.astype(jnp.float32), tuple(_chunk(t, cs) for t in (q, k, v, logf)))
    return _unchunk(os_)[:, :L], sT


def ssd_mixer(u, conv_state, ssm_state, w_in, conv_w, conv_b, dt_bias, a_log, d_skip, norm_w, w_out):
    bt, L, _ = u.shape
    proj = u @ w_in
    z = proj[..., :SSD_D_INNER]
    xbc = proj[..., SSD_D_INNER:SSD_D_INNER + SSD_CONV_DIM]
    dt_raw = proj[..., SSD_D_INNER + SSD_CONV_DIM:]
    xbc_ext = jnp.concatenate([conv_state.astype(xbc.dtype), xbc], axis=1)
    new_conv = xbc_ext[:, xbc_ext.shape[1] - (SSD_CONV_W - 1):]
    conv = lax.conv_general_dilated(xbc_ext, conv_w[:, None, :].astype(xbc.dtype), window_strides=(1,),
                                    padding='VALID', dimension_numbers=('NWC', 'WIO', 'NWC'),
                                    feature_group_count=SSD_CONV_DIM)
    xbc = jax.nn.silu((conv + conv_b).astype(jnp.float32))
    xs = xbc[..., :SSD_D_INNER].reshape(bt, L, SSD_N_HEADS, SSD_HEAD_DIM)
    bs = xbc[..., SSD_D_INNER:SSD_D_INNER + SSD_GN].reshape(bt, L, SSD_N_GROUPS, SSD_D_STATE)
    cs = xbc[..., SSD_D_INNER + SSD_GN:].reshape(bt, L, SSD_N_GROUPS, SSD_D_STATE)
    dt = jax.nn.softplus(dt_raw.astype(jnp.float32) + dt_bias.astype(jnp.float32))
    a = -jnp.exp(a_log.astype(jnp.float32))
    y, h = ssd_chunk_scan(xs, dt, a, bs, cs, ssm_state)
    y = y + xs * d_skip.astype(jnp.float32)[:, None]
    y = y.reshape(bt, L, SSD_D_INNER) * jax.nn.silu(z.astype(jnp.float32))
    y = group_rmsnorm(y, norm_w, SSD_N_GROUPS).astype(u.dtype)
    return y @ w_out, new_conv, h


def hgrn_mixer(u, state, lb, w_in, norm_w, w_out):
    bt, L, _ = u.shape
    proj = (u @ w_in).astype(jnp.float32)
    q = jax.nn.silu(proj[..., :HGRN_F])
    fz = proj[..., HGRN_F:2 * HGRN_F]
    v = proj[..., 2 * HGRN_F:2 * HGRN_F + D_MODEL]
    g = proj[..., 2 * HGRN_F + D_MODEL:]
    lb = lb.astype(jnp.float32)
    logf = jnp.logaddexp(jax.nn.log_sigmoid(fz), jnp.log(jnp.maximum(lb, LB_FLOOR)) + jax.nn.log_sigmoid(-fz))
    k = (1.0 - lb) * jax.nn.sigmoid(-fz)
    hs = (bt, L, HGRN_N_HEADS)
    o, s = hgrn_chunk_scan(q.reshape(hs + (HGRN_DK,)), k.reshape(hs + (HGRN_DK,)),
                           v.reshape(hs + (HGRN_DV,)), logf.reshape(hs + (HGRN_DK,)), state)
    o = group_rmsnorm(o.reshape(bt, L, D_MODEL), norm_w, HGRN_N_HEADS) * jax.nn.silu(g)
    return o.astype(u.dtype) @ w_out, s


def squared_relu_mlp(u, w_up, w_down):
    return jnp.square(jax.nn.relu(u @ w_up)) @ w_down


def _trunk(x, conv_states, ssm_states, hgrn_states, hgrn_lb, norm_mix_w, norm_mlp_w, norm_f_w,
           ssd_w_in, ssd_conv_w, ssd_conv_b, ssd_dt_bias, ssd_a_log, ssd_d, ssd_norm_w, ssd_w_out,
           hgrn_w_in, hgrn_norm_w, hgrn_w_out, mlp_w_up, mlp_w_down):
    h = x
    new_conv, new_ssm, new_hgrn = [], [], []
    for layer in range(DEPTH):
        u = rmsnorm(h, norm_mix_w[layer])
        j = layer // N_MIXERS
        if layer % N_MIXERS == 0:
            out, cst, sst = ssd_mixer(u, conv_states[j], ssm_states[j], ssd_w_in[j], ssd_conv_w[j], ssd_conv_b[j],
                                      ssd_dt_bias[j], ssd_a_log[j], ssd_d[j], ssd_norm_w[j], ssd_w_out[j])
            new_conv.append(cst)
            new_ssm.append(sst)
        else:
            out, hst = hgrn_mixer(u, hgrn_states[j], hgrn_lb[j], hgrn_w_in[j], hgrn_norm_w[j], hgrn_w_out[j])
            new_hgrn.append(hst)
        h = h + out.astype(h.dtype)
        h = h + squared_relu_mlp(rmsnorm(h, norm_mlp_w[layer]), mlp_w_up[layer], mlp_w_down[layer]).astype(h.dtype)
    y = rmsnorm(h, norm_f_w)
    return y, jnp.stack(new_conv), jnp.stack(new_ssm), jnp.stack(new_hgrn)


def setup_inputs(seed: int = 0) -> dict:
    key = jax.random.key(seed)
    ks = iter(jax.random.split(key, 32))

    def nrm(shape, scale):
        return jax.random.normal(next(ks), shape, jnp.float32) * scale

    LA, LB = N_SSD_LAYERS, N_HGRN_LAYERS
    x_prompt = nrm((BATCH, SEQ, D_MODEL), 1.0)
    x_sample = nrm((DEC_BATCH, DEC_SEQ, D_MODEL), 1.0)
    state_ssd_conv = nrm((LA, DEC_BATCH, SSD_CONV_W - 1, SSD_CONV_DIM), 1.0)
    state_ssd_ssm = nrm((LA, DEC_BATCH, SSD_N_HEADS, SSD_HEAD_DIM, SSD_D_STATE), 0.5)
    state_hgrn = nrm((LB, DEC_BATCH, HGRN_N_HEADS, HGRN_DK, HGRN_DV), 0.5)
    norm_mix_w = 1.0 + nrm((DEPTH, D_MODEL), 0.02)
    norm_mlp_w = 1.0 + nrm((DEPTH, D_MODEL), 0.02)
    norm_f_w = 1.0 + nrm((D_MODEL,), 0.02)
    ssd_w_in = nrm((LA, D_MODEL, SSD_IN_DIM), D_MODEL ** -0.5)
    ssd_conv_w = nrm((LA, SSD_CONV_W, SSD_CONV_DIM), SSD_CONV_W ** -0.5)
    ssd_conv_b = nrm((LA, SSD_CONV_DIM), 0.02)
    dt0 = jnp.exp(jax.random.uniform(next(ks), (LA, SSD_N_HEADS), jnp.float32,
                                     minval=math.log(1e-3), maxval=math.log(1e-1)))
    ssd_dt_bias = dt0 + jnp.log(-jnp.expm1(-dt0))
    ssd_a_log = jnp.log(jax.random.uniform(next(ks), (LA, SSD_N_HEADS), jnp.float32, minval=1.0, maxval=16.0))
    ssd_d = 1.0 + nrm((LA, SSD_N_HEADS), 0.02)
    ssd_norm_w = 1.0 + nrm((LA, SSD_D_INNER), 0.02)
    ssd_w_out = nrm((LA, SSD_D_INNER, D_MODEL), SSD_D_INNER ** -0.5)
    hgrn_w_in = nrm((LB, D_MODEL, HGRN_IN_DIM), D_MODEL ** -0.5)
    hgrn_lb_raw = nrm((LB, HGRN_F), 1.0)
    hgrn_norm_w = 1.0 + nrm((LB, D_MODEL), 0.02)
    hgrn_w_out = nrm((LB, D_MODEL, D_MODEL), D_MODEL ** -0.5)
    mlp_w_up = nrm((DEPTH, D_MODEL, D_FF), D_MODEL ** -0.5)
    mlp_w_down = nrm((DEPTH, D_FF, D_MODEL), D_FF ** -0.5)
    return {'x_prompt': x_prompt, 'x_sample': x_sample,
            'state_ssd_conv': state_ssd_conv, 'state_ssd_ssm': state_ssd_ssm, 'state_hgrn': state_hgrn,
            'norm_mix_w': norm_mix_w, 'norm_mlp_w': norm_mlp_w, 'norm_f_w': norm_f_w,
            'ssd_w_in': ssd_w_in, 'ssd_conv_w': ssd_conv_w, 'ssd_conv_b': ssd_conv_b,
            'ssd_dt_bias': ssd_dt_bias, 'ssd_a_log': ssd_a_log, 'ssd_d': ssd_d,
            'ssd_norm_w': ssd_norm_w, 'ssd_w_out': ssd_w_out,
            'hgrn_w_in': hgrn_w_in, 'hgrn_lb_raw': hgrn_lb_raw, 'hgrn_norm_w': hgrn_norm_w, 'hgrn_w_out': hgrn_w_out,
            'mlp_w_up': mlp_w_up, 'mlp_w_down': mlp_w_down}


def reference(x_prompt, x_sample, state_ssd_conv, state_ssd_ssm, state_hgrn,
              norm_mix_w, norm_mlp_w, norm_f_w,
              ssd_w_in, ssd_conv_w, ssd_conv_b, ssd_dt_bias, ssd_a_log, ssd_d, ssd_norm_w, ssd_w_out,
              hgrn_w_in, hgrn_lb_raw, hgrn_norm_w, hgrn_w_out, mlp_w_up, mlp_w_down):
    p = jax.nn.softmax(hgrn_lb_raw.astype(jnp.float32), axis=0)
    hgrn_lb = jnp.cumsum(p, axis=0) - p[0]
    bp = x_prompt.shape[0]
    zero_conv = jnp.zeros((N_SSD_LAYERS, bp, SSD_CONV_W - 1, SSD_CONV_DIM), x_prompt.dtype)
    zero_ssm = jnp.zeros((N_SSD_LAYERS, bp, SSD_N_HEADS, SSD_HEAD_DIM, SSD_D_STATE), jnp.float32)
    zero_hgrn = jnp.zeros((N_HGRN_LAYERS, bp, HGRN_N_HEADS, HGRN_DK, HGRN_DV), jnp.float32)
    y_prompt, conv_p, ssm_p, hgrn_p = _trunk(
        x_prompt, zero_conv, zero_ssm, zero_hgrn, hgrn_lb, norm_mix_w, norm_mlp_w, norm_f_w,
        ssd_w_in, ssd_conv_w, ssd_conv_b, ssd_dt_bias, ssd_a_log, ssd_d, ssd_norm_w, ssd_w_out,
        hgrn_w_in, hgrn_norm_w, hgrn_w_out, mlp_w_up, mlp_w_down)
    y_sample, conv_s, ssm_s, hgrn_s = _trunk(
        x_sample, state_ssd_conv, state_ssd_ssm, state_hgrn, hgrn_lb, norm_mix_w, norm_mlp_w, norm_f_w,
        ssd_w_in, ssd_conv_w, ssd_conv_b, ssd_dt_bias, ssd_a_log, ssd_d, ssd_norm_w, ssd_w_out,
        hgrn_w_in, hgrn_norm_w, hgrn_w_out, mlp_w_up, mlp_w_down)
    return (y_prompt, y_sample, conv_p, ssm_p, hgrn_p, conv_s, ssm_s, hgrn_s)
```

```python
import numpy as np
import concourse.bass as bass
import concourse.mybir as mybir
from concourse.bass_utils import run_bass_kernel_spmd

F32 = mybir.dt.float32
BF16 = mybir.dt.bfloat16
AF = mybir.ActivationFunctionType
ALU = mybir.AluOpType

D = 1024
SEQ = 2048
NSEQ_S = 16
LS = 4
EPS = 1e-5
ENGS = ("sp", "pe", "act", "dve", "pool")
PAGE = 64
EPOCH = 12000
WSLOT = 4096
NBUF = 5
ARENA = 27392


class Inst:
    __slots__ = ("eng", "fn", "waits", "sem", "val", "idx", "needs_inc", "is_dma", "deps", "succ", "ndeps",
                 "est", "occ", "lat", "grp", "gidx", "fin", "tag", "st", "crit")

    def __init__(self, eng, fn, is_dma):
        self.eng = eng
        self.fn = fn
        self.waits = []
        self.sem = None
        self.val = None
        self.idx = -1
        self.needs_inc = False
        self.is_dma = is_dma
        self.deps = ()
        self.succ = []
        self.ndeps = 0
        self.est = 0.0
        self.occ = 0.1
        self.lat = 0.1
        self.grp = None
        self.gidx = 0
        self.fin = 0.0
        self.tag = None
        self.st = 0.0
        self.crit = None


_ESZ = {}


def _esize(dt):
    k = str(dt)
    v = _ESZ.get(k)
    if v is None:
        v = 2 if ("bfloat16" in k or "float16" in k) else 4
        _ESZ[k] = v
    return v


def ap_pages(ap):
    space = str(ap.space).upper()
    if "DRAM" in space or "HBM" in space:
        return ()
    name = ap.tensor.name
    if "PSUM" in space:
        return ((name, 0),)
    es = _esize(ap.dtype)
    apl = ap.ap
    row = apl[0][0]
    foff = (ap.offset % row if row > 0 else ap.offset) * es
    dims = [(abs(s) * es, n) for (s, n) in apl[1:] if n > 1 and s != 0]
    PB = 256
    if not dims:
        return ((name, foff // PB),)
    dims.sort()
    s0, n0 = dims[0]
    inner = (n0 - 1) * s0 + es
    outer = dims[1:]
    nouter = 1
    for _, n in outer:
        nouter *= n
    out = set()
    if nouter <= 512:
        starts = [foff]
        for s_, n in outer:
            starts = [b_ + i * s_ for b_ in starts for i in range(n)]
        for b_ in starts:
            for pg in range(b_ // PB, (b_ + inner - 1) // PB + 1):
                out.add((name, pg))
    else:
        ext = inner + sum((n - 1) * s_ for s_, n in outer)
        for pg in range(foff // PB, (foff + ext - 1) // PB + 1):
            out.add((name, pg))
    return tuple(out)


class Prog:
    def __init__(self, nc, n_dma_sems=24, n_epochs=14):
        self.nc = nc
        self.plan = False
        self.sched = True
        self.all = []
        self.q = {e: [] for e in ENGS}
        self.lastw = {}
        self.readers = {}
        self.dma_sems = {"sp": [nc.alloc_semaphore(f"dq{i}") for i in range(n_dma_sems)],
                         "pool": [nc.alloc_semaphore(f"dg{i}") for i in range(8)]}
        self.dma_n = {"sp": 0, "pool": 0}
        self.dma_last = {"sp": [None] * n_dma_sems, "pool": [None] * 8}
        self.eng_sems = {e: [nc.alloc_semaphore(f"s_{e}_{k}") for k in range(n_epochs)]
                         for e in ("pe", "act", "dve", "pool")}
        self.n_inst = 0
        self.tag = None
        self.prio = "order"
        self.prio_w = 0.0

    def emit(self, eng, fn, outs=(), ins=(), dma=False, occ=0.2, lat=None, grp=None, extra=()):
        if self.plan:
            return None
        inst = Inst(eng, fn, dma)
        inst.gidx = len(self.all)
        inst.tag = self.tag
        inst.occ = occ
        inst.lat = occ if lat is None else lat
        inst.grp = grp
        rp = set()
        for a in ins:
            rp.update(ap_pages(a))
        wp = set()
        for a in outs:
            wp.update(ap_pages(a))
        deps = set()
        for r in rp:
            w = self.lastw.get(r)
            if w is not None:
                deps.add(w)
        for w_ in wp:
            w = self.lastw.get(w_)
            if w is not None:
                deps.add(w)
            rd = self.readers.get(w_)
            if rd:
                deps.update(rd)
        if dma:
            pool_ = self.dma_sems[eng]
            k = self.dma_n[eng] % len(pool_)
            inst.sem = pool_[k]
            inst.val = 16 * (self.dma_n[eng] // len(pool_) + 1)
            prev = self.dma_last[eng][k]
            if prev is not None:
                deps.add(prev)
            self.dma_last[eng][k] = inst
            self.dma_n[eng] += 1
        for x_ in extra:
            if x_ is not None:
                deps.add(x_)
        deps.discard(inst)
        inst.deps = deps
        for r in rp:
            if r not in wp:
                self.readers.setdefault(r, []).append(inst)
        for w_ in wp:
            self.lastw[w_] = inst
            self.readers[w_] = []
        self.all.append(inst)
        self.n_inst += 1
        return inst

    def schedule(self):
        import heapq
        HOP = 0.3
        for inst in self.all:
            inst.ndeps = len(inst.deps)
            for d in inst.deps:
                d.succ.append(inst)
        if self.prio == "cp":
            rank = {}
            for inst in reversed(self.all):
                r = 0.0
                for s_ in inst.succ:
                    rs_ = rank[s_]
                    if rs_ > r:
                        r = rs_
                rank[inst] = r + (inst.lat if inst.is_dma else inst.occ)
            W_ = self.prio_w
            for inst in self.all:
                inst.gidx = inst.gidx - W_ * rank[inst]
        fut = {e: [] for e in ENGS}
        avail = {e: [] for e in ENGS}
        free = {e: 0.0 for e in ENGS}
        cur_grp = [None]
        for inst in self.all:
            if inst.ndeps == 0:
                heapq.heappush(fut[inst.eng], (0.0, inst.gidx, inst))
        nleft = len(self.all)
        while nleft:
            best = None
            for e in ENGS:
                fq, aq = fut[e], avail[e]
                while fq and fq[0][0] <= free[e]:
                    _, gi, it = heapq.heappop(fq)
                    heapq.heappush(aq, (gi, it))
                if aq:
                    cand = (free[e], aq[0][0], e, True)
                elif fq:
                    cand = (fq[0][0], fq[0][1], e, False)
                else:
                    continue
                if best is None or cand < best:
                    best = cand
            start, _, e, from_av = best
            if from_av:
                aq = avail[e]
                pick = None
                if e == "act" and cur_grp[0] is not None and len(aq) > 1:
                    small = heapq.nsmallest(6, aq)
                    for gi, it in small:
                        if it.grp is None or it.grp == cur_grp[0]:
                            pick = (gi, it)
                            break
                    if pick is not None and pick != aq[0]:
                        aq.remove(pick)
                        heapq.heapify(aq)
                    else:
                        pick = heapq.heappop(aq)
                else:
                    pick = heapq.heappop(aq)
                inst = pick[1]
            else:
                _, _, inst = heapq.heappop(fut[e])
            occ = inst.occ
            if e == "act" and inst.grp is not None:
                if cur_grp[0] is not None and cur_grp[0] != inst.grp:
                    occ += 1.3
                cur_grp[0] = inst.grp
            inst.st = start
            if self.q[e] and start <= free[e] + 1e-9 and free[e] > 0:
                inst.crit = self.q[e][-1]
            else:
                cd = None
                for d_ in inst.deps:
                    if cd is None or d_.fin > cd.fin:
                        cd = d_
                inst.crit = cd
            inst.fin = start + (inst.lat if inst.is_dma else occ)
            free[e] = start + occ
            inst.idx = len(self.q[e])
            self.q[e].append(inst)
            nleft -= 1
            for s_ in inst.succ:
                t_ = inst.fin + (HOP if s_.eng != e else 0.06)
                if t_ > s_.est:
                    s_.est = t_
                s_.ndeps -= 1
                if s_.ndeps == 0:
                    heapq.heappush(fut[s_.eng], (s_.est, s_.gidx, s_))
        self.makespan = max(free.values())

    def finalize(self):
        if self.sched:
            self.schedule()
        else:
            for inst in self.all:
                inst.idx = len(self.q[inst.eng])
                self.q[inst.eng].append(inst)
        for e in ENGS:
            waited = {}
            for inst in self.q[e]:
                best = {}
                for d in inst.deps:
                    if d.is_dma:
                        key = ("dma", d.sem.name)
                        if waited.get(key, 0) >= d.val:
                            continue
                        cur = best.get(key)
                        if cur is None or d.val > cur.val:
                            best[key] = d
                    else:
                        if d.eng == "pe" and e == "pe":
                            continue
                        if waited.get(d.eng, -1) >= d.idx:
                            continue
                        cur = best.get(d.eng)
                        if cur is None or d.idx > cur.idx:
                            best[d.eng] = d
                for key, d in best.items():
                    if d.is_dma:
                        waited[key] = d.val
                    else:
                        waited[d.eng] = d.idx
                        d.needs_inc = True
                    inst.waits.append(d)
        for e in ("pe", "act", "dve", "pool"):
            k = 0
            for inst in self.q[e]:
                if inst.needs_inc:
                    inst.sem = self.eng_sems[e][k // EPOCH]
                    inst.val = k % EPOCH + 1
                    k += 1
            assert k <= EPOCH * len(self.eng_sems[e]), (e, k)
        fin = Inst("sp", None, False)
        for lst in self.dma_last.values():
            for d in lst:
                if d is not None:
                    fin.waits.append(d)
        self.q["sp"].append(fin)

    def replay(self, eng_name, e):
        for inst in self.q[eng_name]:
            for d in inst.waits:
                e.wait_ge(d.sem, d.val)
            if inst.fn is None:
                continue
            bi = inst.fn(e)
            if inst.is_dma:
                bi.then_inc(inst.sem, 16)
            elif inst.needs_inc:
                bi.then_inc(inst.sem, 1)

    def run(self):
        self.finalize()
        with self.nc.Block() as block:
            @block.sync
            def _(e):
                self.replay("sp", e)

            @block.tensor
            def _(e):
                self.replay("pe", e)

            @block.scalar
            def _(e):
                self.replay("act", e)

            @block.vector
            def _(e):
                self.replay("dve", e)

            @block.gpsimd
            def _(e):
                self.replay("pool", e)


def _isap(x):
    return not isinstance(x, (int, float))


C_ID, C_TRI, C_LST, C_BD32, C_MB32, C_TRIS, C_LSTS, C_MSEL, C_MSBC, C_RM32, C_RM4, C_END = (
    0, 128, 256, 384, 512, 516, 580, 644, 660, 1684, 1940, 2004)


def _const_table():
    c = np.zeros((128, C_END), np.float32)
    k = np.arange(128)
    c[:, C_ID:C_ID + 128] = np.eye(128)
    c[:, C_TRI:C_TRI + 128] = (k[:, None] <= k[None, :])
    c[:, C_LST:C_LST + 128] = (k[:, None] > k[None, :])
    c[:, C_BD32:C_BD32 + 128] = (k[:, None] <= k[None, :]) & (k[:, None] // 32 == k[None, :] // 32)
    c[:, C_MB32:C_MB32 + 4] = (k[:, None] // 32 == np.arange(4)[None, :])
    k6 = np.arange(64)
    same = (k6[:, None] // 4 == k6[None, :] // 4)
    c[:64, C_TRIS:C_TRIS + 64] = (k6[:, None] <= k6[None, :]) & same
    c[:64, C_LSTS:C_LSTS + 64] = (k6[:, None] > k6[None, :]) & same
    c[:64, C_MSEL:C_MSEL + 16] = (k6[:, None] // 4 == np.arange(16)[None, :])
    ms = (np.arange(16)[:, None] == (k6[None, :] // 4)).astype(np.float32)
    c[:, C_MSBC:C_MSBC + 1024] = ms.reshape(1, 1024)
    t = np.arange(256)
    c[:, C_RM32:C_RM32 + 256] = (t % 32 != 0)[None, :]
    c[:, C_RM4:C_RM4 + 64] = (np.arange(64) % 4 != 0)[None, :]
    return c


PC_MIX, PC_MLP, PC_FIN, PC_CW, PC_CB, PC_SNW, PC_HNW, PC_LBR, PC_END = 0, 32, 64, 72, 264, 312, 344, 360, 376


def _pack_cols(inp):
    def cols(v):
        v = np.asarray(v, np.float32)
        sh = v.shape[:-1]
        n = v.shape[-1] // 128
        return np.moveaxis(v.reshape(sh + (n, 128)), -1, 0)
    pc = np.zeros((128, PC_END), np.float32)
    pc[:, PC_MIX:PC_MIX + 32] = cols(inp["norm_mix_w"]).reshape(128, 32)
    pc[:, PC_MLP:PC_MLP + 32] = cols(inp["norm_mlp_w"]).reshape(128, 32)
    pc[:, PC_FIN:PC_FIN + 8] = cols(inp["norm_f_w"]).reshape(128, 8)
    cw = cols(inp["ssd_conv_w"])
    pc[:, PC_CW:PC_CW + 192] = np.transpose(cw, (0, 1, 3, 2)).reshape(128, 192)
    pc[:, PC_CB:PC_CB + 48] = cols(inp["ssd_conv_b"]).reshape(128, 48)
    pc[:, PC_SNW:PC_SNW + 32] = cols(inp["ssd_norm_w"]).reshape(128, 32)
    pc[:, PC_HNW:PC_HNW + 16] = cols(inp["hgrn_norm_w"]).reshape(128, 16)
    pc[:, PC_LBR:PC_LBR + 16] = cols(inp["hgrn_lb_raw"]).reshape(128, 16)
    return pc


def _pack_rows(inp):
    pr = np.zeros((128, 192), np.float32)
    pr[:, 0:64] = np.asarray(inp["ssd_dt_bias"], np.float32).reshape(1, 64)
    pr[:, 64:128] = np.asarray(inp["ssd_a_log"], np.float32).reshape(1, 64)
    pr[:, 128:192] = np.asarray(inp["ssd_d"], np.float32).reshape(1, 64)
    return pr


def build(nc, cfg):
    NPT = cfg.get("np_tiles", 8)
    NSUB = cfg.get("nsub", 8)
    DO_SAMPLE = cfg.get("do_sample", True)
    FINAL_NORM = cfg.get("final_norm", True)
    P = Prog(nc)
    P.sched = cfg.get("sched", True)
    P.prio = cfg.get("prio", "cp")
    P.prio_w = cfg.get("prio_w", 10.0)
    USE_WCACHE = cfg.get("wcache", True)
    POOLX = cfg.get("poolx", "dve")
    SAMPLE_FIRST = cfg.get("sample_first", False)

    def din(name, shape):
        return nc.dram_tensor(name, list(shape), F32, kind="ExternalInput")

    def dout(name, shape):
        return nc.dram_tensor(name, list(shape), F32, kind="ExternalOutput")

    xp = din("xp", [SEQ, D])
    xs = din("xs", [64, D])
    st_conv = din("st_conv", [2, 48, 3072])
    st_ssm = din("st_ssm", [2, 16, 2048, 128])
    st_hgrn = din("st_hgrn", [2, 16, 8, 128, 128])
    w_sin = din("ssd_w_in", [2, 1024, 5152])
    w_sout = din("ssd_w_out", [2, 2048, 1024])
    w_hin = din("hgrn_w_in", [2, 1024, 4096])
    w_hout = din("hgrn_w_out", [2, 1024, 1024])
    w_up = din("mlp_w_up", [4, 1024, 4096])
    w_dn = din("mlp_w_down", [4, 4096, 1024])
    pcols_d = din("pcols", [128, PC_END])
    prows_d = din("prows", [128, 192])
    y_p = dout("y_p", [SEQ, D])
    y_s = dout("y_s", [64, D])
    conv_p = dout("conv_p", [2, 3, 3072])
    ssm_p = dout("ssm_p", [2, 2048, 128])
    hgrn_p = dout("hgrn_p", [2, 8, 128, 128])
    conv_s = dout("conv_s", [2, 48, 3072])
    ssm_s = dout("ssm_s", [2, 16, 2048, 128])
    hgrn_s = dout("hgrn_s", [2, 16, 8, 128, 128])
    cst_d = nc.inline_tensor(_const_table(), "cst_tab")

    sb = nc.alloc_sbuf_tensor
    cst = sb("cst", [128, C_END], F32)
    pcols = sb("pcols_sb", [128, PC_END], F32)
    prows = sb("prows_sb", [128, 192], F32)
    ones = sb("ones", [128, 128], F32)
    misc = sb("misc", [128, 512], F32)
    rstd = sb("rstd", [128, 256], F32)
    hT = sb("hT", [128, 8, 256], F32)
    uT = sb("uT", [128, 8, 256], BF16)
    yTb = sb("yTb", [128, 16, 256], BF16)
    ones_b = sb("ones_b", [128, 128], BF16)
    hst = [sb(f"hst{j}", [128, 2048], F32) for j in range(2)]
    shg = [sb(f"shg{j}", [128, 8, 128], F32) for j in range(2)]
    wring = [sb(f"wring{i}", [128, WSLOT], BF16) for i in range(NBUF)]
    arena = sb("arena", [128, ARENA], F32)
    PS = [nc.alloc_psum_tensor(f"ps{i}", [128, 512], F32) for i in range(8)]

    ident = cst[:, C_ID:C_ID + 128]

    def fsz(ap):
        n = 1
        for d_ in ap.shape[1:]:
            n *= d_
        return n

    GRP = {str(AF.Exp): "E", str(AF.Ln): "E", str(AF.Sigmoid): "S", str(AF.Sqrt): "Q", str(AF.Silu): "U"}

    def MM(out, lhsT, rhs, start, stop):
        passes = 4 if _esize(rhs.dtype) == 4 else 1
        P.emit("pe", lambda e: e.matmul(out, lhsT, rhs, start=start, stop=stop, skip_group_check=True),
               [out], [lhsT, rhs], occ=0.015 + fsz(rhs) * passes / 2000.0, lat=0.2 + fsz(rhs) * passes / 2000.0)

    def TR(out, in_, k):
        idn = cst[0:k, C_ID:C_ID + k]
        P.emit("pe", lambda e: e.transpose(out, in_, idn), [out], [in_, idn], occ=0.12, lat=0.3)

    def ACT(out, in_, func, bias=None, scale=None, accum=None):
        kw = {}
        ins = [in_]
        outs = [out]
        if bias is not None:
            kw["bias"] = bias
            if _isap(bias):
                ins.append(bias)
        if scale is not None:
            kw["scale"] = scale
            if _isap(scale):
                ins.append(scale)
        if accum is not None:
            kw["accum_out"] = accum
            outs.append(accum)
        c = 0.25 + fsz(in_) / 1100.0 + (0.1 if accum is not None else 0.0)
        P.emit("act", lambda e: e.activation(out, in_, func, **kw), outs, ins, occ=c, lat=c + 0.1, grp=GRP.get(str(func)))

    def vcost(eng, n):
        return (0.2 + n / 900.0) if eng == "dve" else (0.4 + n / 300.0)

    def TT(eng, out, a, b, op):
        c = vcost(eng, fsz(out))
        P.emit(eng, lambda e: e.tensor_tensor(out, a, b, op), [out], [a, b], occ=c, lat=c + 0.1)

    def TS(eng, out, a, s1, op0, s2=None, op1=None):
        ins = [a] + ([s1] if _isap(s1) else []) + ([s2] if (s2 is not None and _isap(s2)) else [])
        c = vcost(eng, fsz(out))
        if op1 is None:
            P.emit(eng, lambda e: e.tensor_scalar(out, a, s1, None, op0), [out], ins, occ=c, lat=c + 0.1)
        else:
            P.emit(eng, lambda e: e.tensor_scalar(out, a, s1, s2, op0, op1), [out], ins, occ=c, lat=c + 0.1)

    def STT(out, in0, scalar, in1, op0, op1):
        ins = [in0, in1] + ([scalar] if _isap(scalar) else [])
        c = 0.3 + fsz(out) / 900.0
        P.emit("dve", lambda e: e.scalar_tensor_tensor(out, in0, scalar, in1, op0, op1), [out], ins, occ=c, lat=c + 0.1)

    def CP(eng, out, in_):
        if eng == "act":
            c = 0.25 + fsz(in_) / 1100.0
            P.emit("act", lambda e: e.activation(out, in_, AF.Copy), [out], [in_], occ=c, lat=c + 0.1)
        else:
            c = vcost(eng, fsz(out))
            P.emit(eng, lambda e: e.tensor_copy(out, in_), [out], [in_], occ=c, lat=c + 0.1)

    def MEMSET(eng, out, v):
        c = vcost(eng, fsz(out))
        P.emit(eng, lambda e: e.memset(out, v), [out], [], occ=c, lat=c + 0.1)

    def RECIP(out, in_):
        c = 0.2 + fsz(out) / 900.0
        P.emit("dve", lambda e: e.reciprocal(out, in_), [out], [in_], occ=c, lat=c + 0.1)

    def SCAN(out, d0, d1):
        c = 0.2 + 2.0 * fsz(out) / 900.0
        P.emit("dve", lambda e: e.tensor_tensor_scan(out, d0, d1, 0.0, ALU.mult, ALU.add), [out], [d0, d1], occ=c, lat=c + 0.1)

    def dbytes(ap):
        n = 1
        for d_ in ap.shape:
            n *= d_
        return n * 4

    def DMA(out, in_, extra=()):
        return P.emit("sp", lambda e: e.dma_start(out=out, in_=in_), [out], [in_], dma=True, occ=0.15,
                      lat=2.2 + dbytes(in_) / 150e3, extra=extra)

    def WDMA(out, in_):
        return P.emit("pool", lambda e: e.dma_start(out=out, in_=in_), [out], [in_], dma=True, occ=1.0,
                      lat=3.0 + dbytes(in_) / 150e3)

    def av(off, np_, *shape):
        n = 1
        for s in shape:
            n *= s
        assert off + n <= ARENA, (off, shape)
        a = arena[0:np_, off:off + n]
        if len(shape) == 2:
            a = a.rearrange("p (a b) -> p a b", b=shape[1])
        elif len(shape) == 3:
            a = a.rearrange("p (a b c) -> p a b c", b=shape[1], c=shape[2])
        return a

    def avb(off, np_, *shape):
        n = 1
        for s_ in shape:
            n *= s_
        assert n % 2 == 0 and off + n // 2 <= ARENA, (off, shape)
        a = arena[0:np_, off:off + n // 2].bitcast(BF16)
        if len(shape) == 2:
            a = a.rearrange("p (a b) -> p a b", b=shape[1])
        elif len(shape) == 3:
            a = a.rearrange("p (a b c) -> p a b c", b=shape[1], c=shape[2])
        return a

    def bcl(ap2, n):
        sh = list(ap2.shape)
        return ap2.unsqueeze(len(sh)).broadcast_to(sh + [n])

    def bcm(ap2, n):
        sh = list(ap2.shape)
        return ap2.unsqueeze(1).broadcast_to([sh[0], n] + sh[1:])

    class PsumAlloc:
        def __init__(self):
            self.free = list(range(8))
            self.i = 0

        def get(self):
            self.i = (self.i + 1) % len(self.free)
            return PS[self.free[self.i]]

        def reserve(self, n):
            got = [self.free.pop() for _ in range(n)]
            return [PS[g] for g in got], got

        def release(self, ids):
            self.free.extend(ids)
            self.free.sort()

    psum = PsumAlloc()

    class WStream:
        def __init__(self):
            self.specs = []
            self.cur = 0
            self.issued = 0
            self.wb = {}
            self.cache = None

        def view(self, slot, shape):
            kc, nb = shape[1], shape[2]
            return wring[slot][:, 0:kc * nb].rearrange("p (a b) -> p a b", b=nb)

        def next(self, ap):
            if P.plan:
                self.specs.append(ap)
                return self.view(0, ap.shape)
            i = self.cur
            assert tuple(self.specs[i].shape) == tuple(ap.shape)
            npass = max(1, NPT + (1 if DO_SAMPLE else 0))
            nb_t = len(self.specs) // npass
            if self.cache is None and USE_WCACHE:
                self.cache = nc.dram_tensor("wcache", [nb_t, 128, WSLOT], BF16)
            while self.issued < min(i + NBUF, len(self.specs)):
                k = self.issued
                sp_ap = self.specs[k]
                n_ = sp_ap.shape[1] * sp_ap.shape[2]
                flat = wring[k % NBUF][:, 0:n_]
                cb = k % nb_t
                if not USE_WCACHE:
                    WDMA(self.view(k % NBUF, sp_ap.shape), sp_ap)
                elif k < nb_t:
                    WDMA(self.view(k % NBUF, sp_ap.shape), sp_ap)
                    self.wb[cb] = DMA(self.cache[cb][:, 0:n_], flat)
                else:
                    DMA(flat, self.cache[cb][:, 0:n_], extra=[self.wb[cb]])
                self.issued += 1
            self.cur += 1
            return self.view(i % NBUF, ap.shape)

    WS = WStream()

    def wblk(w, l, c0, nb):
        return w[l][:, c0:c0 + nb].rearrange("(kc p) n -> p kc n", p=128)

    evac_rr = [0]

    def evac_copy(out, in_):
        CP("act", out, in_)

    def setup():
        DMA(cst[:, :], cst_d.ap())
        DMA(pcols[:, :], pcols_d[:, :])
        DMA(prows[:, :], prows_d[:, :])
        MEMSET("pool", ones[:, :], 1.0)
        MEMSET("pool", ones_b[:, :], 1.0)
        ACT(misc[:, 0:64], prows[:, 64:128], AF.Exp)
        TS("dve", misc[:, 0:64], misc[:, 0:64], -1.0, ALU.mult)
        MEMSET("pool", misc[:, 64:72], 0.0)
        TT("dve", misc[:, 72:80], pcols[:, PC_LBR + 8:PC_LBR + 16], pcols[:, PC_LBR:PC_LBR + 8], ALU.subtract)
        ACT(misc[:, 72:80], misc[:, 72:80], AF.Sigmoid)
        TS("dve", misc[:, 80:96], misc[:, 64:80], -1.0, ALU.mult, 1.0, ALU.add)
        for j in range(2):
            MEMSET("pool", hst[j][:, :], 0.0)
            MEMSET("pool", shg[j][:, :, :], 0.0)

    a_bc = misc[:, 0:64]

    def hist(j):
        return misc[:, 96 + j * 72:96 + (j + 1) * 72].rearrange("p (a b) -> p a b", b=3)

    class T:
        pass

    def mk_tile(kind, ti):
        t = T()
        t.kind = kind
        t.ti = ti
        if kind == "p":
            t.Q = 256
            t.chunks = [(0, 128), (128, 128)]
            t.NS, t.L = 1, 256
            t.C = 32
            t.last = (ti == NPT - 1)
        else:
            t.Q = 64
            t.chunks = [(0, 64)]
            t.NS, t.L = 16, 4
            t.C = 4
            t.last = True
        return t

    def rmsnorm(t, wc0, dst=None):
        Q = t.Q
        if dst is None:
            dst = uT
        sq = yTb[:, 0:8, 0:Q]
        for kc in range(8):
            ACT(sq[:, kc, :], hT[:, kc, 0:Q], AF.Square)
        b = psum.get()
        for kc in range(8):
            MM(b[:, 0:Q], ones_b[:, :], sq[:, kc, :], kc == 0, kc == 7)
        ACT(rstd[:, 0:Q], b[:, 0:Q], AF.Ln, bias=EPS, scale=1.0 / D)
        ACT(rstd[:, 0:Q], rstd[:, 0:Q], AF.Exp, scale=-0.5)
        for kc in range(8):
            STT(dst[:, kc, 0:Q], hT[:, kc, 0:Q], pcols[:, wc0 + kc:wc0 + kc + 1], rstd[:, 0:Q], ALU.mult, ALU.mult)

    def ssd_layer(t, layer):
        j = layer // 2
        Q, NS, L = t.Q, t.NS, t.L
        sp_ = (t.kind == "s")
        if not sp_:
            o_xpre, o_xbc, o_z, o_xtok, o_yw = 0, 6216, 12360, 16456, 18504
            o_rb, o_wt, o_bt, o_cbt, o_sm = 20552, 21576, 22600, 23112, 23240
            o_xdt, o_xtl, o_yacc = 0, 2048, 4096
        else:
            o_xpre, o_xbc, o_z, o_xtok, o_yw = 0, 2688, 4224, 6272, 8320
            o_rb, o_wt, o_bt, o_cbt, o_sm = 10368, 10880, 11392, 11904, 11968
            o_xdt, o_xtl, o_yacc = 13312, 15360, 17408
            NATIN, HTS, SOUTB = [19456, 0, 24320], [21504, 13312], [6272, 8320]
            CMB, BMB = [23552, 2048], [23808, 10368]
        W_ = 3 + L
        xpre = av(o_xpre, 128, 24, NS, W_)
        xbc = av(o_xbc, 128, 24, Q)
        yT = yTb[:, :, 0:Q]
        o_dta, o_ee, o_ss, o_rs, o_dtt, o_db, o_r2 = o_sm, o_sm + 32, o_sm + 96, o_sm + 100, o_sm + 104, o_sm + 168, o_sm + 680
        PSTR = 0 if sp_ else 200
        if not sp_:
            o_db = o_sm + 104 + 64

        P.tag = (t.kind, t.ti, layer, "inproj")
        rmsnorm(t, PC_MIX + layer * 8)

        if sp_:
            stg = av(o_xdt, 48, 3072)
            DMA(stg, st_conv[j])
            for blk in range(6):
                b = psum.get()
                for q4 in range(4):
                    ch = blk * 4 + q4
                    TR(b[:, q4 * 48:(q4 + 1) * 48], stg[:, ch * 128:(ch + 1) * 128], 48)
                evac_copy(xpre[:, blk * 4:(blk + 1) * 4, :, 0:3],
                          b[:, 0:192].rearrange("p (a s k) -> p a s k", s=16, k=3))
        else:
            if t.ti == 0:
                MEMSET("pool", xpre[:, :, 0, 0:3], 0.0)
            else:
                CP("dve", xpre[:, :, 0, 0:3], hist(j))

        for xb in range(6):
            wv = WS.next(wblk(w_sin, j, 2048 + xb * 512, 512))
            for oi in range(4):
                ch = xb * 4 + oi
                b = psum.get()
                for kc in range(8):
                    MM(b[:, 0:Q], wv[:, kc, oi * 128:(oi + 1) * 128], uT[:, kc, 0:Q], kc == 0, kc == 7)
                evac_copy(xpre[:, ch, :, 3:3 + L], b[:, 0:Q].rearrange("p (s l) -> p s l", l=L))
        wv = WS.next(wblk(w_sin, j, 5120, 32))
        for ci, (o, n) in enumerate(t.chunks):
            b = psum.get()
            for kc in range(8):
                MM(b[0:n, 0:32], uT[:, kc, o:o + n], wv[:, kc, :], kc == 0, kc == 7)
            dtt = av(o_dtt + ci * 32, n, 32)
            TT("dve", dtt, b[0:n, 0:32], prows[0:n, j * 32:(j + 1) * 32], ALU.add)
            ACT(dtt, dtt, AF.Exp)
            ACT(dtt, dtt, AF.Ln, bias=1.0)

        if sp_:
            cc = av(o_xdt, 128, 24, 48)
            CP("dve", cc.rearrange("p a (s k) -> p a s k", k=3), xpre[:, :, :, 4:7])
            stg2 = av(o_xtl, 48, 3072)
            for blk in range(6):
                b = psum.get()
                for q4 in range(4):
                    ch = blk * 4 + q4
                    TR(b[0:48, q4 * 128:(q4 + 1) * 128], cc[:, ch, :], 128)
                evac_copy(stg2[:, blk * 512:(blk + 1) * 512], b[0:48, :])
            DMA(conv_s[j], stg2)
        else:
            CP("dve", hist(j), xpre[:, :, 0, Q:Q + 3])
            if t.last:
                stg2 = av(o_xtok, 3, 3072)
                for blk in range(6):
                    b = psum.get()
                    for q4 in range(4):
                        ch = blk * 4 + q4
                        TR(b[0:3, q4 * 128:(q4 + 1) * 128], xpre[:, ch, 0, Q:Q + 3], 128)
                    evac_copy(stg2[:, blk * 512:(blk + 1) * 512], b[0:3, :])
                DMA(conv_p[j], stg2)

        for cb in range(4):
            wv = WS.next(wblk(w_sin, j, cb * 512, 512))
            for ci, (o, n) in enumerate(t.chunks):
                b = psum.get()
                for kc in range(8):
                    MM(b[0:n, 0:512], uT[:, kc, o:o + n], wv[:, kc, :], kc == 0, kc == 7)
                ACT(av(o_z + ci * 2048 + cb * 512, n, 512), b[0:n, 0:512], AF.Silu)
        P.tag = (t.kind, t.ti, layer, "conv")
        def ovw(ch):
            return xbc[:, ch, :].rearrange("p (s l) -> p s l", l=L)
        for half in range(2):
            chs = range(half * 12, (half + 1) * 12)
            for ch in chs:
                cw = PC_CW + (j * 24 + ch) * 4
                ACT(ovw(ch), xpre[:, ch, :, 0:L], AF.Identity, scale=pcols[:, cw:cw + 1])
            for k in range(1, 4):
                for ch in chs:
                    cw = PC_CW + (j * 24 + ch) * 4
                    STT(ovw(ch), xpre[:, ch, :, k:k + L], pcols[:, cw + k:cw + k + 1], ovw(ch), ALU.mult, ALU.add)
            for ch in chs:
                ACT(xbc[:, ch, :], xbc[:, ch, :], AF.Silu, bias=pcols[:, PC_CB + j * 24 + ch:PC_CB + j * 24 + ch + 1])

        TRIm = cst[:, C_TRIS:C_TRIS + 64] if sp_ else cst[:, C_TRI:C_TRI + 128]
        LSTm = cst[:, C_LSTS:C_LSTS + 64] if sp_ else cst[:, C_LST:C_LST + 128]

        for ci, (o, n) in enumerate(t.chunks):
            Xtok = av(o_xtok, n, 2048)
            par = 0 if sp_ else ci % 2
            if sp_:
                Btok = avb(o_bt, n, 512)
                Xdt = avb(o_xdt, n, 2048)
                Xtl = avb(o_xtl, n, 2048)
                yacc = av(o_yacc, n, 2048)
                WTo = [0, 256]
            else:
                Btok = avb(o_bt + par * 256, n, 512)
                Xdt = avb([0, 1024][par], n, 2048)
                Xtl = avb([2048, 3072][par], n, 2048)
                yacc = av([4096, 24320][par], n, 2048)
                WTo = [26368, 26880]
            yw = av(o_yw, n, 2048)
            pso = par * 400
            dta = av(o_dta + pso, n, 32)
            Ee = av(o_ee + pso, n, 64)
            dB = av(o_db + pso, 128, NS * 32)
            dtt = av(o_dtt + ci * 32, n, 32)
            ztk = av(o_z + ci * 2048, n, 2048)
            P.tag = (t.kind, t.ti, layer, "A%d" % ci)
            for blk in range(4):
                b = psum.get()
                for q4 in range(4):
                    TR(b[0:n, q4 * 128:(q4 + 1) * 128], xbc[:, blk * 4 + q4, o:o + n], 128)
                evac_copy(Xtok[:, blk * 512:(blk + 1) * 512], b[0:n, :])
            b = psum.get()
            for g in range(4):
                TR(b[0:n, g * 128:(g + 1) * 128], xbc[:, 16 + g, o:o + n], 128)
            evac_copy(Btok[:, :], b[0:n, :])
            TT("dve", dta, dtt, a_bc[0:n, j * 32:(j + 1) * 32], ALU.mult)
            b = psum.get()
            MM(b[0:n, 0:32], TRIm[0:n, 0:n], dta, True, False)
            MM(b[0:n, 32:64], LSTm[0:n, 0:n], dta, False, True)
            ACT(Ee, b[0:n, 0:64], AF.Exp)
            if not sp_:
                b = psum.get()
                MM(b[:, 0:32], ones[0:n, :], dta, True, True)
                ACT(dB, b[:, 0:32], AF.Exp)
            X3 = Xtok.rearrange("p (h d) -> p h d", d=64)
            TT("dve", Xdt.rearrange("p (h d) -> p h d", d=64), X3, bcl(dtt, 64), ALU.mult)
            dtl = av(o_sm + 232 + pso, n, 32)
            TT("dve", dtl, dtt, Ee[:, 32:64], ALU.mult)
            TT(POOLX, Xtl.rearrange("p (h d) -> p h d", d=64), X3, bcl(dtl, 64), ALU.mult)
            TT(POOLX, yacc.rearrange("p (h d) -> p h d", d=64), X3, bcl(prows[0:n, 128 + j * 32:128 + (j + 1) * 32], 64),
               ALU.mult)
            P.tag = (t.kind, t.ti, layer, "B%d" % ci)
            RWo = [o_rb, o_wt]
            CBo = [o_cbt, o_sm + 256] if sp_ else [o_cbt, o_sm + 824]

            def stage_b1(g):
                CBTm = av(CBo[g % 2], n, n)
                Rb = av(RWo[g % 2], n, 8, n)
                b = psum.get()
                MM(b[0:n, 0:n], xbc[:, 16 + g, o:o + n], xbc[:, 20 + g, o:o + n], True, True)
                TT("dve", CBTm, b[0:n, 0:n], TRIm[0:n, 0:n], ALU.mult)
                TT("dve", Rb, bcm(TRIm[0:n, 0:n], 8), bcl(dta[:, g * 8:(g + 1) * 8], n), ALU.mult)

            def stage_b2(g):
                CBTm = av(CBo[g % 2], n, n)
                RW3 = av(RWo[g % 2], n, 8, n)
                WT = avb(WTo[g % 2], n, 8, n)
                rwf = av(RWo[g % 2], n, 8 * n)
                for hf in range(8 * n // 512):
                    bs = psum.get()
                    MM(bs[0:n, 0:512], LSTm[0:n, 0:n], rwf[:, hf * 512:(hf + 1) * 512], True, True)
                    ACT(rwf[:, hf * 512:(hf + 1) * 512], bs[0:n, 0:512], AF.Exp)
                TT("dve", WT, RW3, bcm(CBTm, 8), ALU.mult)
                b = psum.get()
                for r in range(8):
                    MM(b[0:n, r * 64:(r + 1) * 64], WT[:, r, :], Xdt[:, (g * 8 + r) * 64:(g * 8 + r + 1) * 64], r == 0, r == 7)
                TT("dve", yacc[:, g * 512:(g + 1) * 512], b[0:n, 0:512], yacc[:, g * 512:(g + 1) * 512], ALU.add)
            stage_b1(0)
            for g in range(4):
                if g + 1 < 4:
                    stage_b1(g + 1)
                stage_b2(g)
            P.tag = (t.kind, t.ti, layer, "C%d" % ci)
            YI, yid = psum.reserve(4)
            if not sp_:
                hstate = hst[j]
                for g in range(4):
                    MM(YI[g][0:n, 0:512], xbc[:, 20 + g, o:o + n], hstate[:, g * 512:(g + 1) * 512], True, True)
                ub = []
                for g in range(4):
                    b = psum.get()
                    MM(b[:, 0:512], Btok[:, g * 128:(g + 1) * 128], Xtl[:, g * 512:(g + 1) * 512], True, True)
                    ub.append(b)
                for g in range(4):
                    hs3 = hstate[:, g * 512:(g + 1) * 512].rearrange("p (h d) -> p h d", d=64)
                    TT("dve", hs3, hs3, bcl(dB[:, g * 8:(g + 1) * 8], 64), ALU.mult)
                for g in range(4):
                    TT("dve", hstate[:, g * 512:(g + 1) * 512], hstate[:, g * 512:(g + 1) * 512], ub[g][:, 0:512], ALU.add)
            else:
                dtaX = av(o_xdt, 64, 2048)
                CP("dve", dtaX.rearrange("p (h d) -> p h d", d=64), bcl(dta, 64))
                bD = psum.get()
                for c in range(16):
                    MM(bD[:, c * 16:(c + 1) * 16], dtaX[:, c * 128:(c + 1) * 128], cst[0:64, C_MSEL:C_MSEL + 16], c == 0, c == 15)
                dcolS = av(o_r2, 128, 16, 16)
                ACT(av(o_r2, 128, 256), bD[:, 0:256], AF.Exp)

                def st_load(s_):
                    DMA(av(NATIN[s_ % 3], 128, 16, 128), st_ssm[j, s_].rearrange("(c q) n -> q c n", q=128))

                def st_tr(s_):
                    natin = av(NATIN[s_ % 3], 128, 16, 128)
                    hts = avb(HTS[s_ % 2], 128, 2048)
                    for blk in range(4):
                        b = psum.get()
                        for q4 in range(4):
                            TR(b[:, q4 * 128:(q4 + 1) * 128], natin[:, blk * 4 + q4, :], 128)
                        evac_copy(hts[:, blk * 512:(blk + 1) * 512], b[:, :])
                    Cm = avb(CMB[s_ % 2], 128, 4, 64)
                    TT("dve", Cm, xbc[:, 20:24, 0:64], bcm(cst[:, C_MSBC + s_ * 64:C_MSBC + (s_ + 1) * 64], 4), ALU.mult)
                    Bm = avb(BMB[s_ % 2], 64, 512)
                    TS("dve", Bm, Btok[:, :], cst[0:64, C_MSEL + s_:C_MSEL + s_ + 1], ALU.mult)

                def st_comp(s_):
                    natin = av(NATIN[s_ % 3], 128, 16, 128)
                    hts = avb(HTS[s_ % 2], 128, 2048)
                    sout = av(SOUTB[s_ % 2], 128, 16, 128)
                    Cm = avb(CMB[s_ % 2], 128, 4, 64)
                    Bm = avb(BMB[s_ % 2], 64, 512)
                    for g in range(4):
                        MM(YI[g][0:n, 0:512], Cm[:, g, :], hts[:, g * 512:(g + 1) * 512], s_ == 0, s_ == NS - 1)
                    for blk in range(4):
                        b = psum.get()
                        for q4 in range(4):
                            c = blk * 4 + q4
                            MM(b[:, q4 * 128:(q4 + 1) * 128], Xtl[:, c * 128:(c + 1) * 128], Bm[:, blk * 128:(blk + 1) * 128],
                               q4 == 0, q4 == 3)
                        so = sout[:, blk * 4:(blk + 1) * 4, :]
                        TT(POOLX, so, natin[:, blk * 4:(blk + 1) * 4, :],
                           dcolS[:, blk * 4:(blk + 1) * 4, s_:s_ + 1].broadcast_to([128, 4, 128]), ALU.mult)
                        TT("dve", so, so, b[:, :].rearrange("p (a b) -> p a b", b=128), ALU.add)
                    DMA(ssm_s[j, s_].rearrange("(c q) n -> q c n", q=128), sout)
                st_load(0)
                st_load(1)
                st_load(2)
                st_tr(0)
                for s_ in range(NS):
                    if s_ + 1 < NS:
                        st_tr(s_ + 1)
                    st_comp(s_)
                    if s_ + 3 < NS:
                        st_load(s_ + 3)
            P.tag = (t.kind, t.ti, layer, "D%d" % ci)
            for g in range(4):
                ywg = yw[:, g * 512:(g + 1) * 512]
                TT("dve", ywg.rearrange("p (h d) -> p h d", d=64), YI[g][0:n, 0:512].rearrange("p (h d) -> p h d", d=64),
                   bcl(Ee[:, g * 8:(g + 1) * 8], 64), ALU.mult)
            for g in range(4):
                ywg = yw[:, g * 512:(g + 1) * 512]
                TT("dve", ywg, ywg, yacc[:, g * 512:(g + 1) * 512], ALU.add)
            psum.release(yid)
            TT("dve", yw, yw, ztk, ALU.mult)
            ss = av(o_ss + pso, n, 4)
            rs = av(o_rs + pso, n, 4)
            for g in range(4):
                ACT(yacc[:, g * 512:(g + 1) * 512], yw[:, g * 512:(g + 1) * 512], AF.Square, accum=ss[:, g:g + 1])
            ACT(rs, ss, AF.Ln, bias=EPS, scale=1.0 / 512)
            ACT(rs, rs, AF.Exp, scale=-0.5)
            for g in range(4):
                ACT(yw[:, g * 512:(g + 1) * 512], yw[:, g * 512:(g + 1) * 512], AF.Identity, scale=rs[:, g:g + 1])
            for blk in range(4):
                b = psum.get()
                for q4 in range(4):
                    c = blk * 4 + q4
                    TR(b[:, q4 * n:(q4 + 1) * n], yw[:, c * 128:(c + 1) * 128], n)
                TT("dve", yT[:, blk * 4:(blk + 1) * 4, o:o + n], b[:, 0:4 * n].rearrange("p (a b) -> p a b", b=n),
                   bcl(pcols[:, PC_SNW + j * 16 + blk * 4:PC_SNW + j * 16 + (blk + 1) * 4], n), ALU.mult)
        P.tag = (t.kind, t.ti, layer, "out")
        for ob in range(4):
            wv = WS.next(wblk(w_sout, j, ob * 256, 256))
            for oi in range(2):
                oc = ob * 2 + oi
                b = psum.get()
                for kc in range(16):
                    MM(b[:, 0:Q], wv[:, kc, oi * 128:(oi + 1) * 128], yT[:, kc, 0:Q], kc == 0, kc == 15)
                TT("dve", hT[:, oc, 0:Q], hT[:, oc, 0:Q], b[:, 0:Q], ALU.add)
        if (not sp_) and t.last:
            natout = av(0, 128, 16, 128)
            for blk in range(4):
                b = psum.get()
                for q4 in range(4):
                    c = blk * 4 + q4
                    TR(b[:, q4 * 128:(q4 + 1) * 128], hst[j][:, c * 128:(c + 1) * 128], 128)
                evac_copy(natout[:, blk * 4:(blk + 1) * 4, :], b[:, :].rearrange("p (a b) -> p a b", b=128))
            DMA(ssm_p[j].rearrange("(c q) n -> q c n", q=128), natout)

    def hgrn_layer(t, layer):
        j = layer // 2
        Q, C = t.Q, t.C
        sp_ = (t.kind == "s")
        B = 8 * Q
        NC = Q // C
        o_q, o_f, o_k, o_c, o_e1, o_e2, o_g, o_o = 0, B, 2 * B, 3 * B, 4 * B, 5 * B, 6 * B, 7 * B
        o_v = 8 * B
        o_kta = o_v + 2048
        o_ktc = o_kta + 1024
        o_att = o_ktc + 1024
        o_d = o_att + 1024
        o_sin = o_d + 256
        o_sout = o_sin + 1024
        assert o_sout + (4096 if sp_ else 1024) <= ARENA

        def buf(o_):
            return av(o_, 128, 8, Q)

        def flat(o_):
            return av(o_, 128, B)
        Qb, Fb, Kb, Cb, E1, E2, Gb, Ob = [buf(x) for x in (o_q, o_f, o_k, o_c, o_e1, o_e2, o_g, o_o)]
        oTf = yTb[:, 0:8, 0:Q]
        lbc = misc[:, 64 + j * 8:64 + (j + 1) * 8]
        omlc = misc[:, 80 + j * 8:80 + (j + 1) * 8]

        P.tag = (t.kind, t.ti, layer, 'hg_in')
        rmsnorm(t, PC_MIX + layer * 8)

        def fm_block(c0, fn):
            for blk in range(2):
                wv = WS.next(wblk(w_hin, j, c0 + blk * 512, 512))
                for oi in range(4):
                    h = blk * 4 + oi
                    b = psum.get()
                    for kc in range(8):
                        MM(b[:, 0:Q], wv[:, kc, oi * 128:(oi + 1) * 128], uT[:, kc, 0:Q], kc == 0, kc == 7)
                    fn(h, b[:, 0:Q])
        fm_block(1024, lambda h, b: CP("act", Fb[:, h, :], b))
        fm_block(0, lambda h, b: ACT(Qb[:, h, :], b, AF.Silu))
        for blk in range(2):
            wv = WS.next(wblk(w_hin, j, 2048 + blk * 512, 512))
            for ci, (o, n) in enumerate(t.chunks):
                b = psum.get()
                for kc in range(8):
                    MM(b[0:n, 0:512], uT[:, kc, o:o + n], wv[:, kc, :], kc == 0, kc == 7)
                evac_copy(avb(o_v + ci * 1024, n, 1024)[:, blk * 512:(blk + 1) * 512], b[0:n, 0:512])
        fm_block(3072, lambda h, b: ACT(Gb[:, h, :], b, AF.Silu))

        P.tag = (t.kind, t.ti, layer, 'hg_chain')
        rm = cst[:, C_RM4:C_RM4 + 64] if sp_ else cst[:, C_RM32:C_RM32 + 256]
        HB = 4 * Q
        NCh = 4 * NC
        dcy = av(o_d, 128, 8 * NC)

        def hf2(o_, hf):
            return av(o_ + hf * HB, 128, HB)

        def hf3(o_, hf):
            return av(o_ + hf * HB, 128, 4, Q)

        def hfc(o_, hf):
            return av(o_ + hf * HB, 128, NCh, C)
        steps = [
            lambda hf: ACT(hf2(o_k, hf), hf2(o_f, hf), AF.Sigmoid, scale=-1.0),
            lambda hf: ACT(hf2(o_f, hf), hf2(o_f, hf), AF.Sigmoid),
            lambda hf: [ACT(Fb[:, hf * 4 + h4, :], Fb[:, hf * 4 + h4, :], AF.Ln, scale=omlc[:, hf * 4 + h4:hf * 4 + h4 + 1],
                            bias=lbc[:, hf * 4 + h4:hf * 4 + h4 + 1]) for h4 in range(4)],
            lambda hf: TT("dve", hf3(o_k, hf), hf3(o_k, hf), bcl(omlc[:, hf * 4:(hf + 1) * 4], Q), ALU.mult),
            lambda hf: [SCAN(Cb[:, hf * 4 + h4, :], rm, Fb[:, hf * 4 + h4, :]) for h4 in range(4)],
            lambda hf: ACT(hf2(o_e1, hf), hf2(o_c, hf), AF.Exp),
            lambda hf: TT("dve", hf2(o_q, hf), hf2(o_q, hf), hf2(o_e1, hf), ALU.mult),
            lambda hf: ACT(hf2(o_e1, hf), hf2(o_c, hf), AF.Exp, scale=-1.0),
            lambda hf: TT("dve", hf2(o_e1, hf), hf2(o_e1, hf), hf2(o_k, hf), ALU.mult),
            lambda hf: TT("dve", hfc(o_e2, hf), hfc(o_c, hf)[:, :, C - 1:C].broadcast_to([128, NCh, C]), hfc(o_c, hf), ALU.subtract),
            lambda hf: ACT(hf2(o_e2, hf), hf2(o_e2, hf), AF.Exp),
            lambda hf: TT("dve", hf2(o_e2, hf), hf2(o_e2, hf), hf2(o_k, hf), ALU.mult),
            lambda hf: ACT(dcy[:, hf * NCh:(hf + 1) * NCh].unsqueeze(2), hfc(o_c, hf)[:, :, C - 1:C], AF.Exp),
        ]
        for st in steps:
            for hf in range(2):
                st(hf)

        if sp_:
            agroups = [(0, 64)]
            BD = cst[0:64, C_TRIS:C_TRIS + 64]
        else:
            agroups = [(0, 128), (128, 128)]
            BD = cst[:, C_BD32:C_BD32 + 128]
        for ai, (ao, an) in enumerate(agroups):
            P.tag = (t.kind, t.ti, layer, 'hg_att%d' % ai)
            NCA = an // C
            hpb = 512 // an
            nb = 8 // hpb
            gp = 0 if sp_ else ai % 2
            attm = avb([o_att, 24320][gp], an, 8 * an)
            ktall = av([o_kta, 25344][gp], an, 1024)
            Vt = avb(o_v + ai * 1024, an, 1024)
            for bi in range(nb):
                b = psum.get()
                for hh in range(hpb):
                    h = bi * hpb + hh
                    MM(b[0:an, hh * an:(hh + 1) * an], E1[:, h, ao:ao + an], Qb[:, h, ao:ao + an], hh == 0, hh == hpb - 1)
                TT("dve", attm[:, bi * 512:(bi + 1) * 512].rearrange("p (a b) -> p a b", b=an),
                   b[0:an, :].rearrange("p (a b) -> p a b", b=an), bcm(BD, hpb), ALU.mult)
            for bi in range(2):
                b = psum.get()
                for hh in range(4):
                    TR(b[0:an, hh * 128:(hh + 1) * 128], E2[:, bi * 4 + hh, ao:ao + an], 128)
                evac_copy(ktall[:, bi * 512:(bi + 1) * 512], b[0:an, :])
            OB, obid = psum.reserve(nb)
            for h in range(8):
                bi, hh = h // hpb, h % hpb
                MM(OB[bi][:, hh * an:(hh + 1) * an], Vt[:, h * 128:(h + 1) * 128], attm[:, h * an:(h + 1) * an],
                   hh == 0, False)
            if sp_:
                KTC = [o_ktc, o_sout + 1024]
                SINB = [o_sin, o_sout + 2048]
                SOB = [o_sout, o_sout + 3072]
            else:
                KTC = [[o_f, o_f + 1024, o_k, o_k + 1024], [o_c, o_c + 1024, o_ktc, 26368]][gp]

            def mk_ktc(c):
                ktc = avb(KTC[c % len(KTC)], an, 1024)
                msk = cst[0:64, C_MSEL + c:C_MSEL + c + 1] if sp_ else cst[:, C_MB32 + c:C_MB32 + c + 1]
                TS(POOLX, ktc, ktall, msk, ALU.mult)
            if sp_:
                DMA(av(SINB[0], 128, 8, 128), st_hgrn[j, 0].rearrange("h k v -> k h v"))
                mk_ktc(0)
            else:
                for c in range(NCA):
                    mk_ktc(c)
            for c in range(NCA):
                cg = ao // C + c
                tok0 = ao + c * C
                if sp_:
                    Sst = av(SINB[c % 2], 128, 8, 128)
                    Sds = av(SOB[c % 2], 128, 8, 128)
                    if c + 1 < NCA:
                        DMA(av(SINB[(c + 1) % 2], 128, 8, 128), st_hgrn[j, c + 1].rearrange("h k v -> k h v"))
                        mk_ktc(c + 1)
                else:
                    Sst = shg[j]
                    Sds = shg[j]
                for h in range(8):
                    bi, hh = h // hpb, h % hpb
                    MM(OB[bi][:, hh * an + c * C:hh * an + (c + 1) * C], Sst[:, h, :], Qb[:, h, tok0:tok0 + C],
                       False, (c == NCA - 1 and hh == hpb - 1))
                ktc = avb(KTC[c % len(KTC)], an, 1024)
                ubs = []
                for bi2 in range(2):
                    b = psum.get()
                    for hh in range(4):
                        h = bi2 * 4 + hh
                        MM(b[:, hh * 128:(hh + 1) * 128], ktc[:, h * 128:(h + 1) * 128], Vt[:, h * 128:(h + 1) * 128],
                           hh == 0, hh == 3)
                    ubs.append(b)
                for bi2 in range(2):
                    for hh in range(4):
                        h = bi2 * 4 + hh
                        STT(Sds[:, h, :], Sst[:, h, :], dcy[:, h * NC + cg:h * NC + cg + 1], ubs[bi2][:, hh * 128:(hh + 1) * 128],
                            ALU.mult, ALU.add)
                if sp_:
                    DMA(hgrn_s[j, c].rearrange("h k v -> k h v"), Sds)
            for bi in range(nb):
                CP("act", Ob[:, bi * hpb:(bi + 1) * hpb, ao:ao + an], OB[bi][:, :].rearrange("p (a b) -> p a b", b=an))
            psum.release(obid)

        P.tag = (t.kind, t.ti, layer, 'hg_out')
        for ai, (ao, an) in enumerate(agroups):
            tsl = slice(ao, ao + an)
            ACT(Fb[:, :, tsl], Ob[:, :, tsl], AF.Square)
            hb = 512 // an
            for bi in range(8 // hb):
                b = psum.get()
                for hh in range(hb):
                    h = bi * hb + hh
                    MM(b[:, hh * an:(hh + 1) * an], ones[:, :], Fb[:, h, tsl], hh == 0, hh == hb - 1)
                ACT(E1[:, bi * hb:(bi + 1) * hb, tsl], b[:, 0:hb * an].rearrange("p (a b) -> p a b", b=an), AF.Ln,
                    bias=EPS, scale=1.0 / 128)
            ACT(E1[:, :, tsl], E1[:, :, tsl], AF.Exp, scale=-0.5)
            TT("dve", Ob[:, :, tsl], Ob[:, :, tsl], E1[:, :, tsl], ALU.mult)
            TT("dve", Ob[:, :, tsl], Ob[:, :, tsl], Gb[:, :, tsl], ALU.mult)
            TT("dve", oTf[:, :, tsl], Ob[:, :, tsl], bcl(pcols[:, PC_HNW + j * 8:PC_HNW + (j + 1) * 8], an), ALU.mult)
        for ob in range(2):
            wv = WS.next(wblk(w_hout, j, ob * 512, 512))
            for oi in range(4):
                oc = ob * 4 + oi
                b = psum.get()
                for kc in range(8):
                    MM(b[:, 0:Q], wv[:, kc, oi * 128:(oi + 1) * 128], oTf[:, kc, :], kc == 0, kc == 7)
                TT("dve", hT[:, oc, 0:Q], hT[:, oc, 0:Q], b[:, 0:Q], ALU.add)
        if (not sp_) and t.last:
            DMA(hgrn_p[j].rearrange("h k v -> k h v"), shg[j][:, :, :])

    def mlp_layer(t, layer):
        Q = t.Q
        P.tag = (t.kind, t.ti, layer, "mlp")
        hid = arena[:, 0:4096].bitcast(BF16).rearrange("p (a b) -> p a b", b=256)[:, :, 0:Q]
        rmsnorm(t, PC_MLP + layer * 8)
        pend = [None]
        for ub in range(8):
            wv = WS.next(wblk(w_up, layer, ub * 512, 512))
            for oi in range(4):
                oc = ub * 4 + oi
                b = psum.get()
                for kc in range(8):
                    MM(b[:, 0:Q], wv[:, kc, oi * 128:(oi + 1) * 128], uT[:, kc, 0:Q], kc == 0, kc == 7)
                rt = av(4096 + (oc % 4) * 256, 128, Q)
                ACT(rt, b[:, 0:Q], AF.Relu)
                if pend[0] is not None:
                    pend[0]()
                pend[0] = (lambda oc=oc, rt=rt: TT("dve", hid[:, oc, :], rt, rt, ALU.mult))
        pend[0]()
        for db in range(8):
            wv = WS.next(wblk(w_dn, layer, db * 128, 128))
            b = psum.get()
            for kc in range(32):
                MM(b[:, 0:Q], wv[:, kc, :], hid[:, kc, :], kc == 0, kc == 31)
            TT("dve", hT[:, db, 0:Q], hT[:, db, 0:Q], b[:, 0:Q], ALU.add)

    def tile_prog(t):
        Q = t.Q
        src = xs if t.kind == "s" else xp
        dst = y_s if t.kind == "s" else y_p
        row0 = 0 if t.kind == "s" else t.ti * 256
        for ci, (o, n) in enumerate(t.chunks):
            xt = av(ci * 1024, n, 1024)
            DMA(xt, src[row0 + o:row0 + o + n, :])
            for bi in range(2):
                b = psum.get()
                for q4 in range(4):
                    kc = bi * 4 + q4
                    TR(b[:, q4 * n:(q4 + 1) * n], xt[:, kc * 128:(kc + 1) * 128], n)
                evac_copy(hT[:, bi * 4:(bi + 1) * 4, o:o + n], b[:, 0:4 * n].rearrange("p (a b) -> p a b", b=n))
        sub = 0
        for layer in range(4):
            if sub < NSUB:
                if layer % 2 == 0:
                    ssd_layer(t, layer)
                else:
                    hgrn_layer(t, layer)
            sub += 1
            if sub < NSUB:
                mlp_layer(t, layer)
            sub += 1
        if FINAL_NORM:
            fin = av(4096, 128, 8, Q)
            rmsnorm(t, PC_FIN, fin)
        else:
            fin = hT
        for ci, (o, n) in enumerate(t.chunks):
            yt = av(ci * 1024, n, 1024)
            for bi in range(2):
                b = psum.get()
                for q4 in range(4):
                    kc = bi * 4 + q4
                    TR(b[0:n, q4 * 128:(q4 + 1) * 128], fin[:, kc, o:o + n], 128)
                evac_copy(yt[:, bi * 512:(bi + 1) * 512], b[0:n, :])
            DMA(dst[row0 + o:row0 + o + n, :], yt)

    def whole():
        setup()
        if DO_SAMPLE and SAMPLE_FIRST:
            tile_prog(mk_tile("s", 0))
        for ti in range(NPT):
            tile_prog(mk_tile("p", ti))
        if DO_SAMPLE and not SAMPLE_FIRST:
            tile_prog(mk_tile("s", 0))

    P.plan = True
    whole()
    P.plan = False
    psum.__init__()
    evac_rr[0] = 0
    whole()
    assert WS.cur == len(WS.specs)
    P.run()
    return P


def make_in_maps(inp):
    f = lambda a: np.ascontiguousarray(np.asarray(a, np.float32))
    pc = _pack_cols(inp)
    pr = _pack_rows(inp)
    shared = {k: f(inp[k]) for k in ("ssd_w_in", "ssd_w_out", "hgrn_w_in", "hgrn_w_out", "mlp_w_up", "mlp_w_down")}
    maps = []
    for c in range(8):
        sl = slice(16 * c, 16 * (c + 1))
        m = dict(shared)
        m["xp"] = f(inp["x_prompt"][c])
        m["xs"] = f(np.asarray(inp["x_sample"])[sl].reshape(64, D))
        m["st_conv"] = f(np.asarray(inp["state_ssd_conv"])[:, sl].reshape(2, 48, 3072))
        m["st_ssm"] = f(np.asarray(inp["state_ssd_ssm"])[:, sl].reshape(2, 16, 2048, 128))
        m["st_hgrn"] = f(np.asarray(inp["state_hgrn"])[:, sl])
        m["pcols"] = pc
        m["prows"] = pr
        maps.append(m)
    return maps


def assemble(results):
    r = results
    y_p = np.stack([r[c]["y_p"] for c in range(8)], 0)
    y_s = np.concatenate([r[c]["y_s"].reshape(16, 4, D) for c in range(8)], 0)
    conv_p = np.stack([r[c]["conv_p"] for c in range(8)], 1)
    ssm_p = np.stack([r[c]["ssm_p"].reshape(2, 32, 64, 128) for c in range(8)], 1)
    hgrn_p = np.stack([r[c]["hgrn_p"] for c in range(8)], 1)
    conv_s = np.concatenate([r[c]["conv_s"].reshape(2, 16, 3, 3072) for c in range(8)], 1)
    ssm_s = np.concatenate([r[c]["ssm_s"].reshape(2, 16, 32, 64, 128) for c in range(8)], 1)
    hgrn_s = np.concatenate([r[c]["hgrn_s"] for c in range(8)], 1)
    return tuple(np.ascontiguousarray(a, dtype=np.float32)
                 for a in (y_p, y_s, conv_p, ssm_p, hgrn_p, conv_s, ssm_s, hgrn_s))


def kernel(**inputs):
    nc = bass.Bass("TRN2", target_bir_lowering=False)
    build(nc, {})
    in_maps = make_in_maps(inputs)
    res = run_bass_kernel_spmd(nc, in_maps, core_ids=list(range(8)))
    return assemble(res.results)
```

```python
import numpy as np
import concourse.bass as bass
import concourse.mybir as mybir
from concourse.bass_utils import run_bass_kernel_spmd

F32 = mybir.dt.float32
BF16 = mybir.dt.bfloat16
AF = mybir.ActivationFunctionType
ALU = mybir.AluOpType

D = 1024
SEQ = 2048
NSEQ_S = 16
LS = 4
EPS = 1e-5
ENGS = ("sp", "pe", "act", "dve", "pool")
PAGE = 64
EPOCH = 12000
WSLOT = 4096
NBUF = 5
ARENA = 27392


class Inst:
    __slots__ = ("eng", "fn", "waits", "sem", "val", "idx", "needs_inc", "is_dma", "deps", "succ", "ndeps",
                 "est", "occ", "lat", "grp", "gidx", "fin", "tag", "st", "crit")

    def __init__(self, eng, fn, is_dma):
        self.eng = eng
        self.fn = fn
        self.waits = []
        self.sem = None
        self.val = None
        self.idx = -1
        self.needs_inc = False
        self.is_dma = is_dma
        self.deps = ()
        self.succ = []
        self.ndeps = 0
        self.est = 0.0
        self.occ = 0.1
        self.lat = 0.1
        self.grp = None
        self.gidx = 0
        self.fin = 0.0
        self.tag = None
        self.st = 0.0
        self.crit = None


_ESZ = {}


def _esize(dt):
    k = str(dt)
    v = _ESZ.get(k)
    if v is None:
        v = 2 if ("bfloat16" in k or "float16" in k) else 4
        _ESZ[k] = v
    return v


def ap_pages(ap):
    space = str(ap.space).upper()
    if "DRAM" in space or "HBM" in space:
        return ()
    name = ap.tensor.name
    if "PSUM" in space:
        return ((name, 0),)
    es = _esize(ap.dtype)
    apl = ap.ap
    row = apl[0][0]
    foff = (ap.offset % row if row > 0 else ap.offset) * es
    dims = [(abs(s) * es, n) for (s, n) in apl[1:] if n > 1 and s != 0]
    PB = 256
    if not dims:
        return ((name, foff // PB),)
    dims.sort()
    s0, n0 = dims[0]
    inner = (n0 - 1) * s0 + es
    outer = dims[1:]
    nouter = 1
    for _, n in outer:
        nouter *= n
    out = set()
    if nouter <= 512:
        starts = [foff]
        for s_, n in outer:
            starts = [b_ + i * s_ for b_ in starts for i in range(n)]
        for b_ in starts:
            for pg in range(b_ // PB, (b_ + inner - 1) // PB + 1):
                out.add((name, pg))
    else:
        ext = inner + sum((n - 1) * s_ for s_, n in outer)
        for pg in range(foff // PB, (foff + ext - 1) // PB + 1):
            out.add((name, pg))
    return tuple(out)


class Prog:
    def __init__(self, nc, n_dma_sems=24, n_epochs=14):
        self.nc = nc
        self.plan = False
        self.sched = True
        self.all = []
        self.q = {e: [] for e in ENGS}
        self.lastw = {}
        self.readers = {}
        self.dma_sems = {"sp": [nc.alloc_semaphore(f"dq{i}") for i in range(n_dma_sems)],
                         "pool": [nc.alloc_semaphore(f"dg{i}") for i in range(8)]}
        self.dma_n = {"sp": 0, "pool": 0}
        self.dma_last = {"sp": [None] * n_dma_sems, "pool": [None] * 8}
        self.eng_sems = {e: [nc.alloc_semaphore(f"s_{e}_{k}") for k in range(n_epochs)]
                         for e in ("pe", "act", "dve", "pool")}
        self.n_inst = 0
        self.tag = None
        self.prio = "order"
        self.prio_w = 0.0

    def emit(self, eng, fn, outs=(), ins=(), dma=False, occ=0.2, lat=None, grp=None, extra=()):
        if self.plan:
            return None
        inst = Inst(eng, fn, dma)
        inst.gidx = len(self.all)
        inst.tag = self.tag
        inst.occ = occ
        inst.lat = occ if lat is None else lat
        inst.grp = grp
        rp = set()
        for a in ins:
            rp.update(ap_pages(a))
        wp = set()
        for a in outs:
            wp.update(ap_pages(a))
        deps = set()
        for r in rp:
            w = self.lastw.get(r)
            if w is not None:
                deps.add(w)
        for w_ in wp:
            w = self.lastw.get(w_)
            if w is not None:
                deps.add(w)
            rd = self.readers.get(w_)
            if rd:
                deps.update(rd)
        if dma:
            pool_ = self.dma_sems[eng]
            k = self.dma_n[eng] % len(pool_)
            inst.sem = pool_[k]
            inst.val = 16 * (self.dma_n[eng] // len(pool_) + 1)
            prev = self.dma_last[eng][k]
            if prev is not None:
                deps.add(prev)
            self.dma_last[eng][k] = inst
            self.dma_n[eng] += 1
        for x_ in extra:
            if x_ is not None:
                deps.add(x_)
        deps.discard(inst)
        inst.deps = deps
        for r in rp:
            if r not in wp:
                self.readers.setdefault(r, []).append(inst)
        for w_ in wp:
            self.lastw[w_] = inst
            self.readers[w_] = []
        self.all.append(inst)
        self.n_inst += 1
        return inst

    def schedule(self):
        import heapq
        HOP = 0.3
        for inst in self.all:
            inst.ndeps = len(inst.deps)
            for d in inst.deps:
                d.succ.append(inst)
        if self.prio == "cp":
            rank = {}
            for inst in reversed(self.all):
                r = 0.0
                for s_ in inst.succ:
                    rs_ = rank[s_]
                    if rs_ > r:
                        r = rs_
                rank[inst] = r + (inst.lat if inst.is_dma else inst.occ)
            W_ = self.prio_w
            for inst in self.all:
                inst.gidx = inst.gidx - W_ * rank[inst]
        fut = {e: [] for e in ENGS}
        avail = {e: [] for e in ENGS}
        free = {e: 0.0 for e in ENGS}
        cur_grp = [None]
        for inst in self.all:
            if inst.ndeps == 0:
                heapq.heappush(fut[inst.eng], (0.0, inst.gidx, inst))
        nleft = len(self.all)
        while nleft:
            best = None
            for e in ENGS:
                fq, aq = fut[e], avail[e]
                while fq and fq[0][0] <= free[e]:
                    _, gi, it = heapq.heappop(fq)
                    heapq.heappush(aq, (gi, it))
                if aq:
                    cand = (free[e], aq[0][0], e, True)
                elif fq:
                    cand = (fq[0][0], fq[0][1], e, False)
                else:
                    continue
                if best is None or cand < best:
                    best = cand
            start, _, e, from_av = best
            if from_av:
                aq = avail[e]
                pick = None
                if e == "act" and cur_grp[0] is not None and len(aq) > 1:
                    small = heapq.nsmallest(6, aq)
                    for gi, it in small:
                        if it.grp is None or it.grp == cur_grp[0]:
                            pick = (gi, it)
                            break
                    if pick is not None and pick != aq[0]:
                        aq.remove(pick)
                        heapq.heapify(aq)
                    else:
                        pick = heapq.heappop(aq)
                else:
                    pick = heapq.heappop(aq)
                inst = pick[1]
            else:
                _, _, inst = heapq.heappop(fut[e])
            occ = inst.occ
            if e == "act" and inst.grp is not None:
                if cur_grp[0] is not None and cur_grp[0] != inst.grp:
                    occ += 1.3
                cur_grp[0] = inst.grp
            inst.st = start
            if self.q[e] and start <= free[e] + 1e-9 and free[e] > 0:
                inst.crit = self.q[e][-1]
            else:
                cd = None
                for d_ in inst.deps:
                    if cd is None or d_.fin > cd.fin:
                        cd = d_
                inst.crit = cd
            inst.fin = start + (inst.lat if inst.is_dma else occ)
            free[e] = start + occ
            inst.idx = len(self.q[e])
            self.q[e].append(inst)
            nleft -= 1
            for s_ in inst.succ:
                t_ = inst.fin + (HOP if s_.eng != e else 0.06)
                if t_ > s_.est:
                    s_.est = t_
                s_.ndeps -= 1
                if s_.ndeps == 0:
                    heapq.heappush(fut[s_.eng], (s_.est, s_.gidx, s_))
        self.makespan = max(free.values())

    def finalize(self):
        if self.sched:
            self.schedule()
        else:
            for inst in self.all:
                inst.idx = len(self.q[inst.eng])
                self.q[inst.eng].append(inst)
        for e in ENGS:
            waited = {}
            for inst in self.q[e]:
                best = {}
                for d in inst.deps:
                    if d.is_dma:
                        key = ("dma", d.sem.name)
                        if waited.get(key, 0) >= d.val:
                            continue
                        cur = best.get(key)
                        if cur is None or d.val > cur.val:
                            best[key] = d
                    else:
                        if d.eng == "pe" and e == "pe":
                            continue
                        if waited.get(d.eng, -1) >= d.idx:
                            continue
                        cur = best.get(d.eng)
                        if cur is None or d.idx > cur.idx:
                            best[d.eng] = d
                for key, d in best.items():
                    if d.is_dma:
                        waited[key] = d.val
                    else:
                        waited[d.eng] = d.idx
                        d.needs_inc = True
                    inst.waits.append(d)
        for e in ("pe", "act", "dve", "pool"):
            k = 0
            for inst in self.q[e]:
                if inst.needs_inc:
                    inst.sem = self.eng_sems[e][k // EPOCH]
                    inst.val = k % EPOCH + 1
                    k += 1
            assert k <= EPOCH * len(self.eng_sems[e]), (e, k)
        fin = Inst("sp", None, False)
        for lst in self.dma_last.values():
            for d in lst:
                if d is not None:
                    fin.waits.append(d)
        self.q["sp"].append(fin)

    def replay(self, eng_name, e):
        for inst in self.q[eng_name]:
            for d in inst.waits:
                e.wait_ge(d.sem, d.val)
            if inst.fn is None:
                continue
            bi = inst.fn(e)
            if inst.is_dma:
                bi.then_inc(inst.sem, 16)
            elif inst.needs_inc:
                bi.then_inc(inst.sem, 1)

    def run(self):
        self.finalize()
        with self.nc.Block() as block:
            @block.sync
            def _(e):
                self.replay("sp", e)

            @block.tensor
            def _(e):
                self.replay("pe", e)

            @block.scalar
            def _(e):
                self.replay("act", e)

            @block.vector
            def _(e):
                self.replay("dve", e)

            @block.gpsimd
            def _(e):
                self.replay("pool", e)


def _isap(x):
    return not isinstance(x, (int, float))


C_ID, C_TRI, C_LST, C_BD32, C_MB32, C_TRIS, C_LSTS, C_MSEL, C_MSBC, C_RM32, C_RM4, C_END = (
    0, 128, 256, 384, 512, 516, 580, 644, 660, 1684, 1940, 2004)


def _const_table():
    c = np.zeros((128, C_END), np.float32)
    k = np.arange(128)
    c[:, C_ID:C_ID + 128] = np.eye(128)
    c[:, C_TRI:C_TRI + 128] = (k[:, None] <= k[None, :])
    c[:, C_LST:C_LST + 128] = (k[:, None] > k[None, :])
    c[:, C_BD32:C_BD32 + 128] = (k[:, None] <= k[None, :]) & (k[:, None] // 32 == k[None, :] // 32)
    c[:, C_MB32:C_MB32 + 4] = (k[:, None] // 32 == np.arange(4)[None, :])
    k6 = np.arange(64)
    same = (k6[:, None] // 4 == k6[None, :] // 4)
    c[:64, C_TRIS:C_TRIS + 64] = (k6[:, None] <= k6[None, :]) & same
    c[:64, C_LSTS:C_LSTS + 64] = (k6[:, None] > k6[None, :]) & same
    c[:64, C_MSEL:C_MSEL + 16] = (k6[:, None] // 4 == np.arange(16)[None, :])
    ms = (np.arange(16)[:, None] == (k6[None, :] // 4)).astype(np.float32)
    c[:, C_MSBC:C_MSBC + 1024] = ms.reshape(1, 1024)
    t = np.arange(256)
    c[:, C_RM32:C_RM32 + 256] = (t % 32 != 0)[None, :]
    c[:, C_RM4:C_RM4 + 64] = (np.arange(64) % 4 != 0)[None, :]
    return c


PC_MIX, PC_MLP, PC_FIN, PC_CW, PC_CB, PC_SNW, PC_HNW, PC_LBR, PC_END = 0, 32, 64, 72, 264, 312, 344, 360, 376


def _pack_cols(inp):
    def cols(v):
        v = np.asarray(v, np.float32)
        sh = v.shape[:-1]
        n = v.shape[-1] // 128
        return np.moveaxis(v.reshape(sh + (n, 128)), -1, 0)
    pc = np.zeros((128, PC_END), np.float32)
    pc[:, PC_MIX:PC_MIX + 32] = cols(inp["norm_mix_w"]).reshape(128, 32)
    pc[:, PC_MLP:PC_MLP + 32] = cols(inp["norm_mlp_w"]).reshape(128, 32)
    pc[:, PC_FIN:PC_FIN + 8] = cols(inp["norm_f_w"]).reshape(128, 8)
    cw = cols(inp["ssd_conv_w"])
    pc[:, PC_CW:PC_CW + 192] = np.transpose(cw, (0, 1, 3, 2)).reshape(128, 192)
    pc[:, PC_CB:PC_CB + 48] = cols(inp["ssd_conv_b"]).reshape(128, 48)
    pc[:, PC_SNW:PC_SNW + 32] = cols(inp["ssd_norm_w"]).reshape(128, 32)
    pc[:, PC_HNW:PC_HNW + 16] = cols(inp["hgrn_norm_w"]).reshape(128, 16)
    pc[:, PC_LBR:PC_LBR + 16] = cols(inp["hgrn_lb_raw"]).reshape(128, 16)
    return pc


def _pack_rows(inp):
    pr = np.zeros((128, 192), np.float32)
    pr[:, 0:64] = np.asarray(inp["ssd_dt_bias"], np.float32).reshape(1, 64)
    pr[:, 64:128] = np.asarray(inp["ssd_a_log"], np.float32).reshape(1, 64)
    pr[:, 128:192] = np.asarray(inp["ssd_d"], np.float32).reshape(1, 64)
    return pr


def build(nc, cfg):
    NPT = cfg.get("np_tiles", 8)
    NSUB = cfg.get("nsub", 8)
    DO_SAMPLE = cfg.get("do_sample", True)
    FINAL_NORM = cfg.get("final_norm", True)
    P = Prog(nc)
    P.sched = cfg.get("sched", True)
    P.prio = cfg.get("prio", "cp")
    P.prio_w = cfg.get("prio_w", 10.0)
    USE_WCACHE = cfg.get("wcache", True)
    POOLX = cfg.get("poolx", "dve")
    SAMPLE_FIRST = cfg.get("sample_first", False)

    def din(name, shape):
        return nc.dram_tensor(name, list(shape), F32, kind="ExternalInput")

    def dout(name, shape):
        return nc.dram_tensor(name, list(shape), F32, kind="ExternalOutput")

    xp = din("xp", [SEQ, D])
    xs = din("xs", [64, D])
    st_conv = din("st_conv", [2, 48, 3072])
    st_ssm = din("st_ssm", [2, 16, 2048, 128])
    st_hgrn = din("st_hgrn", [2, 16, 8, 128, 128])
    w_sin = din("ssd_w_in", [2, 1024, 5152])
    w_sout = din("ssd_w_out", [2, 2048, 1024])
    w_hin = din("hgrn_w_in", [2, 1024, 4096])
    w_hout = din("hgrn_w_out", [2, 1024, 1024])
    w_up = din("mlp_w_up", [4, 1024, 4096])
    w_dn = din("mlp_w_down", [4, 4096, 1024])
    pcols_d = din("pcols", [128, PC_END])
    prows_d = din("prows", [128, 192])
    y_p = dout("y_p", [SEQ, D])
    y_s = dout("y_s", [64, D])
    conv_p = dout("conv_p", [2, 3, 3072])
    ssm_p = dout("ssm_p", [2, 2048, 128])
    hgrn_p = dout("hgrn_p", [2, 8, 128, 128])
    conv_s = dout("conv_s", [2, 48, 3072])
    ssm_s = dout("ssm_s", [2, 16, 2048, 128])
    hgrn_s = dout("hgrn_s", [2, 16, 8, 128, 128])
    cst_d = nc.inline_tensor(_const_table(), "cst_tab")

    sb = nc.alloc_sbuf_tensor
    cst = sb("cst", [128, C_END], F32)
    pcols = sb("pcols_sb", [128, PC_END], F32)
    prows = sb("prows_sb", [128, 192], F32)
    ones = sb("ones", [128, 128], F32)
    misc = sb("misc", [128, 512], F32)
    rstd = sb("rstd", [128, 256], F32)
    hT = sb("hT", [128, 8, 256], F32)
    uT = sb("uT", [128, 8, 256], BF16)
    yTb = sb("yTb", [128, 16, 256], BF16)
    ones_b = sb("ones_b", [128, 128], BF16)
    hst = [sb(f"hst{j}", [128, 2048], F32) for j in range(2)]
    shg = [sb(f"shg{j}", [128, 8, 128], F32) for j in range(2)]
    wring = [sb(f"wring{i}", [128, WSLOT], BF16) for i in range(NBUF)]
    arena = sb("arena", [128, ARENA], F32)
    PS = [nc.alloc_psum_tensor(f"ps{i}", [128, 512], F32) for i in range(8)]

    ident = cst[:, C_ID:C_ID + 128]

    def fsz(ap):
        n = 1
        for d_ in ap.shape[1:]:
            n *= d_
        return n

    GRP = {str(AF.Exp): "E", str(AF.Ln): "E", str(AF.Sigmoid): "S", str(AF.Sqrt): "Q", str(AF.Silu): "U"}

    def MM(out, lhsT, rhs, start, stop):
        passes = 4 if _esize(rhs.dtype) == 4 else 1
        P.emit("pe", lambda e: e.matmul(out, lhsT, rhs, start=start, stop=stop, skip_group_check=True),
               [out], [lhsT, rhs], occ=0.015 + fsz(rhs) * passes / 2000.0, lat=0.2 + fsz(rhs) * passes / 2000.0)

    def TR(out, in_, k):
        idn = cst[0:k, C_ID:C_ID + k]
        P.emit("pe", lambda e: e.transpose(out, in_, idn), [out], [in_, idn], occ=0.12, lat=0.3)

    def ACT(out, in_, func, bias=None, scale=None, accum=None):
        kw = {}
        ins = [in_]
        outs = [out]
        if bias is not None:
            kw["bias"] = bias
            if _isap(bias):
                ins.append(bias)
        if scale is not None:
            kw["scale"] = scale
            if _isap(scale):
                ins.append(scale)
        if accum is not None:
            kw["accum_out"] = accum
            outs.append(accum)
        c = 0.25 + fsz(in_) / 1100.0 + (0.1 if accum is not None else 0.0)
        P.emit("act", lambda e: e.activation(out, in_, func, **kw), outs, ins, occ=c, lat=c + 0.1, grp=GRP.get(str(func)))

    def vcost(eng, n):
        return (0.2 + n / 900.0) if eng == "dve" else (0.4 + n / 300.0)

    def TT(eng, out, a, b, op):
        c = vcost(eng, fsz(out))
        P.emit(eng, lambda e: e.tensor_tensor(out, a, b, op), [out], [a, b], occ=c, lat=c + 0.1)

    def TS(eng, out, a, s1, op0, s2=None, op1=None):
        ins = [a] + ([s1] if _isap(s1) else []) + ([s2] if (s2 is not None and _isap(s2)) else [])
        c = vcost(eng, fsz(out))
        if op1 is None:
            P.emit(eng, lambda e: e.tensor_scalar(out, a, s1, None, op0), [out], ins, occ=c, lat=c + 0.1)
        else:
            P.emit(eng, lambda e: e.tensor_scalar(out, a, s1, s2, op0, op1), [out], ins, occ=c, lat=c + 0.1)

    def STT(out, in0, scalar, in1, op0, op1):
        ins = [in0, in1] + ([scalar] if _isap(scalar) else [])
        c = 0.3 + fsz(out) / 900.0
        P.emit("dve", lambda e: e.scalar_tensor_tensor(out, in0, scalar, in1, op0, op1), [out], ins, occ=c, lat=c + 0.1)

    def CP(eng, out, in_):
        if eng == "act":
            c = 0.25 + fsz(in_) / 1100.0
            P.emit("act", lambda e: e.activation(out, in_, AF.Copy), [out], [in_], occ=c, lat=c + 0.1)
        else:
            c = vcost(eng, fsz(out))
            P.emit(eng, lambda e: e.tensor_copy(out, in_), [out], [in_], occ=c, lat=c + 0.1)

    def MEMSET(eng, out, v):
        c = vcost(eng, fsz(out))
        P.emit(eng, lambda e: e.memset(out, v), [out], [], occ=c, lat=c + 0.1)

    def RECIP(out, in_):
        c = 0.2 + fsz(out) / 900.0
        P.emit("dve", lambda e: e.reciprocal(out, in_), [out], [in_], occ=c, lat=c + 0.1)

    def SCAN(out, d0, d1):
        c = 0.2 + 2.0 * fsz(out) / 900.0
        P.emit("dve", lambda e: e.tensor_tensor_scan(out, d0, d1, 0.0, ALU.mult, ALU.add), [out], [d0, d1], occ=c, lat=c + 0.1)

    def dbytes(ap):
        n = 1
        for d_ in ap.shape:
            n *= d_
        return n * 4

    def DMA(out, in_, extra=()):
        return P.emit("sp", lambda e: e.dma_start(out=out, in_=in_), [out], [in_], dma=True, occ=0.15,
                      lat=2.2 + dbytes(in_) / 150e3, extra=extra)

    def WDMA(out, in_):
        return P.emit("pool", lambda e: e.dma_start(out=out, in_=in_), [out], [in_], dma=True, occ=1.0,
                      lat=3.0 + dbytes(in_) / 150e3)

    def av(off, np_, *shape):
        n = 1
        for s in shape:
            n *= s
        assert off + n <= ARENA, (off, shape)
        a = arena[0:np_, off:off + n]
        if len(shape) == 2:
            a = a.rearrange("p (a b) -> p a b", b=shape[1])
        elif len(shape) == 3:
            a = a.rearrange("p (a b c) -> p a b c", b=shape[1], c=shape[2])
        return a

    def avb(off, np_, *shape):
        n = 1
        for s_ in shape:
            n *= s_
        assert n % 2 == 0 and off + n // 2 <= ARENA, (off, shape)
        a = arena[0:np_, off:off + n // 2].bitcast(BF16)
        if len(shape) == 2:
            a = a.rearrange("p (a b) -> p a b", b=shape[1])
        elif len(shape) == 3:
            a = a.rearrange("p (a b c) -> p a b c", b=shape[1], c=shape[2])
        return a

    def bcl(ap2, n):
        sh = list(ap2.shape)
        return ap2.unsqueeze(len(sh)).broadcast_to(sh + [n])

    def bcm(ap2, n):
        sh = list(ap2.shape)
        return ap2.unsqueeze(1).broadcast_to([sh[0], n] + sh[1:])

    class PsumAlloc:
        def __init__(self):
            self.free = list(range(8))
            self.i = 0

        def get(self):
            self.i = (self.i + 1) % len(self.free)
            return PS[self.free[self.i]]

        def reserve(self, n):
            got = [self.free.pop() for _ in range(n)]
            return [PS[g] for g in got], got

        def release(self, ids):
            self.free.extend(ids)
            self.free.sort()

    psum = PsumAlloc()

    class WStream:
        def __init__(self):
            self.specs = []
            self.cur = 0
            self.issued = 0
            self.wb = {}
            self.cache = None

        def view(self, slot, shape):
            kc, nb = shape[1], shape[2]
            return wring[slot][:, 0:kc * nb].rearrange("p (a b) -> p a b", b=nb)

        def next(self, ap):
            if P.plan:
                self.specs.append(ap)
                return self.view(0, ap.shape)
            i = self.cur
            assert tuple(self.specs[i].shape) == tuple(ap.shape)
            npass = max(1, NPT + (1 if DO_SAMPLE else 0))
            nb_t = len(self.specs) // npass
            if self.cache is None and USE_WCACHE:
                self.cache = nc.dram_tensor("wcache", [nb_t, 128, WSLOT], BF16)
            while self.issued < min(i + NBUF, len(self.specs)):
                k = self.issued
                sp_ap = self.specs[k]
                n_ = sp_ap.shape[1] * sp_ap.shape[2]
                flat = wring[k % NBUF][:, 0:n_]
                cb = k % nb_t
                if not USE_WCACHE:
                    WDMA(self.view(k % NBUF, sp_ap.shape), sp_ap)
                elif k < nb_t:
                    WDMA(self.view(k % NBUF, sp_ap.shape), sp_ap)
                    self.wb[cb] = DMA(self.cache[cb][:, 0:n_], flat)
                else:
                    DMA(flat, self.cache[cb][:, 0:n_], extra=[self.wb[cb]])
                self.issued += 1
            self.cur += 1
            return self.view(i % NBUF, ap.shape)

    WS = WStream()

    def wblk(w, l, c0, nb):
        return w[l][:, c0:c0 + nb].rearrange("(kc p) n -> p kc n", p=128)

    evac_rr = [0]

    def evac_copy(out, in_):
        CP("act", out, in_)

    def setup():
        DMA(cst[:, :], cst_d.ap())
        DMA(pcols[:, :], pcols_d[:, :])
        DMA(prows[:, :], prows_d[:, :])
        MEMSET("pool", ones[:, :], 1.0)
        MEMSET("pool", ones_b[:, :], 1.0)
        ACT(misc[:, 0:64], prows[:, 64:128], AF.Exp)
        TS("dve", misc[:, 0:64], misc[:, 0:64], -1.0, ALU.mult)
        MEMSET("pool", misc[:, 64:72], 0.0)
        TT("dve", misc[:, 72:80], pcols[:, PC_LBR + 8:PC_LBR + 16], pcols[:, PC_LBR:PC_LBR + 8], ALU.subtract)
        ACT(misc[:, 72:80], misc[:, 72:80], AF.Sigmoid)
        TS("dve", misc[:, 80:96], misc[:, 64:80], -1.0, ALU.mult, 1.0, ALU.add)
        for j in range(2):
            MEMSET("pool", hst[j][:, :], 0.0)
            MEMSET("pool", shg[j][:, :, :], 0.0)

    a_bc = misc[:, 0:64]

    def hist(j):
        return misc[:, 96 + j * 72:96 + (j + 1) * 72].rearrange("p (a b) -> p a b", b=3)

    class T:
        pass

    def mk_tile(kind, ti):
        t = T()
        t.kind = kind
        t.ti = ti
        if kind == "p":
            t.Q = 256
            t.chunks = [(0, 128), (128, 128)]
            t.NS, t.L = 1, 256
            t.C = 32
            t.last = (ti == NPT - 1)
        else:
            t.Q = 64
            t.chunks = [(0, 64)]
            t.NS, t.L = 16, 4
            t.C = 4
            t.last = True
        return t

    def rmsnorm(t, wc0, dst=None):
        Q = t.Q
        if dst is None:
            dst = uT
        sq = yTb[:, 0:8, 0:Q]
        for kc in range(8):
            ACT(sq[:, kc, :], hT[:, kc, 0:Q], AF.Square)
        b = psum.get()
        for kc in range(8):
            MM(b[:, 0:Q], ones_b[:, :], sq[:, kc, :], kc == 0, kc == 7)
        ACT(rstd[:, 0:Q], b[:, 0:Q], AF.Ln, bias=EPS, scale=1.0 / D)
        ACT(rstd[:, 0:Q], rstd[:, 0:Q], AF.Exp, scale=-0.5)
        for kc in range(8):
            STT(dst[:, kc, 0:Q], hT[:, kc, 0:Q], pcols[:, wc0 + kc:wc0 + kc + 1], rstd[:, 0:Q], ALU.mult, ALU.mult)

    def ssd_layer(t, layer):
        j = layer // 2
        Q, NS, L = t.Q, t.NS, t.L
        sp_ = (t.kind == "s")
        if not sp_:
            o_xpre, o_xbc, o_z, o_xtok, o_yw = 0, 6216, 12360, 16456, 18504
            o_rb, o_wt, o_bt, o_cbt, o_sm = 20552, 21576, 22600, 23112, 23240
            o_xdt, o_xtl, o_yacc = 0, 2048, 4096
        else:
            o_xpre, o_xbc, o_z, o_xtok, o_yw = 0, 2688, 4224, 6272, 8320
            o_rb, o_wt, o_bt, o_cbt, o_sm = 10368, 10880, 11392, 11904, 11968
            o_xdt, o_xtl, o_yacc = 13312, 15360, 17408
            NATIN, HTS, SOUTB = [19456, 0, 24320], [21504, 13312], [6272, 8320]
            CMB, BMB = [23552, 2048], [23808, 10368]
        W_ = 3 + L
        xpre = av(o_xpre, 128, 24, NS, W_)
        xbc = av(o_xbc, 128, 24, Q)
        yT = yTb[:, :, 0:Q]
        o_dta, o_ee, o_ss, o_rs, o_dtt, o_db, o_r2 = o_sm, o_sm + 32, o_sm + 96, o_sm + 100, o_sm + 104, o_sm + 168, o_sm + 680
        PSTR = 0 if sp_ else 200
        if not sp_:
            o_db = o_sm + 104 + 64

        P.tag = (t.kind, t.ti, layer, "inproj")
        rmsnorm(t, PC_MIX + layer * 8)

        if sp_:
            stg = av(o_xdt, 48, 3072)
            DMA(stg, st_conv[j])
            for blk in range(6):
                b = psum.get()
                for q4 in range(4):
                    ch = blk * 4 + q4
                    TR(b[:, q4 * 48:(q4 + 1) * 48], stg[:, ch * 128:(ch + 1) * 128], 48)
                evac_copy(xpre[:, blk * 4:(blk + 1) * 4, :, 0:3],
                          b[:, 0:192].rearrange("p (a s k) -> p a s k", s=16, k=3))
        else:
            if t.ti == 0:
                MEMSET("pool", xpre[:, :, 0, 0:3], 0.0)
            else:
                CP("dve", xpre[:, :, 0, 0:3], hist(j))

        for xb in range(6):
            wv = WS.next(wblk(w_sin, j, 2048 + xb * 512, 512))
            for oi in range(4):
                ch = xb * 4 + oi
                b = psum.get()
                for kc in range(8):
                    MM(b[:, 0:Q], wv[:, kc, oi * 128:(oi + 1) * 128], uT[:, kc, 0:Q], kc == 0, kc == 7)
                evac_copy(xpre[:, ch, :, 3:3 + L], b[:, 0:Q].rearrange("p (s l) -> p s l", l=L))
        wv = WS.next(wblk(w_sin, j, 5120, 32))
        for ci, (o, n) in enumerate(t.chunks):
            b = psum.get()
            for kc in range(8):
                MM(b[0:n, 0:32], uT[:, kc, o:o + n], wv[:, kc, :], kc == 0, kc == 7)
            dtt = av(o_dtt + ci * 32, n, 32)
            TT("dve", dtt, b[0:n, 0:32], prows[0:n, j * 32:(j + 1) * 32], ALU.add)
            ACT(dtt, dtt, AF.Exp)
            ACT(dtt, dtt, AF.Ln, bias=1.0)

        if sp_:
            cc = av(o_xdt, 128, 24, 48)
            CP("dve", cc.rearrange("p a (s k) -> p a s k", k=3), xpre[:, :, :, 4:7])
            stg2 = av(o_xtl, 48, 3072)
            for blk in range(6):
                b = psum.get()
                for q4 in range(4):
                    ch = blk * 4 + q4
                    TR(b[0:48, q4 * 128:(q4 + 1) * 128], cc[:, ch, :], 128)
                evac_copy(stg2[:, blk * 512:(blk + 1) * 512], b[0:48, :])
            DMA(conv_s[j], stg2)
        else:
            CP("dve", hist(j), xpre[:, :, 0, Q:Q + 3])
            if t.last:
                stg2 = av(o_xtok, 3, 3072)
                for blk in range(6):
                    b = psum.get()
                    for q4 in range(4):
                        ch = blk * 4 + q4
                        TR(b[0:3, q4 * 128:(q4 + 1) * 128], xpre[:, ch, 0, Q:Q + 3], 128)
                    evac_copy(stg2[:, blk * 512:(blk + 1) * 512], b[0:3, :])
                DMA(conv_p[j], stg2)

        for cb in range(4):
            wv = WS.next(wblk(w_sin, j, cb * 512, 512))
            for ci, (o, n) in enumerate(t.chunks):
                b = psum.get()
                for kc in range(8):
                    MM(b[0:n, 0:512], uT[:, kc, o:o + n], wv[:, kc, :], kc == 0, kc == 7)
                ACT(av(o_z + ci * 2048 + cb * 512, n, 512), b[0:n, 0:512], AF.Silu)
        P.tag = (t.kind, t.ti, layer, "conv")
        def ovw(ch):
            return xbc[:, ch, :].rearrange("p (s l) -> p s l", l=L)
        for half in range(2):
            chs = range(half * 12, (half + 1) * 12)
            for ch in chs:
                cw = PC_CW + (j * 24 + ch) * 4
                ACT(ovw(ch), xpre[:, ch, :, 0:L], AF.Identity, scale=pcols[:, cw:cw + 1])
            for k in range(1, 4):
                for ch in chs:
                    cw = PC_CW + (j * 24 + ch) * 4
                    STT(ovw(ch), xpre[:, ch, :, k:k + L], pcols[:, cw + k:cw + k + 1], ovw(ch), ALU.mult, ALU.add)
            for ch in chs:
                ACT(xbc[:, ch, :], xbc[:, ch, :], AF.Silu, bias=pcols[:, PC_CB + j * 24 + ch:PC_CB + j * 24 + ch + 1])

        TRIm = cst[:, C_TRIS:C_TRIS + 64] if sp_ else cst[:, C_TRI:C_TRI + 128]
        LSTm = cst[:, C_LSTS:C_LSTS + 64] if sp_ else cst[:, C_LST:C_LST + 128]

        for ci, (o, n) in enumerate(t.chunks):
            Xtok = av(o_xtok, n, 2048)
            par = 0 if sp_ else ci % 2
            if sp_:
                Btok = avb(o_bt, n, 512)
                Xdt = avb(o_xdt, n, 2048)
                Xtl = avb(o_xtl, n, 2048)
                yacc = av(o_yacc, n, 2048)
                WTo = [0, 256]
            else:
                Btok = avb(o_bt + par * 256, n, 512)
                Xdt = avb([0, 1024][par], n, 2048)
                Xtl = avb([2048, 3072][par], n, 2048)
                yacc = av([4096, 24320][par], n, 2048)
                WTo = [26368, 26880]
            yw = av(o_yw, n, 2048)
            pso = par * 400
            dta = av(o_dta + pso, n, 32)
            Ee = av(o_ee + pso, n, 64)
            dB = av(o_db + pso, 128, NS * 32)
            dtt = av(o_dtt + ci * 32, n, 32)
            ztk = av(o_z + ci * 2048, n, 2048)
            P.tag = (t.kind, t.ti, layer, "A%d" % ci)
            for blk in range(4):
                b = psum.get()
                for q4 in range(4):
                    TR(b[0:n, q4 * 128:(q4 + 1) * 128], xbc[:, blk * 4 + q4, o:o + n], 128)
                evac_copy(Xtok[:, blk * 512:(blk + 1) * 512], b[0:n, :])
            b = psum.get()
            for g in range(4):
                TR(b[0:n, g * 128:(g + 1) * 128], xbc[:, 16 + g, o:o + n], 128)
            evac_copy(Btok[:, :], b[0:n, :])
            TT("dve", dta, dtt, a_bc[0:n, j * 32:(j + 1) * 32], ALU.mult)
            b = psum.get()
            MM(b[0:n, 0:32], TRIm[0:n, 0:n], dta, True, False)
            MM(b[0:n, 32:64], LSTm[0:n, 0:n], dta, False, True)
            ACT(Ee, b[0:n, 0:64], AF.Exp)
            if not sp_:
                b = psum.get()
                MM(b[:, 0:32], ones[0:n, :], dta, True, True)
                ACT(dB, b[:, 0:32], AF.Exp)
            dtl = av(o_sm + 232 + pso, n, 32)
            TT("dve", dtl, dtt, Ee[:, 32:64], ALU.mult)
            dsk = prows[0:n, 128 + j * 32:128 + (j + 1) * 32]
            for g in range(4):
                cs = slice(g * 512, (g + 1) * 512)
                hs = slice(g * 8, (g + 1) * 8)
                X3 = Xtok[:, cs].rearrange("p (h d) -> p h d", d=64)
                TT("dve", Xdt[:, cs].rearrange("p (h d) -> p h d", d=64), X3, bcl(dtt[:, hs], 64), ALU.mult)
                TT(POOLX, Xtl[:, cs].rearrange("p (h d) -> p h d", d=64), X3, bcl(dtl[:, hs], 64), ALU.mult)
                TT(POOLX, yacc[:, cs].rearrange("p (h d) -> p h d", d=64), X3, bcl(dsk[:, hs], 64), ALU.mult)
            P.tag = (t.kind, t.ti, layer, "B%d" % ci)
            RWo = [o_rb, o_wt]
            CBo = [o_cbt, o_sm + 256] if sp_ else [o_cbt, o_sm + 824]

            def stage_b1(g):
                CBTm = av(CBo[g % 2], n, n)
                Rb = av(RWo[g % 2], n, 8, n)
                b = psum.get()
                MM(b[0:n, 0:n], xbc[:, 16 + g, o:o + n], xbc[:, 20 + g, o:o + n], True, True)
                TT("dve", CBTm, b[0:n, 0:n], TRIm[0:n, 0:n], ALU.mult)
                TT("dve", Rb, bcm(TRIm[0:n, 0:n], 8), bcl(dta[:, g * 8:(g + 1) * 8], n), ALU.mult)

            def stage_b2(g):
                CBTm = av(CBo[g % 2], n, n)
                RW3 = av(RWo[g % 2], n, 8, n)
                WT = avb(WTo[g % 2], n, 8, n)
                rwf = av(RWo[g % 2], n, 8 * n)
                for hf in range(8 * n // 512):
                    bs = psum.get()
                    MM(bs[0:n, 0:512], LSTm[0:n, 0:n], rwf[:, hf * 512:(hf + 1) * 512], True, True)
                    ACT(rwf[:, hf * 512:(hf + 1) * 512], bs[0:n, 0:512], AF.Exp)
                TT("dve", WT, RW3, bcm(CBTm, 8), ALU.mult)
                b = psum.get()
                for r in range(8):
                    MM(b[0:n, r * 64:(r + 1) * 64], WT[:, r, :], Xdt[:, (g * 8 + r) * 64:(g * 8 + r + 1) * 64], r == 0, r == 7)
                TT("dve", yacc[:, g * 512:(g + 1) * 512], b[0:n, 0:512], yacc[:, g * 512:(g + 1) * 512], ALU.add)
            stage_b1(0)
            for g in range(4):
                if g + 1 < 4:
                    stage_b1(g + 1)
                stage_b2(g)
            P.tag = (t.kind, t.ti, layer, "C%d" % ci)
            YI, yid = psum.reserve(4)
            if not sp_:
                hstate = hst[j]
                for g in range(4):
                    MM(YI[g][0:n, 0:512], xbc[:, 20 + g, o:o + n], hstate[:, g * 512:(g + 1) * 512], True, True)
                ub = []
                for g in range(4):
                    b = psum.get()
                    MM(b[:, 0:512], Btok[:, g * 128:(g + 1) * 128], Xtl[:, g * 512:(g + 1) * 512], True, True)
                    ub.append(b)
                for g in range(4):
                    hs3 = hstate[:, g * 512:(g + 1) * 512].rearrange("p (h d) -> p h d", d=64)
                    TT("dve", hs3, hs3, bcl(dB[:, g * 8:(g + 1) * 8], 64), ALU.mult)
                for g in range(4):
                    TT("dve", hstate[:, g * 512:(g + 1) * 512], hstate[:, g * 512:(g + 1) * 512], ub[g][:, 0:512], ALU.add)
            else:
                dtaX = av(o_xdt, 64, 2048)
                CP("dve", dtaX.rearrange("p (h d) -> p h d", d=64), bcl(dta, 64))
                bD = psum.get()
                for c in range(16):
                    MM(bD[:, c * 16:(c + 1) * 16], dtaX[:, c * 128:(c + 1) * 128], cst[0:64, C_MSEL:C_MSEL + 16], c == 0, c == 15)
                dcolS = av(o_r2, 128, 16, 16)
                ACT(av(o_r2, 128, 256), bD[:, 0:256], AF.Exp)

                def st_load(s_):
                    DMA(av(NATIN[s_ % 3], 128, 16, 128), st_ssm[j, s_].rearrange("(c q) n -> q c n", q=128))

                def st_tr(s_):
                    natin = av(NATIN[s_ % 3], 128, 16, 128)
                    hts = avb(HTS[s_ % 2], 128, 2048)
                    for blk in range(4):
                        b = psum.get()
                        for q4 in range(4):
                            TR(b[:, q4 * 128:(q4 + 1) * 128], natin[:, blk * 4 + q4, :], 128)
                        evac_copy(hts[:, blk * 512:(blk + 1) * 512], b[:, :])
                    Cm = avb(CMB[s_ % 2], 128, 4, 64)
                    TT("dve", Cm, xbc[:, 20:24, 0:64], bcm(cst[:, C_MSBC + s_ * 64:C_MSBC + (s_ + 1) * 64], 4), ALU.mult)
                    Bm = avb(BMB[s_ % 2], 64, 512)
                    TS("dve", Bm, Btok[:, :], cst[0:64, C_MSEL + s_:C_MSEL + s_ + 1], ALU.mult)

                def st_comp(s_):
                    natin = av(NATIN[s_ % 3], 128, 16, 128)
                    hts = avb(HTS[s_ % 2], 128, 2048)
                    sout = av(SOUTB[s_ % 2], 128, 16, 128)
                    Cm = avb(CMB[s_ % 2], 128, 4, 64)
                    Bm = avb(BMB[s_ % 2], 64, 512)
                    for g in range(4):
                        MM(YI[g][0:n, 0:512], Cm[:, g, :], hts[:, g * 512:(g + 1) * 512], s_ == 0, s_ == NS - 1)
                    for blk in range(4):
                        b = psum.get()
                        for q4 in range(4):
                            c = blk * 4 + q4
                            MM(b[:, q4 * 128:(q4 + 1) * 128], Xtl[:, c * 128:(c + 1) * 128], Bm[:, blk * 128:(blk + 1) * 128],
                               q4 == 0, q4 == 3)
                        so = sout[:, blk * 4:(blk + 1) * 4, :]
                        TT(POOLX, so, natin[:, blk * 4:(blk + 1) * 4, :],
                           dcolS[:, blk * 4:(blk + 1) * 4, s_:s_ + 1].broadcast_to([128, 4, 128]), ALU.mult)
                        TT("dve", so, so, b[:, :].rearrange("p (a b) -> p a b", b=128), ALU.add)
                    DMA(ssm_s[j, s_].rearrange("(c q) n -> q c n", q=128), sout)
                st_load(0)
                st_load(1)
                st_load(2)
                st_tr(0)
                for s_ in range(NS):
                    if s_ + 1 < NS:
                        st_tr(s_ + 1)
                    st_comp(s_)
                    if s_ + 3 < NS:
                        st_load(s_ + 3)
            P.tag = (t.kind, t.ti, layer, "D%d" % ci)
            for g in range(4):
                ywg = yw[:, g * 512:(g + 1) * 512]
                TT("dve", ywg.rearrange("p (h d) -> p h d", d=64), YI[g][0:n, 0:512].rearrange("p (h d) -> p h d", d=64),
                   bcl(Ee[:, g * 8:(g + 1) * 8], 64), ALU.mult)
            for g in range(4):
                ywg = yw[:, g * 512:(g + 1) * 512]
                TT("dve", ywg, ywg, yacc[:, g * 512:(g + 1) * 512], ALU.add)
            psum.release(yid)
            for g in range(4):
                TT("dve", yw[:, g * 512:(g + 1) * 512], yw[:, g * 512:(g + 1) * 512], ztk[:, g * 512:(g + 1) * 512], ALU.mult)
            ss = av(o_ss + pso, n, 4)
            rs = av(o_rs + pso, n, 4)
            for g in range(4):
                ACT(yacc[:, g * 512:(g + 1) * 512], yw[:, g * 512:(g + 1) * 512], AF.Square, accum=ss[:, g:g + 1])
            ACT(rs, ss, AF.Ln, bias=EPS, scale=1.0 / 512)
            ACT(rs, rs, AF.Exp, scale=-0.5)
            for g in range(4):
                ACT(yw[:, g * 512:(g + 1) * 512], yw[:, g * 512:(g + 1) * 512], AF.Identity, scale=rs[:, g:g + 1])
            for blk in range(4):
                b = psum.get()
                for q4 in range(4):
                    c = blk * 4 + q4
                    TR(b[:, q4 * n:(q4 + 1) * n], yw[:, c * 128:(c + 1) * 128], n)
                TT("dve", yT[:, blk * 4:(blk + 1) * 4, o:o + n], b[:, 0:4 * n].rearrange("p (a b) -> p a b", b=n),
                   bcl(pcols[:, PC_SNW + j * 16 + blk * 4:PC_SNW + j * 16 + (blk + 1) * 4], n), ALU.mult)
        P.tag = (t.kind, t.ti, layer, "out")
        for ob in range(4):
            wv = WS.next(wblk(w_sout, j, ob * 256, 256))
            for oi in range(2):
                oc = ob * 2 + oi
                b = psum.get()
                for kc in range(16):
                    MM(b[:, 0:Q], wv[:, kc, oi * 128:(oi + 1) * 128], yT[:, kc, 0:Q], kc == 0, kc == 15)
                TT("dve", hT[:, oc, 0:Q], hT[:, oc, 0:Q], b[:, 0:Q], ALU.add)
        if (not sp_) and t.last:
            natout = av(0, 128, 16, 128)
            for blk in range(4):
                b = psum.get()
                for q4 in range(4):
                    c = blk * 4 + q4
                    TR(b[:, q4 * 128:(q4 + 1) * 128], hst[j][:, c * 128:(c + 1) * 128], 128)
                evac_copy(natout[:, blk * 4:(blk + 1) * 4, :], b[:, :].rearrange("p (a b) -> p a b", b=128))
            DMA(ssm_p[j].rearrange("(c q) n -> q c n", q=128), natout)

    def hgrn_layer(t, layer):
        j = layer // 2
        Q, C = t.Q, t.C
        sp_ = (t.kind == "s")
        B = 8 * Q
        NC = Q // C
        o_q, o_f, o_k, o_c, o_e1, o_e2, o_g, o_o = 0, B, 2 * B, 3 * B, 4 * B, 5 * B, 6 * B, 7 * B
        o_v = 8 * B
        o_kta = o_v + 2048
        o_ktc = o_kta + 1024
        o_att = o_ktc + 1024
        o_d = o_att + 1024
        o_sin = o_d + 256
        o_sout = o_sin + 1024
        assert o_sout + (4096 if sp_ else 1024) <= ARENA

        def buf(o_):
            return av(o_, 128, 8, Q)

        def flat(o_):
            return av(o_, 128, B)
        Qb, Fb, Kb, Cb, E1, E2, Gb, Ob = [buf(x) for x in (o_q, o_f, o_k, o_c, o_e1, o_e2, o_g, o_o)]
        oTf = yTb[:, 0:8, 0:Q]
        lbc = misc[:, 64 + j * 8:64 + (j + 1) * 8]
        omlc = misc[:, 80 + j * 8:80 + (j + 1) * 8]

        P.tag = (t.kind, t.ti, layer, 'hg_in')
        rmsnorm(t, PC_MIX + layer * 8)

        def fm_block(c0, fn):
            for blk in range(2):
                wv = WS.next(wblk(w_hin, j, c0 + blk * 512, 512))
                for oi in range(4):
                    h = blk * 4 + oi
                    b = psum.get()
                    for kc in range(8):
                        MM(b[:, 0:Q], wv[:, kc, oi * 128:(oi + 1) * 128], uT[:, kc, 0:Q], kc == 0, kc == 7)
                    fn(h, b[:, 0:Q])
        fm_block(1024, lambda h, b: CP("act", Fb[:, h, :], b))
        fm_block(0, lambda h, b: ACT(Qb[:, h, :], b, AF.Silu))
        for blk in range(2):
            wv = WS.next(wblk(w_hin, j, 2048 + blk * 512, 512))
            for ci, (o, n) in enumerate(t.chunks):
                b = psum.get()
                for kc in range(8):
                    MM(b[0:n, 0:512], uT[:, kc, o:o + n], wv[:, kc, :], kc == 0, kc == 7)
                evac_copy(avb(o_v + ci * 1024, n, 1024)[:, blk * 512:(blk + 1) * 512], b[0:n, 0:512])
        fm_block(3072, lambda h, b: ACT(Gb[:, h, :], b, AF.Silu))

        P.tag = (t.kind, t.ti, layer, 'hg_chain')
        rm = cst[:, C_RM4:C_RM4 + 64] if sp_ else cst[:, C_RM32:C_RM32 + 256]
        HB = 4 * Q
        NCh = 4 * NC
        dcy = av(o_d, 128, 8 * NC)

        def hf2(o_, hf):
            return av(o_ + hf * HB, 128, HB)

        def hf3(o_, hf):
            return av(o_ + hf * HB, 128, 4, Q)

        def hfc(o_, hf):
            return av(o_ + hf * HB, 128, NCh, C)
        steps = [
            lambda hf: ACT(hf2(o_k, hf), hf2(o_f, hf), AF.Sigmoid, scale=-1.0),
            lambda hf: ACT(hf2(o_f, hf), hf2(o_f, hf), AF.Sigmoid),
            lambda hf: [ACT(Fb[:, hf * 4 + h4, :], Fb[:, hf * 4 + h4, :], AF.Ln, scale=omlc[:, hf * 4 + h4:hf * 4 + h4 + 1],
                            bias=lbc[:, hf * 4 + h4:hf * 4 + h4 + 1]) for h4 in range(4)],
            lambda hf: TT("dve", hf3(o_k, hf), hf3(o_k, hf), bcl(omlc[:, hf * 4:(hf + 1) * 4], Q), ALU.mult),
            lambda hf: [SCAN(Cb[:, hf * 4 + h4, :], rm, Fb[:, hf * 4 + h4, :]) for h4 in range(4)],
            lambda hf: ACT(hf2(o_e1, hf), hf2(o_c, hf), AF.Exp),
            lambda hf: TT("dve", hf2(o_q, hf), hf2(o_q, hf), hf2(o_e1, hf), ALU.mult),
            lambda hf: ACT(hf2(o_e1, hf), hf2(o_c, hf), AF.Exp, scale=-1.0),
            lambda hf: TT("dve", hf2(o_e1, hf), hf2(o_e1, hf), hf2(o_k, hf), ALU.mult),
            lambda hf: TT("dve", hfc(o_e2, hf), hfc(o_c, hf)[:, :, C - 1:C].broadcast_to([128, NCh, C]), hfc(o_c, hf), ALU.subtract),
            lambda hf: ACT(hf2(o_e2, hf), hf2(o_e2, hf), AF.Exp),
            lambda hf: TT("dve", hf2(o_e2, hf), hf2(o_e2, hf), hf2(o_k, hf), ALU.mult),
            lambda hf: ACT(dcy[:, hf * NCh:(hf + 1) * NCh].unsqueeze(2), hfc(o_c, hf)[:, :, C - 1:C], AF.Exp),
        ]
        for st in steps:
            for hf in range(2):
                st(hf)

        if sp_:
            agroups = [(0, 64)]
            BD = cst[0:64, C_TRIS:C_TRIS + 64]
        else:
            agroups = [(0, 128), (128, 128)]
            BD = cst[:, C_BD32:C_BD32 + 128]
        for ai, (ao, an) in enumerate(agroups):
            P.tag = (t.kind, t.ti, layer, 'hg_att%d' % ai)
            NCA = an // C
            hpb = 512 // an
            nb = 8 // hpb
            gp = 0 if sp_ else ai % 2
            attm = avb([o_att, 24320][gp], an, 8 * an)
            ktall = av([o_kta, 25344][gp], an, 1024)
            Vt = avb(o_v + ai * 1024, an, 1024)
            for bi in range(nb):
                b = psum.get()
                for hh in range(hpb):
                    h = bi * hpb + hh
                    MM(b[0:an, hh * an:(hh + 1) * an], E1[:, h, ao:ao + an], Qb[:, h, ao:ao + an], hh == 0, hh == hpb - 1)
                TT("dve", attm[:, bi * 512:(bi + 1) * 512].rearrange("p (a b) -> p a b", b=an),
                   b[0:an, :].rearrange("p (a b) -> p a b", b=an), bcm(BD, hpb), ALU.mult)
            for bi in range(2):
                b = psum.get()
                for hh in range(4):
                    TR(b[0:an, hh * 128:(hh + 1) * 128], E2[:, bi * 4 + hh, ao:ao + an], 128)
                evac_copy(ktall[:, bi * 512:(bi + 1) * 512], b[0:an, :])
            OB, obid = psum.reserve(nb)
            for h in range(8):
                bi, hh = h // hpb, h % hpb
                MM(OB[bi][:, hh * an:(hh + 1) * an], Vt[:, h * 128:(h + 1) * 128], attm[:, h * an:(h + 1) * an],
                   hh == 0, False)
            if sp_:
                KTC = [o_ktc, o_sout + 1024]
                SINB = [o_sin, o_sout + 2048]
                SOB = [o_sout, o_sout + 3072]
            else:
                KTC = [[o_f, o_f + 1024, o_k, o_k + 1024], [o_c, o_c + 1024, o_ktc, 26368]][gp]

            def mk_ktc(c):
                ktc = avb(KTC[c % len(KTC)], an, 1024)
                msk = cst[0:64, C_MSEL + c:C_MSEL + c + 1] if sp_ else cst[:, C_MB32 + c:C_MB32 + c + 1]
                TS(POOLX, ktc, ktall, msk, ALU.mult)
            if sp_:
                DMA(av(SINB[0], 128, 8, 128), st_hgrn[j, 0].rearrange("h k v -> k h v"))
                mk_ktc(0)
            else:
                for c in range(NCA):
                    mk_ktc(c)
            for c in range(NCA):
                cg = ao // C + c
                tok0 = ao + c * C
                if sp_:
                    Sst = av(SINB[c % 2], 128, 8, 128)
                    Sds = av(SOB[c % 2], 128, 8, 128)
                    if c + 1 < NCA:
                        DMA(av(SINB[(c + 1) % 2], 128, 8, 128), st_hgrn[j, c + 1].rearrange("h k v -> k h v"))
                        mk_ktc(c + 1)
                else:
                    Sst = shg[j]
                    Sds = shg[j]
                for h in range(8):
                    bi, hh = h // hpb, h % hpb
                    MM(OB[bi][:, hh * an + c * C:hh * an + (c + 1) * C], Sst[:, h, :], Qb[:, h, tok0:tok0 + C],
                       False, (c == NCA - 1 and hh == hpb - 1))
                ktc = avb(KTC[c % len(KTC)], an, 1024)
                ubs = []
                for bi2 in range(2):
                    b = psum.get()
                    for hh in range(4):
                        h = bi2 * 4 + hh
                        MM(b[:, hh * 128:(hh + 1) * 128], ktc[:, h * 128:(h + 1) * 128], Vt[:, h * 128:(h + 1) * 128],
                           hh == 0, hh == 3)
                    ubs.append(b)
                for bi2 in range(2):
                    for hh in range(4):
                        h = bi2 * 4 + hh
                        STT(Sds[:, h, :], Sst[:, h, :], dcy[:, h * NC + cg:h * NC + cg + 1], ubs[bi2][:, hh * 128:(hh + 1) * 128],
                            ALU.mult, ALU.add)
                if sp_:
                    DMA(hgrn_s[j, c].rearrange("h k v -> k h v"), Sds)
            for bi in range(nb):
                CP("act", Ob[:, bi * hpb:(bi + 1) * hpb, ao:ao + an], OB[bi][:, :].rearrange("p (a b) -> p a b", b=an))
            psum.release(obid)

        P.tag = (t.kind, t.ti, layer, 'hg_out')
        for ai, (ao, an) in enumerate(agroups):
            tsl = slice(ao, ao + an)
            ACT(Fb[:, :, tsl], Ob[:, :, tsl], AF.Square)
            hb = 512 // an
            for bi in range(8 // hb):
                b = psum.get()
                for hh in range(hb):
                    h = bi * hb + hh
                    MM(b[:, hh * an:(hh + 1) * an], ones[:, :], Fb[:, h, tsl], hh == 0, hh == hb - 1)
                ACT(E1[:, bi * hb:(bi + 1) * hb, tsl], b[:, 0:hb * an].rearrange("p (a b) -> p a b", b=an), AF.Ln,
                    bias=EPS, scale=1.0 / 128)
            ACT(E1[:, :, tsl], E1[:, :, tsl], AF.Exp, scale=-0.5)
            TT("dve", Ob[:, :, tsl], Ob[:, :, tsl], E1[:, :, tsl], ALU.mult)
            TT("dve", Ob[:, :, tsl], Ob[:, :, tsl], Gb[:, :, tsl], ALU.mult)
            TT("dve", oTf[:, :, tsl], Ob[:, :, tsl], bcl(pcols[:, PC_HNW + j * 8:PC_HNW + (j + 1) * 8], an), ALU.mult)
        for ob in range(2):
            wv = WS.next(wblk(w_hout, j, ob * 512, 512))
            for oi in range(4):
                oc = ob * 4 + oi
                b = psum.get()
                for kc in range(8):
                    MM(b[:, 0:Q], wv[:, kc, oi * 128:(oi + 1) * 128], oTf[:, kc, :], kc == 0, kc == 7)
                TT("dve", hT[:, oc, 0:Q], hT[:, oc, 0:Q], b[:, 0:Q], ALU.add)
        if (not sp_) and t.last:
            DMA(hgrn_p[j].rearrange("h k v -> k h v"), shg[j][:, :, :])

    def mlp_layer(t, layer):
        Q = t.Q
        P.tag = (t.kind, t.ti, layer, "mlp")
        hid = arena[:, 0:4096].bitcast(BF16).rearrange("p (a b) -> p a b", b=256)[:, :, 0:Q]
        rmsnorm(t, PC_MLP + layer * 8)
        pend = [None]
        for ub in range(8):
            wv = WS.next(wblk(w_up, layer, ub * 512, 512))
            for oi in range(4):
                oc = ub * 4 + oi
                b = psum.get()
                for kc in range(8):
                    MM(b[:, 0:Q], wv[:, kc, oi * 128:(oi + 1) * 128], uT[:, kc, 0:Q], kc == 0, kc == 7)
                rt = av(4096 + (oc % 4) * 256, 128, Q)
                ACT(rt, b[:, 0:Q], AF.Relu)
                if pend[0] is not None:
                    pend[0]()
                pend[0] = (lambda oc=oc, rt=rt: TT("dve", hid[:, oc, :], rt, rt, ALU.mult))
        pend[0]()
        for db in range(8):
            wv = WS.next(wblk(w_dn, layer, db * 128, 128))
            b = psum.get()
            for kc in range(32):
                MM(b[:, 0:Q], wv[:, kc, :], hid[:, kc, :], kc == 0, kc == 31)
            TT("dve", hT[:, db, 0:Q], hT[:, db, 0:Q], b[:, 0:Q], ALU.add)

    def tile_prog(t):
        Q = t.Q
        src = xs if t.kind == "s" else xp
        dst = y_s if t.kind == "s" else y_p
        row0 = 0 if t.kind == "s" else t.ti * 256
        for ci, (o, n) in enumerate(t.chunks):
            xt = av(ci * 1024, n, 1024)
            DMA(xt, src[row0 + o:row0 + o + n, :])
            for bi in range(2):
                b = psum.get()
                for q4 in range(4):
                    kc = bi * 4 + q4
                    TR(b[:, q4 * n:(q4 + 1) * n], xt[:, kc * 128:(kc + 1) * 128], n)
                evac_copy(hT[:, bi * 4:(bi + 1) * 4, o:o + n], b[:, 0:4 * n].rearrange("p (a b) -> p a b", b=n))
        sub = 0
        for layer in range(4):
            if sub < NSUB:
                if layer % 2 == 0:
                    ssd_layer(t, layer)
                else:
                    hgrn_layer(t, layer)
            sub += 1
            if sub < NSUB:
                mlp_layer(t, layer)
            sub += 1
        if FINAL_NORM:
            fin = av(4096, 128, 8, Q)
            rmsnorm(t, PC_FIN, fin)
        else:
            fin = hT
        for ci, (o, n) in enumerate(t.chunks):
            yt = av(ci * 1024, n, 1024)
            for bi in range(2):
                b = psum.get()
                for q4 in range(4):
                    kc = bi * 4 + q4
                    TR(b[0:n, q4 * 128:(q4 + 1) * 128], fin[:, kc, o:o + n], 128)
                evac_copy(yt[:, bi * 512:(bi + 1) * 512], b[0:n, :])
            DMA(dst[row0 + o:row0 + o + n, :], yt)

    def whole():
        setup()
        if DO_SAMPLE and SAMPLE_FIRST:
            tile_prog(mk_tile("s", 0))
        for ti in range(NPT):
            tile_prog(mk_tile("p", ti))
        if DO_SAMPLE and not SAMPLE_FIRST:
            tile_prog(mk_tile("s", 0))

    P.plan = True
    whole()
    P.plan = False
    psum.__init__()
    evac_rr[0] = 0
    whole()
    assert WS.cur == len(WS.specs)
    P.run()
    return P


def make_in_maps(inp):
    f = lambda a: np.ascontiguousarray(np.asarray(a, np.float32))
    pc = _pack_cols(inp)
    pr = _pack_rows(inp)
    shared = {k: f(inp[k]) for k in ("ssd_w_in", "ssd_w_out", "hgrn_w_in", "hgrn_w_out", "mlp_w_up", "mlp_w_down")}
    maps = []
    for c in range(8):
        sl = slice(16 * c, 16 * (c + 1))
        m = dict(shared)
        m["xp"] = f(inp["x_prompt"][c])
        m["xs"] = f(np.asarray(inp["x_sample"])[sl].reshape(64, D))
        m["st_conv"] = f(np.asarray(inp["state_ssd_conv"])[:, sl].reshape(2, 48, 3072))
        m["st_ssm"] = f(np.asarray(inp["state_ssd_ssm"])[:, sl].reshape(2, 16, 2048, 128))
        m["st_hgrn"] = f(np.asarray(inp["state_hgrn"])[:, sl])
        m["pcols"] = pc
        m["prows"] = pr
        maps.append(m)
    return maps


def assemble(results):
    r = results
    y_p = np.stack([r[c]["y_p"] for c in range(8)], 0)
    y_s = np.concatenate([r[c]["y_s"].reshape(16, 4, D) for c in range(8)], 0)
    conv_p = np.stack([r[c]["conv_p"] for c in range(8)], 1)
    ssm_p = np.stack([r[c]["ssm_p"].reshape(2, 32, 64, 128) for c in range(8)], 1)
    hgrn_p = np.stack([r[c]["hgrn_p"] for c in range(8)], 1)
    conv_s = np.concatenate([r[c]["conv_s"].reshape(2, 16, 3, 3072) for c in range(8)], 1)
    ssm_s = np.concatenate([r[c]["ssm_s"].reshape(2, 16, 32, 64, 128) for c in range(8)], 1)
    hgrn_s = np.concatenate([r[c]["hgrn_s"] for c in range(8)], 1)
    return tuple(np.ascontiguousarray(a, dtype=np.float32)
                 for a in (y_p, y_s, conv_p, ssm_p, hgrn_p, conv_s, ssm_s, hgrn_s))


def kernel(**inputs):
    nc = bass.Bass("TRN2", target_bir_lowering=False)
    build(nc, {})
    in_maps = make_in_maps(inputs)
    res = run_bass_kernel_spmd(nc, in_maps, core_ids=list(range(8)))
    return assemble(res.results)
```

```python
import numpy as np
import concourse.bass as bass
import concourse.mybir as mybir
from concourse.bass_utils import run_bass_kernel_spmd

F32 = mybir.dt.float32
BF16 = mybir.dt.bfloat16
AF = mybir.ActivationFunctionType
ALU = mybir.AluOpType

D = 1024
SEQ = 2048
NSEQ_S = 16
LS = 4
EPS = 1e-5
ENGS = ("sp", "pe", "act", "dve", "pool")
PAGE = 64
EPOCH = 12000
WSLOT = 4096
NBUF = 5
ARENA = 27392


class Inst:
    __slots__ = ("eng", "fn", "waits", "sem", "val", "idx", "needs_inc", "is_dma", "deps", "succ", "ndeps",
                 "est", "occ", "lat", "grp", "gidx", "fin", "tag", "st", "crit")

    def __init__(self, eng, fn, is_dma):
        self.eng = eng
        self.fn = fn
        self.waits = []
        self.sem = None
        self.val = None
        self.idx = -1
        self.needs_inc = False
        self.is_dma = is_dma
        self.deps = ()
        self.succ = []
        self.ndeps = 0
        self.est = 0.0
        self.occ = 0.1
        self.lat = 0.1
        self.grp = None
        self.gidx = 0
        self.fin = 0.0
        self.tag = None
        self.st = 0.0
        self.crit = None


_ESZ = {}


def _esize(dt):
    k = str(dt)
    v = _ESZ.get(k)
    if v is None:
        v = 2 if ("bfloat16" in k or "float16" in k) else 4
        _ESZ[k] = v
    return v


def ap_pages(ap):
    space = str(ap.space).upper()
    if "DRAM" in space or "HBM" in space:
        return ()
    name = ap.tensor.name
    if "PSUM" in space:
        return ((name, 0),)
    es = _esize(ap.dtype)
    apl = ap.ap
    row = apl[0][0]
    foff = (ap.offset % row if row > 0 else ap.offset) * es
    dims = [(abs(s) * es, n) for (s, n) in apl[1:] if n > 1 and s != 0]
    PB = 256
    if not dims:
        return ((name, foff // PB),)
    dims.sort()
    s0, n0 = dims[0]
    inner = (n0 - 1) * s0 + es
    outer = dims[1:]
    nouter = 1
    for _, n in outer:
        nouter *= n
    out = set()
    if nouter <= 512:
        starts = [foff]
        for s_, n in outer:
            starts = [b_ + i * s_ for b_ in starts for i in range(n)]
        for b_ in starts:
            for pg in range(b_ // PB, (b_ + inner - 1) // PB + 1):
                out.add((name, pg))
    else:
        ext = inner + sum((n - 1) * s_ for s_, n in outer)
        for pg in range(foff // PB, (foff + ext - 1) // PB + 1):
            out.add((name, pg))
    return tuple(out)


class Prog:
    def __init__(self, nc, n_dma_sems=24, n_epochs=14):
        self.nc = nc
        self.plan = False
        self.sched = True
        self.all = []
        self.q = {e: [] for e in ENGS}
        self.lastw = {}
        self.readers = {}
        self.dma_sems = {"sp": [nc.alloc_semaphore(f"dq{i}") for i in range(n_dma_sems)],
                         "pool": [nc.alloc_semaphore(f"dg{i}") for i in range(8)]}
        self.dma_n = {"sp": 0, "pool": 0}
        self.dma_last = {"sp": [None] * n_dma_sems, "pool": [None] * 8}
        self.eng_sems = {e: [nc.alloc_semaphore(f"s_{e}_{k}") for k in range(n_epochs)]
                         for e in ("pe", "act", "dve", "pool")}
        self.n_inst = 0
        self.tag = None
        self.prio = "order"
        self.prio_w = 0.0

    def emit(self, eng, fn, outs=(), ins=(), dma=False, occ=0.2, lat=None, grp=None, extra=()):
        if self.plan:
            return None
        inst = Inst(eng, fn, dma)
        inst.gidx = len(self.all)
        inst.tag = self.tag
        inst.occ = occ
        inst.lat = occ if lat is None else lat
        inst.grp = grp
        rp = set()
        for a in ins:
            rp.update(ap_pages(a))
        wp = set()
        for a in outs:
            wp.update(ap_pages(a))
        deps = set()
        for r in rp:
            w = self.lastw.get(r)
            if w is not None:
                deps.add(w)
        for w_ in wp:
            w = self.lastw.get(w_)
            if w is not None:
                deps.add(w)
            rd = self.readers.get(w_)
            if rd:
                deps.update(rd)
        if dma:
            pool_ = self.dma_sems[eng]
            k = self.dma_n[eng] % len(pool_)
            inst.sem = pool_[k]
            inst.val = 16 * (self.dma_n[eng] // len(pool_) + 1)
            prev = self.dma_last[eng][k]
            if prev is not None:
                deps.add(prev)
            self.dma_last[eng][k] = inst
            self.dma_n[eng] += 1
        for x_ in extra:
            if x_ is not None:
                deps.add(x_)
        deps.discard(inst)
        inst.deps = deps
        for r in rp:
            if r not in wp:
                self.readers.setdefault(r, []).append(inst)
        for w_ in wp:
            self.lastw[w_] = inst
            self.readers[w_] = []
        self.all.append(inst)
        self.n_inst += 1
        return inst

    def schedule(self):
        import heapq
        HOP = 0.3
        for inst in self.all:
            inst.ndeps = len(inst.deps)
            for d in inst.deps:
                d.succ.append(inst)
        if self.prio == "cp":
            rank = {}
            for inst in reversed(self.all):
                r = 0.0
                for s_ in inst.succ:
                    rs_ = rank[s_]
                    if rs_ > r:
                        r = rs_
                rank[inst] = r + (inst.lat if inst.is_dma else inst.occ)
            W_ = self.prio_w
            for inst in self.all:
                inst.gidx = inst.gidx - W_ * rank[inst]
        fut = {e: [] for e in ENGS}
        avail = {e: [] for e in ENGS}
        free = {e: 0.0 for e in ENGS}
        cur_grp = [None]
        for inst in self.all:
            if inst.ndeps == 0:
                heapq.heappush(fut[inst.eng], (0.0, inst.gidx, inst))
        nleft = len(self.all)
        while nleft:
            best = None
            for e in ENGS:
                fq, aq = fut[e], avail[e]
                while fq and fq[0][0] <= free[e]:
                    _, gi, it = heapq.heappop(fq)
                    heapq.heappush(aq, (gi, it))
                if aq:
                    cand = (free[e], aq[0][0], e, True)
                elif fq:
                    cand = (fq[0][0], fq[0][1], e, False)
                else:
                    continue
                if best is None or cand < best:
                    best = cand
            start, _, e, from_av = best
            if from_av:
                aq = avail[e]
                pick = None
                if e == "act" and cur_grp[0] is not None and len(aq) > 1:
                    small = heapq.nsmallest(6, aq)
                    for gi, it in small:
                        if it.grp is None or it.grp == cur_grp[0]:
                            pick = (gi, it)
                            break
                    if pick is not None and pick != aq[0]:
                        aq.remove(pick)
                        heapq.heapify(aq)
                    else:
                        pick = heapq.heappop(aq)
                else:
                    pick = heapq.heappop(aq)
                inst = pick[1]
            else:
                _, _, inst = heapq.heappop(fut[e])
            occ = inst.occ
            if e == "act" and inst.grp is not None:
                if cur_grp[0] is not None and cur_grp[0] != inst.grp:
                    occ += 1.3
                cur_grp[0] = inst.grp
            inst.st = start
            if self.q[e] and start <= free[e] + 1e-9 and free[e] > 0:
                inst.crit = self.q[e][-1]
            else:
                cd = None
                for d_ in inst.deps:
                    if cd is None or d_.fin > cd.fin:
                        cd = d_
                inst.crit = cd
            inst.fin = start + (inst.lat if inst.is_dma else occ)
            free[e] = start + occ
            inst.idx = len(self.q[e])
            self.q[e].append(inst)
            nleft -= 1
            for s_ in inst.succ:
                t_ = inst.fin + (HOP if s_.eng != e else 0.06)
                if t_ > s_.est:
                    s_.est = t_
                s_.ndeps -= 1
                if s_.ndeps == 0:
                    heapq.heappush(fut[s_.eng], (s_.est, s_.gidx, s_))
        self.makespan = max(free.values())

    def finalize(self):
        if self.sched:
            self.schedule()
        else:
            for inst in self.all:
                inst.idx = len(self.q[inst.eng])
                self.q[inst.eng].append(inst)
        for e in ENGS:
            waited = {}
            for inst in self.q[e]:
                best = {}
                for d in inst.deps:
                    if d.is_dma:
                        key = ("dma", d.sem.name)
                        if waited.get(key, 0) >= d.val:
                            continue
                        cur = best.get(key)
                        if cur is None or d.val > cur.val:
                            best[key] = d
                    else:
                        if d.eng == "pe" and e == "pe":
                            continue
                        if waited.get(d.eng, -1) >= d.idx:
                            continue
                        cur = best.get(d.eng)
                        if cur is None or d.idx > cur.idx:
                            best[d.eng] = d
                for key, d in best.items():
                    if d.is_dma:
                        waited[key] = d.val
                    else:
                        waited[d.eng] = d.idx
                        d.needs_inc = True
                    inst.waits.append(d)
        for e in ("pe", "act", "dve", "pool"):
            k = 0
            for inst in self.q[e]:
                if inst.needs_inc:
                    inst.sem = self.eng_sems[e][k // EPOCH]
                    inst.val = k % EPOCH + 1
                    k += 1
            assert k <= EPOCH * len(self.eng_sems[e]), (e, k)
        fin = Inst("sp", None, False)
        for lst in self.dma_last.values():
            for d in lst:
                if d is not None:
                    fin.waits.append(d)
        self.q["sp"].append(fin)

    def replay(self, eng_name, e):
        for inst in self.q[eng_name]:
            for d in inst.waits:
                e.wait_ge(d.sem, d.val)
            if inst.fn is None:
                continue
            bi = inst.fn(e)
            if inst.is_dma:
                bi.then_inc(inst.sem, 16)
            elif inst.needs_inc:
                bi.then_inc(inst.sem, 1)

    def run(self):
        self.finalize()
        with self.nc.Block() as block:
            @block.sync
            def _(e):
                self.replay("sp", e)

            @block.tensor
            def _(e):
                self.replay("pe", e)

            @block.scalar
            def _(e):
                self.replay("act", e)

            @block.vector
            def _(e):
                self.replay("dve", e)

            @block.gpsimd
            def _(e):
                self.replay("pool", e)


def _isap(x):
    return not isinstance(x, (int, float))


C_ID, C_TRI, C_LST, C_BD32, C_MB32, C_TRIS, C_LSTS, C_MSEL, C_MSBC, C_RM32, C_RM4, C_END = (
    0, 128, 256, 384, 512, 516, 580, 644, 660, 1684, 1940, 2004)


def _const_table():
    c = np.zeros((128, C_END), np.float32)
    k = np.arange(128)
    c[:, C_ID:C_ID + 128] = np.eye(128)
    c[:, C_TRI:C_TRI + 128] = (k[:, None] <= k[None, :])
    c[:, C_LST:C_LST + 128] = (k[:, None] > k[None, :])
    c[:, C_BD32:C_BD32 + 128] = (k[:, None] <= k[None, :]) & (k[:, None] // 32 == k[None, :] // 32)
    c[:, C_MB32:C_MB32 + 4] = (k[:, None] // 32 == np.arange(4)[None, :])
    k6 = np.arange(64)
    same = (k6[:, None] // 4 == k6[None, :] // 4)
    c[:64, C_TRIS:C_TRIS + 64] = (k6[:, None] <= k6[None, :]) & same
    c[:64, C_LSTS:C_LSTS + 64] = (k6[:, None] > k6[None, :]) & same
    c[:64, C_MSEL:C_MSEL + 16] = (k6[:, None] // 4 == np.arange(16)[None, :])
    ms = (np.arange(16)[:, None] == (k6[None, :] // 4)).astype(np.float32)
    c[:, C_MSBC:C_MSBC + 1024] = ms.reshape(1, 1024)
    t = np.arange(256)
    c[:, C_RM32:C_RM32 + 256] = (t % 32 != 0)[None, :]
    c[:, C_RM4:C_RM4 + 64] = (np.arange(64) % 4 != 0)[None, :]
    return c


PC_MIX, PC_MLP, PC_FIN, PC_CW, PC_CB, PC_SNW, PC_HNW, PC_LBR, PC_END = 0, 32, 64, 72, 264, 312, 344, 360, 376


def _pack_cols(inp):
    def cols(v):
        v = np.asarray(v, np.float32)
        sh = v.shape[:-1]
        n = v.shape[-1] // 128
        return np.moveaxis(v.reshape(sh + (n, 128)), -1, 0)
    pc = np.zeros((128, PC_END), np.float32)
    pc[:, PC_MIX:PC_MIX + 32] = cols(inp["norm_mix_w"]).reshape(128, 32)
    pc[:, PC_MLP:PC_MLP + 32] = cols(inp["norm_mlp_w"]).reshape(128, 32)
    pc[:, PC_FIN:PC_FIN + 8] = cols(inp["norm_f_w"]).reshape(128, 8)
    cw = cols(inp["ssd_conv_w"])
    pc[:, PC_CW:PC_CW + 192] = np.transpose(cw, (0, 1, 3, 2)).reshape(128, 192)
    pc[:, PC_CB:PC_CB + 48] = cols(inp["ssd_conv_b"]).reshape(128, 48)
    pc[:, PC_SNW:PC_SNW + 32] = cols(inp["ssd_norm_w"]).reshape(128, 32)
    pc[:, PC_HNW:PC_HNW + 16] = cols(inp["hgrn_norm_w"]).reshape(128, 16)
    pc[:, PC_LBR:PC_LBR + 16] = cols(inp["hgrn_lb_raw"]).reshape(128, 16)
    return pc


def _pack_rows(inp):
    pr = np.zeros((128, 192), np.float32)
    pr[:, 0:64] = np.asarray(inp["ssd_dt_bias"], np.float32).reshape(1, 64)
    pr[:, 64:128] = np.asarray(inp["ssd_a_log"], np.float32).reshape(1, 64)
    pr[:, 128:192] = np.asarray(inp["ssd_d"], np.float32).reshape(1, 64)
    return pr


def build(nc, cfg):
    NPT = cfg.get("np_tiles", 8)
    NSUB = cfg.get("nsub", 8)
    DO_SAMPLE = cfg.get("do_sample", True)
    FINAL_NORM = cfg.get("final_norm", True)
    P = Prog(nc)
    P.sched = cfg.get("sched", True)
    P.prio = cfg.get("prio", "cp")
    P.prio_w = cfg.get("prio_w", 10.0)
    USE_WCACHE = cfg.get("wcache", True)
    POOLX = cfg.get("poolx", "dve")
    SAMPLE_FIRST = cfg.get("sample_first", False)

    def din(name, shape):
        return nc.dram_tensor(name, list(shape), F32, kind="ExternalInput")

    def dout(name, shape):
        return nc.dram_tensor(name, list(shape), F32, kind="ExternalOutput")

    xp = din("xp", [SEQ, D])
    xs = din("xs", [64, D])
    st_conv = din("st_conv", [2, 48, 3072])
    st_ssm = din("st_ssm", [2, 16, 2048, 128])
    st_hgrn = din("st_hgrn", [2, 16, 8, 128, 128])
    w_sin = din("ssd_w_in", [2, 1024, 5152])
    w_sout = din("ssd_w_out", [2, 2048, 1024])
    w_hin = din("hgrn_w_in", [2, 1024, 4096])
    w_hout = din("hgrn_w_out", [2, 1024, 1024])
    w_up = din("mlp_w_up", [4, 1024, 4096])
    w_dn = din("mlp_w_down", [4, 4096, 1024])
    pcols_d = din("pcols", [128, PC_END])
    prows_d = din("prows", [128, 192])
    y_p = dout("y_p", [SEQ, D])
    y_s = dout("y_s", [64, D])
    conv_p = dout("conv_p", [2, 3, 3072])
    ssm_p = dout("ssm_p", [2, 2048, 128])
    hgrn_p = dout("hgrn_p", [2, 8, 128, 128])
    conv_s = dout("conv_s", [2, 48, 3072])
    ssm_s = dout("ssm_s", [2, 16, 2048, 128])
    hgrn_s = dout("hgrn_s", [2, 16, 8, 128, 128])
    cst_d = nc.inline_tensor(_const_table(), "cst_tab")

    sb = nc.alloc_sbuf_tensor
    cst = sb("cst", [128, C_END], F32)
    pcols = sb("pcols_sb", [128, PC_END], F32)
    prows = sb("prows_sb", [128, 192], F32)
    ones = sb("ones", [128, 128], F32)
    misc = sb("misc", [128, 512], F32)
    rstd = sb("rstd", [128, 256], F32)
    hT = sb("hT", [128, 8, 256], F32)
    uT = sb("uT", [128, 8, 256], BF16)
    yTb = sb("yTb", [128, 16, 256], BF16)
    ones_b = sb("ones_b", [128, 128], BF16)
    hst = [sb(f"hst{j}", [128, 2048], F32) for j in range(2)]
    shg = [sb(f"shg{j}", [128, 8, 128], F32) for j in range(2)]
    wring = [sb(f"wring{i}", [128, WSLOT], BF16) for i in range(NBUF)]
    arena = sb("arena", [128, ARENA], F32)
    PS = [nc.alloc_psum_tensor(f"ps{i}", [128, 512], F32) for i in range(8)]

    ident = cst[:, C_ID:C_ID + 128]

    def fsz(ap):
        n = 1
        for d_ in ap.shape[1:]:
            n *= d_
        return n

    GRP = {str(AF.Exp): "E", str(AF.Ln): "E", str(AF.Sigmoid): "S", str(AF.Sqrt): "Q", str(AF.Silu): "U"}

    def MM(out, lhsT, rhs, start, stop):
        passes = 4 if _esize(rhs.dtype) == 4 else 1
        P.emit("pe", lambda e: e.matmul(out, lhsT, rhs, start=start, stop=stop, skip_group_check=True),
               [out], [lhsT, rhs], occ=0.015 + fsz(rhs) * passes / 2000.0, lat=0.2 + fsz(rhs) * passes / 2000.0)

    def TR(out, in_, k):
        idn = cst[0:k, C_ID:C_ID + k]
        P.emit("pe", lambda e: e.transpose(out, in_, idn), [out], [in_, idn], occ=0.12, lat=0.3)

    def ACT(out, in_, func, bias=None, scale=None, accum=None):
        kw = {}
        ins = [in_]
        outs = [out]
        if bias is not None:
            kw["bias"] = bias
            if _isap(bias):
                ins.append(bias)
        if scale is not None:
            kw["scale"] = scale
            if _isap(scale):
                ins.append(scale)
        if accum is not None:
            kw["accum_out"] = accum
            outs.append(accum)
        c = 0.25 + fsz(in_) / 1100.0 + (0.1 if accum is not None else 0.0)
        P.emit("act", lambda e: e.activation(out, in_, func, **kw), outs, ins, occ=c, lat=c + 0.1, grp=GRP.get(str(func)))

    def vcost(eng, n):
        return (0.2 + n / 900.0) if eng == "dve" else (0.4 + n / 300.0)

    def TT(eng, out, a, b, op):
        c = vcost(eng, fsz(out))
        P.emit(eng, lambda e: e.tensor_tensor(out, a, b, op), [out], [a, b], occ=c, lat=c + 0.1)

    def TS(eng, out, a, s1, op0, s2=None, op1=None):
        ins = [a] + ([s1] if _isap(s1) else []) + ([s2] if (s2 is not None and _isap(s2)) else [])
        c = vcost(eng, fsz(out))
        if op1 is None:
            P.emit(eng, lambda e: e.tensor_scalar(out, a, s1, None, op0), [out], ins, occ=c, lat=c + 0.1)
        else:
            P.emit(eng, lambda e: e.tensor_scalar(out, a, s1, s2, op0, op1), [out], ins, occ=c, lat=c + 0.1)

    def STT(out, in0, scalar, in1, op0, op1):
        ins = [in0, in1] + ([scalar] if _isap(scalar) else [])
        c = 0.3 + fsz(out) / 900.0
        P.emit("dve", lambda e: e.scalar_tensor_tensor(out, in0, scalar, in1, op0, op1), [out], ins, occ=c, lat=c + 0.1)

    def CP(eng, out, in_):
        if eng == "act":
            c = 0.25 + fsz(in_) / 1100.0
            P.emit("act", lambda e: e.activation(out, in_, AF.Copy), [out], [in_], occ=c, lat=c + 0.1)
        else:
            c = vcost(eng, fsz(out))
            P.emit(eng, lambda e: e.tensor_copy(out, in_), [out], [in_], occ=c, lat=c + 0.1)

    def MEMSET(eng, out, v):
        c = vcost(eng, fsz(out))
        P.emit(eng, lambda e: e.memset(out, v), [out], [], occ=c, lat=c + 0.1)

    def RECIP(out, in_):
        c = 0.2 + fsz(out) / 900.0
        P.emit("dve", lambda e: e.reciprocal(out, in_), [out], [in_], occ=c, lat=c + 0.1)

    def SCAN(out, d0, d1):
        c = 0.2 + 2.0 * fsz(out) / 900.0
        P.emit("dve", lambda e: e.tensor_tensor_scan(out, d0, d1, 0.0, ALU.mult, ALU.add), [out], [d0, d1], occ=c, lat=c + 0.1)

    def dbytes(ap):
        n = 1
        for d_ in ap.shape:
            n *= d_
        return n * 4

    def DMA(out, in_, extra=()):
        return P.emit("sp", lambda e: e.dma_start(out=out, in_=in_), [out], [in_], dma=True, occ=0.15,
                      lat=2.2 + dbytes(in_) / 150e3, extra=extra)

    def WDMA(out, in_):
        return P.emit("pool", lambda e: e.dma_start(out=out, in_=in_), [out], [in_], dma=True, occ=1.0,
                      lat=3.0 + dbytes(in_) / 150e3)

    def av(off, np_, *shape):
        n = 1
        for s in shape:
            n *= s
        assert off + n <= ARENA, (off, shape)
        a = arena[0:np_, off:off + n]
        if len(shape) == 2:
            a = a.rearrange("p (a b) -> p a b", b=shape[1])
        elif len(shape) == 3:
            a = a.rearrange("p (a b c) -> p a b c", b=shape[1], c=shape[2])
        return a

    def avb(off, np_, *shape):
        n = 1
        for s_ in shape:
            n *= s_
        assert n % 2 == 0 and off + n // 2 <= ARENA, (off, shape)
        a = arena[0:np_, off:off + n // 2].bitcast(BF16)
        if len(shape) == 2:
            a = a.rearrange("p (a b) -> p a b", b=shape[1])
        elif len(shape) == 3:
            a = a.rearrange("p (a b c) -> p a b c", b=shape[1], c=shape[2])
        return a

    def bcl(ap2, n):
        sh = list(ap2.shape)
        return ap2.unsqueeze(len(sh)).broadcast_to(sh + [n])

    def bcm(ap2, n):
        sh = list(ap2.shape)
        return ap2.unsqueeze(1).broadcast_to([sh[0], n] + sh[1:])

    class PsumAlloc:
        def __init__(self):
            self.free = list(range(8))
            self.i = 0

        def get(self):
            self.i = (self.i + 1) % len(self.free)
            return PS[self.free[self.i]]

        def reserve(self, n):
            got = [self.free.pop() for _ in range(n)]
            return [PS[g] for g in got], got

        def release(self, ids):
            self.free.extend(ids)
            self.free.sort()

    psum = PsumAlloc()

    class WStream:
        def __init__(self):
            self.specs = []
            self.cur = 0
            self.issued = 0
            self.wb = {}
            self.cache = None

        def view(self, slot, shape):
            kc, nb = shape[1], shape[2]
            return wring[slot][:, 0:kc * nb].rearrange("p (a b) -> p a b", b=nb)

        def next(self, ap):
            if P.plan:
                self.specs.append(ap)
                return self.view(0, ap.shape)
            i = self.cur
            assert tuple(self.specs[i].shape) == tuple(ap.shape)
            npass = max(1, NPT + (1 if DO_SAMPLE else 0))
            nb_t = len(self.specs) // npass
            if self.cache is None and USE_WCACHE:
                self.cache = nc.dram_tensor("wcache", [nb_t, 128, WSLOT], BF16)
            while self.issued < min(i + NBUF, len(self.specs)):
                k = self.issued
                sp_ap = self.specs[k]
                n_ = sp_ap.shape[1] * sp_ap.shape[2]
                flat = wring[k % NBUF][:, 0:n_]
                cb = k % nb_t
                if not USE_WCACHE:
                    WDMA(self.view(k % NBUF, sp_ap.shape), sp_ap)
                elif k < nb_t:
                    WDMA(self.view(k % NBUF, sp_ap.shape), sp_ap)
                    self.wb[cb] = DMA(self.cache[cb][:, 0:n_], flat)
                else:
                    DMA(flat, self.cache[cb][:, 0:n_], extra=[self.wb[cb]])
                self.issued += 1
            self.cur += 1
            return self.view(i % NBUF, ap.shape)

    WS = WStream()

    def wblk(w, l, c0, nb):
        return w[l][:, c0:c0 + nb].rearrange("(kc p) n -> p kc n", p=128)

    evac_rr = [0]

    def evac_copy(out, in_):
        CP("act", out, in_)

    def setup():
        DMA(cst[:, :], cst_d.ap())
        DMA(pcols[:, :], pcols_d[:, :])
        DMA(prows[:, :], prows_d[:, :])
        MEMSET("pool", ones[:, :], 1.0)
        MEMSET("pool", ones_b[:, :], 1.0)
        ACT(misc[:, 0:64], prows[:, 64:128], AF.Exp)
        TS("dve", misc[:, 0:64], misc[:, 0:64], -1.0, ALU.mult)
        MEMSET("pool", misc[:, 64:72], 0.0)
        TT("dve", misc[:, 72:80], pcols[:, PC_LBR + 8:PC_LBR + 16], pcols[:, PC_LBR:PC_LBR + 8], ALU.subtract)
        ACT(misc[:, 72:80], misc[:, 72:80], AF.Sigmoid)
        TS("dve", misc[:, 80:96], misc[:, 64:80], -1.0, ALU.mult, 1.0, ALU.add)
        for j in range(2):
            MEMSET("pool", hst[j][:, :], 0.0)
            MEMSET("pool", shg[j][:, :, :], 0.0)

    a_bc = misc[:, 0:64]

    def hist(j):
        return misc[:, 96 + j * 72:96 + (j + 1) * 72].rearrange("p (a b) -> p a b", b=3)

    class T:
        pass

    def mk_tile(kind, ti):
        t = T()
        t.kind = kind
        t.ti = ti
        if kind == "p":
            t.Q = 256
            t.chunks = [(0, 128), (128, 128)]
            t.NS, t.L = 1, 256
            t.C = 32
            t.last = (ti == NPT - 1)
        else:
            t.Q = 64
            t.chunks = [(0, 64)]
            t.NS, t.L = 16, 4
            t.C = 4
            t.last = True
        return t

    def rmsnorm(t, wc0, dst=None):
        Q = t.Q
        if dst is None:
            dst = uT
        sq = yTb[:, 0:8, 0:Q]
        for kc in range(8):
            ACT(sq[:, kc, :], hT[:, kc, 0:Q], AF.Square)
        b = psum.get()
        for kc in range(8):
            MM(b[:, 0:Q], ones_b[:, :], sq[:, kc, :], kc == 0, kc == 7)
        ACT(rstd[:, 0:Q], b[:, 0:Q], AF.Ln, bias=EPS, scale=1.0 / D)
        ACT(rstd[:, 0:Q], rstd[:, 0:Q], AF.Exp, scale=-0.5)
        for kc in range(8):
            STT(dst[:, kc, 0:Q], hT[:, kc, 0:Q], pcols[:, wc0 + kc:wc0 + kc + 1], rstd[:, 0:Q], ALU.mult, ALU.mult)

    def ssd_layer(t, layer):
        j = layer // 2
        Q, NS, L = t.Q, t.NS, t.L
        sp_ = (t.kind == "s")
        if not sp_:
            o_xpre, o_xbc, o_z, o_xtok, o_yw = 0, 6216, 12360, 16456, 18504
            o_rb, o_wt, o_bt, o_cbt, o_sm = 20552, 21576, 22600, 23112, 23240
            o_xdt, o_xtl, o_yacc = 0, 2048, 4096
        else:
            o_xpre, o_xbc, o_z, o_xtok, o_yw = 0, 2688, 4224, 6272, 8320
            o_rb, o_wt, o_bt, o_cbt, o_sm = 10368, 10880, 11392, 11904, 11968
            o_xdt, o_xtl, o_yacc = 13312, 15360, 17408
            NATIN, HTS, SOUTB = [19456, 0, 24320], [21504, 13312], [6272, 8320]
            CMB, BMB = [23552, 2048], [23808, 10368]
        W_ = 3 + L
        xpre = av(o_xpre, 128, 24, NS, W_)
        xbc = av(o_xbc, 128, 24, Q)
        yT = yTb[:, :, 0:Q]
        o_dta, o_ee, o_ss, o_rs, o_dtt, o_db, o_r2 = o_sm, o_sm + 32, o_sm + 96, o_sm + 100, o_sm + 104, o_sm + 168, o_sm + 680
        PSTR = 0 if sp_ else 200
        if not sp_:
            o_db = o_sm + 104 + 64

        P.tag = (t.kind, t.ti, layer, "inproj")
        rmsnorm(t, PC_MIX + layer * 8)

        if sp_:
            stg = av(o_xdt, 48, 3072)
            DMA(stg, st_conv[j])
            for blk in range(6):
                b = psum.get()
                for q4 in range(4):
                    ch = blk * 4 + q4
                    TR(b[:, q4 * 48:(q4 + 1) * 48], stg[:, ch * 128:(ch + 1) * 128], 48)
                evac_copy(xpre[:, blk * 4:(blk + 1) * 4, :, 0:3],
                          b[:, 0:192].rearrange("p (a s k) -> p a s k", s=16, k=3))
        else:
            if t.ti == 0:
                MEMSET("pool", xpre[:, :, 0, 0:3], 0.0)
            else:
                CP("dve", xpre[:, :, 0, 0:3], hist(j))

        for xb in range(6):
            wv = WS.next(wblk(w_sin, j, 2048 + xb * 512, 512))
            for oi in range(4):
                ch = xb * 4 + oi
                b = psum.get()
                for kc in range(8):
                    MM(b[:, 0:Q], wv[:, kc, oi * 128:(oi + 1) * 128], uT[:, kc, 0:Q], kc == 0, kc == 7)
                evac_copy(xpre[:, ch, :, 3:3 + L], b[:, 0:Q].rearrange("p (s l) -> p s l", l=L))
        wv = WS.next(wblk(w_sin, j, 5120, 32))
        for ci, (o, n) in enumerate(t.chunks):
            b = psum.get()
            for kc in range(8):
                MM(b[0:n, 0:32], uT[:, kc, o:o + n], wv[:, kc, :], kc == 0, kc == 7)
            dtt = av(o_dtt + ci * 32, n, 32)
            TT("dve", dtt, b[0:n, 0:32], prows[0:n, j * 32:(j + 1) * 32], ALU.add)
            ACT(dtt, dtt, AF.Exp)
            ACT(dtt, dtt, AF.Ln, bias=1.0)

        if sp_:
            cc = av(o_xdt, 128, 24, 48)
            CP("dve", cc.rearrange("p a (s k) -> p a s k", k=3), xpre[:, :, :, 4:7])
            stg2 = av(o_xtl, 48, 3072)
            for blk in range(6):
                b = psum.get()
                for q4 in range(4):
                    ch = blk * 4 + q4
                    TR(b[0:48, q4 * 128:(q4 + 1) * 128], cc[:, ch, :], 128)
                evac_copy(stg2[:, blk * 512:(blk + 1) * 512], b[0:48, :])
            DMA(conv_s[j], stg2)
        else:
            CP("dve", hist(j), xpre[:, :, 0, Q:Q + 3])
            if t.last:
                stg2 = av(o_xtok, 3, 3072)
                for blk in range(6):
                    b = psum.get()
                    for q4 in range(4):
                        ch = blk * 4 + q4
                        TR(b[0:3, q4 * 128:(q4 + 1) * 128], xpre[:, ch, 0, Q:Q + 3], 128)
                    evac_copy(stg2[:, blk * 512:(blk + 1) * 512], b[0:3, :])
                DMA(conv_p[j], stg2)

        for cb in range(4):
            wv = WS.next(wblk(w_sin, j, cb * 512, 512))
            for ci, (o, n) in enumerate(t.chunks):
                b = psum.get()
                for kc in range(8):
                    MM(b[0:n, 0:512], uT[:, kc, o:o + n], wv[:, kc, :], kc == 0, kc == 7)
                ACT(av(o_z + ci * 2048 + cb * 512, n, 512), b[0:n, 0:512], AF.Silu)
        P.tag = (t.kind, t.ti, layer, "conv")
        def ovw(ch):
            return xbc[:, ch, :].rearrange("p (s l) -> p s l", l=L)
        for half in range(2):
            chs = range(half * 12, (half + 1) * 12)
            for ch in chs:
                cw = PC_CW + (j * 24 + ch) * 4
                ACT(ovw(ch), xpre[:, ch, :, 0:L], AF.Identity, scale=pcols[:, cw:cw + 1])
            for k in range(1, 4):
                for ch in chs:
                    cw = PC_CW + (j * 24 + ch) * 4
                    STT(ovw(ch), xpre[:, ch, :, k:k + L], pcols[:, cw + k:cw + k + 1], ovw(ch), ALU.mult, ALU.add)
            for ch in chs:
                ACT(xbc[:, ch, :], xbc[:, ch, :], AF.Silu, bias=pcols[:, PC_CB + j * 24 + ch:PC_CB + j * 24 + ch + 1])

        TRIm = cst[:, C_TRIS:C_TRIS + 64] if sp_ else cst[:, C_TRI:C_TRI + 128]
        LSTm = cst[:, C_LSTS:C_LSTS + 64] if sp_ else cst[:, C_LST:C_LST + 128]

        for ci, (o, n) in enumerate(t.chunks):
            Xtok = av(o_xtok, n, 2048)
            par = 0 if sp_ else ci % 2
            if sp_:
                Btok = avb(o_bt, n, 512)
                Xdt = avb(o_xdt, n, 2048)
                Xtl = avb(o_xtl, n, 2048)
                yacc = av(o_yacc, n, 2048)
                WTo = [0, 256]
            else:
                Btok = avb(o_bt + par * 256, n, 512)
                Xdt = avb([0, 1024][par], n, 2048)
                Xtl = avb([2048, 3072][par], n, 2048)
                yacc = av([4096, 24320][par], n, 2048)
                WTo = [26368, 26880]
            yw = av(o_yw, n, 2048)
            pso = par * 400
            dta = av(o_dta + pso, n, 32)
            Ee = av(o_ee + pso, n, 64)
            dB = av(o_db + pso, 128, NS * 32)
            dtt = av(o_dtt + ci * 32, n, 32)
            ztk = av(o_z + ci * 2048, n, 2048)
            P.tag = (t.kind, t.ti, layer, "A%d" % ci)
            for blk in range(4):
                b = psum.get()
                for q4 in range(4):
                    TR(b[0:n, q4 * 128:(q4 + 1) * 128], xbc[:, blk * 4 + q4, o:o + n], 128)
                evac_copy(Xtok[:, blk * 512:(blk + 1) * 512], b[0:n, :])
            b = psum.get()
            for g in range(4):
                TR(b[0:n, g * 128:(g + 1) * 128], xbc[:, 16 + g, o:o + n], 128)
            evac_copy(Btok[:, :], b[0:n, :])
            TT("dve", dta, dtt, a_bc[0:n, j * 32:(j + 1) * 32], ALU.mult)
            b = psum.get()
            MM(b[0:n, 0:32], TRIm[0:n, 0:n], dta, True, False)
            MM(b[0:n, 32:64], LSTm[0:n, 0:n], dta, False, True)
            ACT(Ee, b[0:n, 0:64], AF.Exp)
            if not sp_:
                b = psum.get()
                MM(b[:, 0:32], ones[0:n, :], dta, True, True)
                ACT(dB, b[:, 0:32], AF.Exp)
            dtl = av(o_sm + 232 + pso, n, 32)
            TT("dve", dtl, dtt, Ee[:, 32:64], ALU.mult)
            dsk = prows[0:n, 128 + j * 32:128 + (j + 1) * 32]
            for g in range(4):
                cs = slice(g * 512, (g + 1) * 512)
                hs = slice(g * 8, (g + 1) * 8)
                X3 = Xtok[:, cs].rearrange("p (h d) -> p h d", d=64)
                TT("dve", Xdt[:, cs].rearrange("p (h d) -> p h d", d=64), X3, bcl(dtt[:, hs], 64), ALU.mult)
                TT(POOLX, Xtl[:, cs].rearrange("p (h d) -> p h d", d=64), X3, bcl(dtl[:, hs], 64), ALU.mult)
                TT(POOLX, yacc[:, cs].rearrange("p (h d) -> p h d", d=64), X3, bcl(dsk[:, hs], 64), ALU.mult)
            P.tag = (t.kind, t.ti, layer, "B%d" % ci)
            RWo = [o_rb, o_wt]
            CBo = [o_cbt, o_sm + 256] if sp_ else [o_cbt, o_sm + 824]

            def stage_b1(g):
                CBTm = av(CBo[g % 2], n, n)
                Rb = av(RWo[g % 2], n, 8, n)
                b = psum.get()
                MM(b[0:n, 0:n], xbc[:, 16 + g, o:o + n], xbc[:, 20 + g, o:o + n], True, True)
                TT("dve", CBTm, b[0:n, 0:n], TRIm[0:n, 0:n], ALU.mult)
                TT("dve", Rb, bcm(TRIm[0:n, 0:n], 8), bcl(dta[:, g * 8:(g + 1) * 8], n), ALU.mult)

            def stage_b2(g):
                CBTm = av(CBo[g % 2], n, n)
                RW3 = av(RWo[g % 2], n, 8, n)
                WT = avb(WTo[g % 2], n, 8, n)
                rwf = av(RWo[g % 2], n, 8 * n)
                for hf in range(8 * n // 512):
                    bs = psum.get()
                    MM(bs[0:n, 0:512], LSTm[0:n, 0:n], rwf[:, hf * 512:(hf + 1) * 512], True, True)
                    ACT(rwf[:, hf * 512:(hf + 1) * 512], bs[0:n, 0:512], AF.Exp)
                TT("dve", WT, RW3, bcm(CBTm, 8), ALU.mult)
                b = psum.get()
                for r in range(8):
                    MM(b[0:n, r * 64:(r + 1) * 64], WT[:, r, :], Xdt[:, (g * 8 + r) * 64:(g * 8 + r + 1) * 64], r == 0, r == 7)
                TT("dve", yacc[:, g * 512:(g + 1) * 512], b[0:n, 0:512], yacc[:, g * 512:(g + 1) * 512], ALU.add)
            stage_b1(0)
            for g in range(4):
                if g + 1 < 4:
                    stage_b1(g + 1)
                stage_b2(g)
            P.tag = (t.kind, t.ti, layer, "C%d" % ci)
            YI, yid = psum.reserve(4)
            if not sp_:
                hstate = hst[j]
                for g in range(4):
                    MM(YI[g][0:n, 0:512], xbc[:, 20 + g, o:o + n], hstate[:, g * 512:(g + 1) * 512], True, True)
                ub = []
                for g in range(4):
                    b = psum.get()
                    MM(b[:, 0:512], Btok[:, g * 128:(g + 1) * 128], Xtl[:, g * 512:(g + 1) * 512], True, True)
                    ub.append(b)
                for g in range(4):
                    hs3 = hstate[:, g * 512:(g + 1) * 512].rearrange("p (h d) -> p h d", d=64)
                    TT("dve", hs3, hs3, bcl(dB[:, g * 8:(g + 1) * 8], 64), ALU.mult)
                for g in range(4):
                    TT("dve", hstate[:, g * 512:(g + 1) * 512], hstate[:, g * 512:(g + 1) * 512], ub[g][:, 0:512], ALU.add)
            else:
                dtaX = av(o_xdt, 64, 2048)
                CP("dve", dtaX.rearrange("p (h d) -> p h d", d=64), bcl(dta, 64))
                bD = psum.get()
                for c in range(16):
                    MM(bD[:, c * 16:(c + 1) * 16], dtaX[:, c * 128:(c + 1) * 128], cst[0:64, C_MSEL:C_MSEL + 16], c == 0, c == 15)
                dcolS = av(o_r2, 128, 16, 16)
                ACT(av(o_r2, 128, 256), bD[:, 0:256], AF.Exp)

                def st_load(s_):
                    DMA(av(NATIN[s_ % 3], 128, 16, 128), st_ssm[j, s_].rearrange("(c q) n -> q c n", q=128))

                def st_tr(s_):
                    natin = av(NATIN[s_ % 3], 128, 16, 128)
                    hts = avb(HTS[s_ % 2], 128, 2048)
                    for blk in range(4):
                        b = psum.get()
                        for q4 in range(4):
                            TR(b[:, q4 * 128:(q4 + 1) * 128], natin[:, blk * 4 + q4, :], 128)
                        evac_copy(hts[:, blk * 512:(blk + 1) * 512], b[:, :])
                    Cm = avb(CMB[s_ % 2], 128, 4, 64)
                    TT("dve", Cm, xbc[:, 20:24, 0:64], bcm(cst[:, C_MSBC + s_ * 64:C_MSBC + (s_ + 1) * 64], 4), ALU.mult)
                    Bm = avb(BMB[s_ % 2], 64, 512)
                    TS("dve", Bm, Btok[:, :], cst[0:64, C_MSEL + s_:C_MSEL + s_ + 1], ALU.mult)

                def st_comp(s_):
                    natin = av(NATIN[s_ % 3], 128, 16, 128)
                    hts = avb(HTS[s_ % 2], 128, 2048)
                    sout = av(SOUTB[s_ % 2], 128, 16, 128)
                    Cm = avb(CMB[s_ % 2], 128, 4, 64)
                    Bm = avb(BMB[s_ % 2], 64, 512)
                    for g in range(4):
                        MM(YI[g][0:n, 0:512], Cm[:, g, :], hts[:, g * 512:(g + 1) * 512], s_ == 0, s_ == NS - 1)
                    for blk in range(4):
                        b = psum.get()
                        for q4 in range(4):
                            c = blk * 4 + q4
                            MM(b[:, q4 * 128:(q4 + 1) * 128], Xtl[:, c * 128:(c + 1) * 128], Bm[:, blk * 128:(blk + 1) * 128],
                               q4 == 0, q4 == 3)
                        so = sout[:, blk * 4:(blk + 1) * 4, :]
                        TT(POOLX, so, natin[:, blk * 4:(blk + 1) * 4, :],
                           dcolS[:, blk * 4:(blk + 1) * 4, s_:s_ + 1].broadcast_to([128, 4, 128]), ALU.mult)
                        TT("dve", so, so, b[:, :].rearrange("p (a b) -> p a b", b=128), ALU.add)
                    DMA(ssm_s[j, s_].rearrange("(c q) n -> q c n", q=128), sout)
                st_load(0)
                st_load(1)
                st_load(2)
                st_tr(0)
                for s_ in range(NS):
                    if s_ + 1 < NS:
                        st_tr(s_ + 1)
                    st_comp(s_)
                    if s_ + 3 < NS:
                        st_load(s_ + 3)
            P.tag = (t.kind, t.ti, layer, "D%d" % ci)
            for g in range(4):
                ywg = yw[:, g * 512:(g + 1) * 512]
                TT("dve", ywg.rearrange("p (h d) -> p h d", d=64), YI[g][0:n, 0:512].rearrange("p (h d) -> p h d", d=64),
                   bcl(Ee[:, g * 8:(g + 1) * 8], 64), ALU.mult)
            for g in range(4):
                ywg = yw[:, g * 512:(g + 1) * 512]
                TT("dve", ywg, ywg, yacc[:, g * 512:(g + 1) * 512], ALU.add)
            psum.release(yid)
            for g in range(4):
                TT("dve", yw[:, g * 512:(g + 1) * 512], yw[:, g * 512:(g + 1) * 512], ztk[:, g * 512:(g + 1) * 512], ALU.mult)
            ss = av(o_ss + pso, n, 4)
            rs = av(o_rs + pso, n, 4)
            for g in range(4):
                ACT(yacc[:, g * 512:(g + 1) * 512], yw[:, g * 512:(g + 1) * 512], AF.Square, accum=ss[:, g:g + 1])
            ACT(rs, ss, AF.Ln, bias=EPS, scale=1.0 / 512)
            ACT(rs, rs, AF.Exp, scale=-0.5)
            for g in range(4):
                ACT(yw[:, g * 512:(g + 1) * 512], yw[:, g * 512:(g + 1) * 512], AF.Identity, scale=rs[:, g:g + 1])
            for blk in range(4):
                b = psum.get()
                for q4 in range(4):
                    c = blk * 4 + q4
                    TR(b[:, q4 * n:(q4 + 1) * n], yw[:, c * 128:(c + 1) * 128], n)
                TT("dve", yT[:, blk * 4:(blk + 1) * 4, o:o + n], b[:, 0:4 * n].rearrange("p (a b) -> p a b", b=n),
                   bcl(pcols[:, PC_SNW + j * 16 + blk * 4:PC_SNW + j * 16 + (blk + 1) * 4], n), ALU.mult)
        P.tag = (t.kind, t.ti, layer, "out")
        for ob in range(4):
            wv = WS.next(wblk(w_sout, j, ob * 256, 256))
            for oi in range(2):
                oc = ob * 2 + oi
                b = psum.get()
                for kc in range(16):
                    MM(b[:, 0:Q], wv[:, kc, oi * 128:(oi + 1) * 128], yT[:, kc, 0:Q], kc == 0, kc == 15)
                TT("dve", hT[:, oc, 0:Q], hT[:, oc, 0:Q], b[:, 0:Q], ALU.add)
        if (not sp_) and t.last:
            natout = av(0, 128, 16, 128)
            for blk in range(4):
                b = psum.get()
                for q4 in range(4):
                    c = blk * 4 + q4
                    TR(b[:, q4 * 128:(q4 + 1) * 128], hst[j][:, c * 128:(c + 1) * 128], 128)
                evac_copy(natout[:, blk * 4:(blk + 1) * 4, :], b[:, :].rearrange("p (a b) -> p a b", b=128))
            DMA(ssm_p[j].rearrange("(c q) n -> q c n", q=128), natout)

    def hgrn_layer(t, layer):
        j = layer // 2
        Q, C = t.Q, t.C
        sp_ = (t.kind == "s")
        B = 8 * Q
        NC = Q // C
        o_q, o_f, o_k, o_c, o_e1, o_e2, o_g, o_o = 0, B, 2 * B, 3 * B, 4 * B, 5 * B, 6 * B, 7 * B
        o_v = 8 * B
        o_kta = o_v + 2048
        o_ktc = o_kta + 1024
        o_att = o_ktc + 1024
        o_d = o_att + 1024
        o_sin = o_d + 256
        o_sout = o_sin + 1024
        assert o_sout + (9216 if sp_ else 1024) <= ARENA

        def buf(o_):
            return av(o_, 128, 8, Q)

        def flat(o_):
            return av(o_, 128, B)
        Qb, Fb, Kb, Cb, E1, E2, Gb, Ob = [buf(x) for x in (o_q, o_f, o_k, o_c, o_e1, o_e2, o_g, o_o)]
        oTf = yTb[:, 0:8, 0:Q]
        lbc = misc[:, 64 + j * 8:64 + (j + 1) * 8]
        omlc = misc[:, 80 + j * 8:80 + (j + 1) * 8]

        P.tag = (t.kind, t.ti, layer, 'hg_in')
        rmsnorm(t, PC_MIX + layer * 8)

        def fm_block(c0, fn):
            for blk in range(2):
                wv = WS.next(wblk(w_hin, j, c0 + blk * 512, 512))
                for oi in range(4):
                    h = blk * 4 + oi
                    b = psum.get()
                    for kc in range(8):
                        MM(b[:, 0:Q], wv[:, kc, oi * 128:(oi + 1) * 128], uT[:, kc, 0:Q], kc == 0, kc == 7)
                    fn(h, b[:, 0:Q])
        fm_block(1024, lambda h, b: CP("act", Fb[:, h, :], b))
        fm_block(0, lambda h, b: ACT(Qb[:, h, :], b, AF.Silu))
        for blk in range(2):
            wv = WS.next(wblk(w_hin, j, 2048 + blk * 512, 512))
            for ci, (o, n) in enumerate(t.chunks):
                b = psum.get()
                for kc in range(8):
                    MM(b[0:n, 0:512], uT[:, kc, o:o + n], wv[:, kc, :], kc == 0, kc == 7)
                evac_copy(avb(o_v + ci * 1024, n, 1024)[:, blk * 512:(blk + 1) * 512], b[0:n, 0:512])
        fm_block(3072, lambda h, b: ACT(Gb[:, h, :], b, AF.Silu))

        P.tag = (t.kind, t.ti, layer, 'hg_chain')
        rm = cst[:, C_RM4:C_RM4 + 64] if sp_ else cst[:, C_RM32:C_RM32 + 256]
        HB = 4 * Q
        NCh = 4 * NC
        dcy = av(o_d, 128, 8 * NC)

        def hf2(o_, hf):
            return av(o_ + hf * HB, 128, HB)

        def hf3(o_, hf):
            return av(o_ + hf * HB, 128, 4, Q)

        def hfc(o_, hf):
            return av(o_ + hf * HB, 128, NCh, C)
        steps = [
            lambda hf: ACT(hf2(o_k, hf), hf2(o_f, hf), AF.Sigmoid, scale=-1.0),
            lambda hf: ACT(hf2(o_f, hf), hf2(o_f, hf), AF.Sigmoid),
            lambda hf: [ACT(Fb[:, hf * 4 + h4, :], Fb[:, hf * 4 + h4, :], AF.Ln, scale=omlc[:, hf * 4 + h4:hf * 4 + h4 + 1],
                            bias=lbc[:, hf * 4 + h4:hf * 4 + h4 + 1]) for h4 in range(4)],
            lambda hf: TT("dve", hf3(o_k, hf), hf3(o_k, hf), bcl(omlc[:, hf * 4:(hf + 1) * 4], Q), ALU.mult),
            lambda hf: [SCAN(Cb[:, hf * 4 + h4, :], rm, Fb[:, hf * 4 + h4, :]) for h4 in range(4)],
            lambda hf: ACT(hf2(o_e1, hf), hf2(o_c, hf), AF.Exp),
            lambda hf: TT("dve", hf2(o_q, hf), hf2(o_q, hf), hf2(o_e1, hf), ALU.mult),
            lambda hf: ACT(hf2(o_e1, hf), hf2(o_c, hf), AF.Exp, scale=-1.0),
            lambda hf: TT("dve", hf2(o_e1, hf), hf2(o_e1, hf), hf2(o_k, hf), ALU.mult),
            lambda hf: TT("dve", hfc(o_e2, hf), hfc(o_c, hf)[:, :, C - 1:C].broadcast_to([128, NCh, C]), hfc(o_c, hf), ALU.subtract),
            lambda hf: ACT(hf2(o_e2, hf), hf2(o_e2, hf), AF.Exp),
            lambda hf: TT("dve", hf2(o_e2, hf), hf2(o_e2, hf), hf2(o_k, hf), ALU.mult),
            lambda hf: ACT(dcy[:, hf * NCh:(hf + 1) * NCh].unsqueeze(2), hfc(o_c, hf)[:, :, C - 1:C], AF.Exp),
        ]
        for st in steps:
            for hf in range(2):
                st(hf)

        if sp_:
            agroups = [(0, 64)]
            BD = cst[0:64, C_TRIS:C_TRIS + 64]
        else:
            agroups = [(0, 128), (128, 128)]
            BD = cst[:, C_BD32:C_BD32 + 128]
        for ai, (ao, an) in enumerate(agroups):
            P.tag = (t.kind, t.ti, layer, 'hg_att%d' % ai)
            NCA = an // C
            hpb = 512 // an
            nb = 8 // hpb
            gp = 0 if sp_ else ai % 2
            attm = avb([o_att, 24320][gp], an, 8 * an)
            ktall = av([o_kta, 25344][gp], an, 1024)
            Vt = avb(o_v + ai * 1024, an, 1024)
            for bi in range(nb):
                b = psum.get()
                for hh in range(hpb):
                    h = bi * hpb + hh
                    MM(b[0:an, hh * an:(hh + 1) * an], E1[:, h, ao:ao + an], Qb[:, h, ao:ao + an], hh == 0, hh == hpb - 1)
                TT("dve", attm[:, bi * 512:(bi + 1) * 512].rearrange("p (a b) -> p a b", b=an),
                   b[0:an, :].rearrange("p (a b) -> p a b", b=an), bcm(BD, hpb), ALU.mult)
            for bi in range(2):
                b = psum.get()
                for hh in range(4):
                    TR(b[0:an, hh * 128:(hh + 1) * 128], E2[:, bi * 4 + hh, ao:ao + an], 128)
                evac_copy(ktall[:, bi * 512:(bi + 1) * 512], b[0:an, :])
            OB, obid = psum.reserve(nb)
            for h in range(8):
                bi, hh = h // hpb, h % hpb
                MM(OB[bi][:, hh * an:(hh + 1) * an], Vt[:, h * 128:(h + 1) * 128], attm[:, h * an:(h + 1) * an],
                   hh == 0, False)
            if sp_:
                KTC = [o_ktc, o_sout + 1024, o_sout + 4096, o_sout + 5120]
                SINB = [o_sin, o_sout + 2048, o_sout + 6144, o_sout + 7168]
                SOB = [o_sout, o_sout + 3072, o_sout + 8192]
            else:
                KTC = [[o_f, o_f + 1024, o_k, o_k + 1024], [o_c, o_c + 1024, o_ktc, 26368]][gp]

            def mk_ktc(c):
                ktc = avb(KTC[c % len(KTC)], an, 1024)
                msk = cst[0:64, C_MSEL + c:C_MSEL + c + 1] if sp_ else cst[:, C_MB32 + c:C_MB32 + c + 1]
                TS(POOLX, ktc, ktall, msk, ALU.mult)
            PF = 3
            if sp_:
                for c0 in range(PF):
                    DMA(av(SINB[c0 % len(SINB)], 128, 8, 128), st_hgrn[j, c0].rearrange("h k v -> k h v"))
                    mk_ktc(c0)
            else:
                for c in range(NCA):
                    mk_ktc(c)
            for c in range(NCA):
                cg = ao // C + c
                tok0 = ao + c * C
                if sp_:
                    Sst = av(SINB[c % len(SINB)], 128, 8, 128)
                    Sds = av(SOB[c % len(SOB)], 128, 8, 128)
                    if c + PF < NCA:
                        DMA(av(SINB[(c + PF) % len(SINB)], 128, 8, 128), st_hgrn[j, c + PF].rearrange("h k v -> k h v"))
                        mk_ktc(c + PF)
                else:
                    Sst = shg[j]
                    Sds = shg[j]
                for h in range(8):
                    bi, hh = h // hpb, h % hpb
                    MM(OB[bi][:, hh * an + c * C:hh * an + (c + 1) * C], Sst[:, h, :], Qb[:, h, tok0:tok0 + C],
                       False, (c == NCA - 1 and hh == hpb - 1))
                ktc = avb(KTC[c % len(KTC)], an, 1024)
                ubs = []
                for bi2 in range(2):
                    b = psum.get()
                    for hh in range(4):
                        h = bi2 * 4 + hh
                        MM(b[:, hh * 128:(hh + 1) * 128], ktc[:, h * 128:(h + 1) * 128], Vt[:, h * 128:(h + 1) * 128],
                           hh == 0, hh == 3)
                    ubs.append(b)
                for bi2 in range(2):
                    for hh in range(4):
                        h = bi2 * 4 + hh
                        STT(Sds[:, h, :], Sst[:, h, :], dcy[:, h * NC + cg:h * NC + cg + 1], ubs[bi2][:, hh * 128:(hh + 1) * 128],
                            ALU.mult, ALU.add)
                if sp_:
                    DMA(hgrn_s[j, c].rearrange("h k v -> k h v"), Sds)
            for bi in range(nb):
                CP("act", Ob[:, bi * hpb:(bi + 1) * hpb, ao:ao + an], OB[bi][:, :].rearrange("p (a b) -> p a b", b=an))
            psum.release(obid)

        P.tag = (t.kind, t.ti, layer, 'hg_out')
        for ai, (ao, an) in enumerate(agroups):
            tsl = slice(ao, ao + an)
            ACT(Fb[:, :, tsl], Ob[:, :, tsl], AF.Square)
            hb = 512 // an
            for bi in range(8 // hb):
                b = psum.get()
                for hh in range(hb):
                    h = bi * hb + hh
                    MM(b[:, hh * an:(hh + 1) * an], ones[:, :], Fb[:, h, tsl], hh == 0, hh == hb - 1)
                ACT(E1[:, bi * hb:(bi + 1) * hb, tsl], b[:, 0:hb * an].rearrange("p (a b) -> p a b", b=an), AF.Ln,
                    bias=EPS, scale=1.0 / 128)
            ACT(E1[:, :, tsl], E1[:, :, tsl], AF.Exp, scale=-0.5)
            TT("dve", Ob[:, :, tsl], Ob[:, :, tsl], E1[:, :, tsl], ALU.mult)
            TT("dve", Ob[:, :, tsl], Ob[:, :, tsl], Gb[:, :, tsl], ALU.mult)
            TT("dve", oTf[:, :, tsl], Ob[:, :, tsl], bcl(pcols[:, PC_HNW + j * 8:PC_HNW + (j + 1) * 8], an), ALU.mult)
        for ob in range(2):
            wv = WS.next(wblk(w_hout, j, ob * 512, 512))
            for oi in range(4):
                oc = ob * 4 + oi
                b = psum.get()
                for kc in range(8):
                    MM(b[:, 0:Q], wv[:, kc, oi * 128:(oi + 1) * 128], oTf[:, kc, :], kc == 0, kc == 7)
                TT("dve", hT[:, oc, 0:Q], hT[:, oc, 0:Q], b[:, 0:Q], ALU.add)
        if (not sp_) and t.last:
            DMA(hgrn_p[j].rearrange("h k v -> k h v"), shg[j][:, :, :])

    def mlp_layer(t, layer):
        Q = t.Q
        P.tag = (t.kind, t.ti, layer, "mlp")
        hid = arena[:, 0:4096].bitcast(BF16).rearrange("p (a b) -> p a b", b=256)[:, :, 0:Q]
        rmsnorm(t, PC_MLP + layer * 8)
        pend = [None]
        for ub in range(8):
            wv = WS.next(wblk(w_up, layer, ub * 512, 512))
            for oi in range(4):
                oc = ub * 4 + oi
                b = psum.get()
                for kc in range(8):
                    MM(b[:, 0:Q], wv[:, kc, oi * 128:(oi + 1) * 128], uT[:, kc, 0:Q], kc == 0, kc == 7)
                rt = av(4096 + (oc % 4) * 256, 128, Q)
                ACT(rt, b[:, 0:Q], AF.Relu)
                if pend[0] is not None:
                    pend[0]()
                pend[0] = (lambda oc=oc, rt=rt: TT("dve", hid[:, oc, :], rt, rt, ALU.mult))
        pend[0]()
        for db in range(8):
            wv = WS.next(wblk(w_dn, layer, db * 128, 128))
            b = psum.get()
            for kc in range(32):
                MM(b[:, 0:Q], wv[:, kc, :], hid[:, kc, :], kc == 0, kc == 31)
            TT("dve", hT[:, db, 0:Q], hT[:, db, 0:Q], b[:, 0:Q], ALU.add)

    def tile_prog(t):
        Q = t.Q
        src = xs if t.kind == "s" else xp
        dst = y_s if t.kind == "s" else y_p
        row0 = 0 if t.kind == "s" else t.ti * 256
        for ci, (o, n) in enumerate(t.chunks):
            xt = av(8192 + ci * 1024, n, 1024)
            DMA(xt, src[row0 + o:row0 + o + n, :])
            for bi in range(2):
                b = psum.get()
                for q4 in range(4):
                    kc = bi * 4 + q4
                    TR(b[:, q4 * n:(q4 + 1) * n], xt[:, kc * 128:(kc + 1) * 128], n)
                evac_copy(hT[:, bi * 4:(bi + 1) * 4, o:o + n], b[:, 0:4 * n].rearrange("p (a b) -> p a b", b=n))
        sub = 0
        for layer in range(4):
            if sub < NSUB:
                if layer % 2 == 0:
                    ssd_layer(t, layer)
                else:
                    hgrn_layer(t, layer)
            sub += 1
            if sub < NSUB:
                mlp_layer(t, layer)
            sub += 1
        if FINAL_NORM:
            fin = av(4096, 128, 8, Q)
            rmsnorm(t, PC_FIN, fin)
        else:
            fin = hT
        for ci, (o, n) in enumerate(t.chunks):
            yt = av(ci * 1024, n, 1024)
            for bi in range(2):
                b = psum.get()
                for q4 in range(4):
                    kc = bi * 4 + q4
                    TR(b[0:n, q4 * 128:(q4 + 1) * 128], fin[:, kc, o:o + n], 128)
                evac_copy(yt[:, bi * 512:(bi + 1) * 512], b[0:n, :])
            DMA(dst[row0 + o:row0 + o + n, :], yt)

    def whole():
        setup()
        if DO_SAMPLE and SAMPLE_FIRST:
            tile_prog(mk_tile("s", 0))
        for ti in range(NPT):
            tile_prog(mk_tile("p", ti))
        if DO_SAMPLE and not SAMPLE_FIRST:
            tile_prog(mk_tile("s", 0))

    P.plan = True
    whole()
    P.plan = False
    psum.__init__()
    evac_rr[0] = 0
    whole()
    assert WS.cur == len(WS.specs)
    P.run()
    return P


def make_in_maps(inp):
    f = lambda a: np.ascontiguousarray(np.asarray(a, np.float32))
    pc = _pack_cols(inp)
    pr = _pack_rows(inp)
    shared = {k: f(inp[k]) for k in ("ssd_w_in", "ssd_w_out", "hgrn_w_in", "hgrn_w_out", "mlp_w_up", "mlp_w_down")}
    maps = []
    for c in range(8):
        sl = slice(16 * c, 16 * (c + 1))
        m = dict(shared)
        m["xp"] = f(inp["x_prompt"][c])
        m["xs"] = f(np.asarray(inp["x_sample"])[sl].reshape(64, D))
        m["st_conv"] = f(np.asarray(inp["state_ssd_conv"])[:, sl].reshape(2, 48, 3072))
        m["st_ssm"] = f(np.asarray(inp["state_ssd_ssm"])[:, sl].reshape(2, 16, 2048, 128))
        m["st_hgrn"] = f(np.asarray(inp["state_hgrn"])[:, sl])
        m["pcols"] = pc
        m["prows"] = pr
        maps.append(m)
    return maps


def assemble(results):
    r = results
    y_p = np.stack([r[c]["y_p"] for c in range(8)], 0)
    y_s = np.concatenate([r[c]["y_s"].reshape(16, 4, D) for c in range(8)], 0)
    conv_p = np.stack([r[c]["conv_p"] for c in range(8)], 1)
    ssm_p = np.stack([r[c]["ssm_p"].reshape(2, 32, 64, 128) for c in range(8)], 1)
    hgrn_p = np.stack([r[c]["hgrn_p"] for c in range(8)], 1)
    conv_s = np.concatenate([r[c]["conv_s"].reshape(2, 16, 3, 3072) for c in range(8)], 1)
    ssm_s = np.concatenate([r[c]["ssm_s"].reshape(2, 16, 32, 64, 128) for c in range(8)], 1)
    hgrn_s = np.concatenate([r[c]["hgrn_s"] for c in range(8)], 1)
    return tuple(np.ascontiguousarray(a, dtype=np.float32)
                 for a in (y_p, y_s, conv_p, ssm_p, hgrn_p, conv_s, ssm_s, hgrn_s))


def kernel(**inputs):
    nc = bass.Bass("TRN2", target_bir_lowering=False)
    build(nc, {})
    in_maps = make_in_maps(inputs)
    res = run_bass_kernel_spmd(nc, in_maps, core_ids=list(range(8)))
    return assemble(res.results)
```
